# Optimizing a Trainium2 kernel written in Bass

```python
import jax, jax.numpy as jnp
from jax import lax
import numpy as np

D_MODEL = 1024
BATCH = 8
SEQ = 4096
DEPTH = 2

HEAD_DIM = 64
ATT_HEADS = D_MODEL // 128
ATT_WIDTH = ATT_HEADS * HEAD_DIM
DILATED_BRANCHES = ((128, 1), (512, 4), (2048, 16))
SSD_HEAD_DIM = 64
SSD_HEADS = D_MODEL // 128
SSD_WIDTH = SSD_HEADS * SSD_HEAD_DIM
SSD_GROUPS = 2
SSD_STATE = 128
SSD_CONV = 5
SSD_CHUNK = 128
SSD_CONV_DIM = SSD_WIDTH + 2 * SSD_GROUPS * SSD_STATE
CONV_GROUPS = 8
CONV_WIDTH = D_MODEL // 2
SHORT_CONV = 3
D_MIX = ATT_WIDTH + SSD_WIDTH + CONV_WIDTH
SPLIT_SIZES = (ATT_WIDTH, ATT_WIDTH, ATT_WIDTH,
               SSD_WIDTH, SSD_CONV_DIM, 2 * SSD_HEADS,
               CONV_WIDTH, CONV_WIDTH, CONV_WIDTH)
SPLIT_POINTS = tuple(int(v) for v in np.cumsum(SPLIT_SIZES)[:-1])
D_IN = int(sum(SPLIT_SIZES))
D_FF = 2816
FFN_CONV = 3
EPS = 1e-6
NEG = -1e30

kernel_name = "hybrid_dilated_ssd_shortconv_encoder"


def rmsnorm(x, g):
    xf = x.astype(jnp.float32)
    y = xf * lax.rsqrt(jnp.mean(xf * xf, axis=-1, keepdims=True) + EPS)
    return (y * g.astype(jnp.float32)).astype(x.dtype)


def group_rmsnorm(y, g, n_groups, out_dtype):
    shp = y.shape
    yf = y.astype(jnp.float32).reshape(*shp[:-1], n_groups, shp[-1] // n_groups)
    yf = yf * lax.rsqrt(jnp.mean(yf * yf, axis=-1, keepdims=True) + EPS)
    return (yf.reshape(shp) * g.astype(jnp.float32)).astype(out_dtype)


def dwconv_centred(x, w, b):
    k = w.shape[0]
    y = lax.conv_general_dilated(
        x, w[:, None, :].astype(x.dtype), window_strides=(1,),
        padding=[(k // 2, k // 2)], dimension_numbers=('NWC', 'WIO', 'NWC'),
        feature_group_count=x.shape[-1])
    return y + b.astype(x.dtype)


def alibi_slopes(n):
    return 2.0 ** (-8.0 * (jnp.arange(n, dtype=jnp.float32) + 1.0) / n)


def dilated_branch(q, k, v, slopes, window, dilation):
    b, s, h, dh = q.shape
    half = window // (2 * dilation)
    blk = half
    l = s // dilation
    nb = -(-l // blk)
    lp = nb * blk

    def to_sub(t):
        t = t.reshape(b, l, dilation, h, dh).transpose(0, 2, 3, 1, 4)
        return jnp.pad(t, ((0, 0), (0, 0), (0, 0), (0, lp - l), (0, 0)))

    def key_blocks(t):
        t = jnp.pad(to_sub(t), ((0, 0), (0, 0), (0, 0), (blk, blk), (0, 0)))
        t = t.reshape(b, dilation, h, nb + 2, blk, dh)
        return jnp.concatenate([t[:, :, :, :-2], t[:, :, :, 1:-1], t[:, :, :, 2:]], axis=4)

    qs = to_sub(q).reshape(b, dilation, h, nb, blk, dh)
    kb = key_blocks(k)
    vb = key_blocks(v)
    qi = jnp.arange(blk)[:, None]
    kj = jnp.arange(3 * blk)[None, :]
    rel = kj - blk - qi
    kidx = jnp.arange(nb)[:, None, None] * blk + kj[None] - blk
    valid = (jnp.abs(rel) <= half)[None] & (kidx >= 0) & (kidx < l)
    bias = -slopes[:, None, None, None] * (dilation * jnp.abs(rel)).astype(jnp.float32)
    scores = jnp.einsum('brhnqd,brhnkd->brhnqk', qs, kb) * (dh ** -0.5) + bias
    scores = jnp.where(valid, scores, NEG)
    lse = jax.nn.logsumexp(scores, axis=-1)
    o = jnp.einsum('brhnqk,brhnkd->brhnqd', jnp.exp(scores - lse[..., None]), vb)
    o = o.reshape(b, dilation, h, lp, dh)[:, :, :, :l].transpose(0, 3, 1, 2, 4).reshape(b, s, h, dh)
    lse = lse.reshape(b, dilation, h, lp)[:, :, :, :l].transpose(0, 3, 1, 2).reshape(b, s, h)
    return o, lse


def dilated_attention(q, k, v):
    b, s, _ = q.shape
    q, k, v = (t.astype(jnp.float32).reshape(b, s, ATT_HEADS, HEAD_DIM) for t in (q, k, v))
    slopes = alibi_slopes(ATT_HEADS)
    outs, lses = [], []
    for window, dilation in DILATED_BRANCHES:
        o, lse = dilated_branch(q, k, v, slopes, window, dilation)
        outs.append(o)
        lses.append(lse)
    wts = jax.nn.softmax(jnp.stack(lses, axis=0), axis=0)
    out = jnp.einsum('ibsh,ibshd->bshd', wts, jnp.stack(outs, axis=0))
    return out.reshape(b, s, ATT_WIDTH)


def ssd_scan(x, dt, a, bm, cm):
    b, s, h, p = x.shape
    g, n = bm.shape[2], bm.shape[3]
    e = h // g
    l = SSD_CHUNK
    c = s // l
    xc = (x * dt[..., None]).reshape(b, c, l, g, e, p)
    ac = (dt * a).reshape(b, c, l, g, e).transpose(0, 3, 4, 1, 2)
    bc = bm.reshape(b, c, l, g, n)
    cc = cm.reshape(b, c, l, g, n)
    acum = jnp.cumsum(ac, axis=-1)
    lower = jnp.tril(jnp.ones((l, l), dtype=bool))
    decay_in = jnp.exp(jnp.where(lower, acum[..., :, None] - acum[..., None, :], -jnp.inf))
    y_diag = jnp.einsum('bclgn,bcsgn,bgecls,bcsgep->bclgep', cc, bc, decay_in, xc)
    decay_to_end = jnp.exp(acum[..., -1:] - acum)
    chunk_states = jnp.einsum('bclgn,bgecl,bclgep->bcgepn', bc, decay_to_end, xc)
    chunk_decay = jnp.exp(acum[..., -1])

    def step(state, inp):
        st, dec = inp
        return state * dec[..., None, None] + st, state

    h0 = jnp.zeros((b, g, e, p, n), dtype=x.dtype)
    _, prev = lax.scan(step, h0, (chunk_states.transpose(1, 0, 2, 3, 4, 5),
                                  chunk_decay.transpose(3, 0, 1, 2)))
    prev = prev.transpose(1, 0, 2, 3, 4, 5)
    y_off = jnp.einsum('bclgn,bcgepn,bgecl->bclgep', cc, prev, jnp.exp(acum))
    return (y_diag + y_off).reshape(b, s, h, p)


def ssd_mixer(z, xbc, dt, conv_w, conv_b, dt_bias, a_log, d_skip, norm_g, out_dtype):
    b, s, _ = z.shape
    xbc = jax.nn.silu(dwconv_centred(xbc, conv_w, conv_b)).astype(jnp.float32)
    xs, bm, cm = jnp.split(xbc, [SSD_WIDTH, SSD_WIDTH + SSD_GROUPS * SSD_STATE], axis=-1)
    xs = xs.reshape(b, s, SSD_HEADS, SSD_HEAD_DIM)
    bm = bm.reshape(b, s, SSD_GROUPS, SSD_STATE)
    cm = cm.reshape(b, s, SSD_GROUPS, SSD_STATE)
    dt = dt.astype(jnp.float32)
    dtb = dt_bias.astype(jnp.float32)
    dt_f = jax.nn.softplus(dt[..., :SSD_HEADS] + dtb[0])
    dt_b = jax.nn.softplus(dt[..., SSD_HEADS:] + dtb[1])
    a = -jnp.exp(a_log.astype(jnp.float32))
    y_fwd = ssd_scan(xs, dt_f, a[0], bm, cm)
    flip = lambda t: jnp.flip(t, axis=1)
    y_bwd = flip(ssd_scan(flip(xs), flip(dt_b), a[1], flip(bm), flip(cm)))
    y = y_fwd + y_bwd + d_skip.astype(jnp.float32)[:, None] * xs
    y = y.reshape(b, s, SSD_WIDTH) * jax.nn.silu(z.astype(jnp.float32))
    return group_rmsnorm(y, norm_g, SSD_GROUPS, out_dtype)


def short_conv_mixer(gate_b, gate_c, hc, conv_w, conv_b):
    return gate_b * dwconv_centred(gate_c * hc, conv_w, conv_b)


def conv_ffn(h, w_up, conv_w, conv_b, w_down):
    u = dwconv_centred(h @ w_up, conv_w, conv_b)
    gate, up = jnp.split(u, 2, axis=-1)
    return (jax.nn.silu(gate) * up) @ w_down


def setup_inputs(seed: int = 0) -> dict:
    key = jax.random.key(seed)
    ks = jax.random.split(key, 24)
    nrm = lambda k, shape, scale: jax.random.normal(k, shape, jnp.float32) * scale
    gain = lambda k, shape: 1.0 + 0.02 * jax.random.normal(k, shape, jnp.float32)
    dt0 = jnp.exp(jax.random.uniform(ks[5], (DEPTH, 2, SSD_HEADS), jnp.float32,
                                     np.log(1e-3), np.log(1e-1)))
    return {
        'x': jax.random.normal(ks[0], (BATCH, SEQ, D_MODEL), jnp.float32),
        'mix_norm': gain(ks[1], (DEPTH, D_MODEL)),
        'w_in': nrm(ks[2], (DEPTH, D_MODEL, D_IN), D_MODEL ** -0.5),
        'ssd_conv_w': nrm(ks[3], (DEPTH, SSD_CONV, SSD_CONV_DIM), SSD_CONV ** -0.5),
        'ssd_conv_b': nrm(ks[4], (DEPTH, SSD_CONV_DIM), 0.02),
        'ssd_dt_bias': dt0 + jnp.log(-jnp.expm1(-dt0)),
        'ssd_a_log': jnp.log(jax.random.uniform(ks[6], (DEPTH, 2, SSD_HEADS), jnp.float32, 1.0, 16.0)),
        'ssd_d': gain(ks[7], (DEPTH, SSD_HEADS)),
        'ssd_norm': gain(ks[8], (DEPTH, SSD_WIDTH)),
        'sc_conv_w': nrm(ks[9], (DEPTH, SHORT_CONV, CONV_WIDTH), SHORT_CONV ** -0.5),
        'sc_conv_b': nrm(ks[10], (DEPTH, CONV_WIDTH), 0.02),
        'attn_norm': gain(ks[11], (DEPTH, ATT_WIDTH)),
        'sc_norm': gain(ks[12], (DEPTH, CONV_WIDTH)),
        'w_out': nrm(ks[13], (DEPTH, D_MIX, D_MODEL), D_MIX ** -0.5),
        'ffn_norm': gain(ks[14], (DEPTH, D_MODEL)),
        'w_up': nrm(ks[15], (DEPTH, D_MODEL, 2 * D_FF), D_MODEL ** -0.5),
        'ffn_conv_w': nrm(ks[16], (DEPTH, FFN_CONV, 2 * D_FF), FFN_CONV ** -0.5),
        'ffn_conv_b': nrm(ks[17], (DEPTH, 2 * D_FF), 0.02),
        'w_down': nrm(ks[18], (DEPTH, D_FF, D_MODEL), D_FF ** -0.5),
        'final_norm': gain(ks[19], (D_MODEL,)),
    }


def reference(x, mix_norm, w_in, ssd_conv_w, ssd_conv_b, ssd_dt_bias, ssd_a_log, ssd_d,
              ssd_norm, sc_conv_w, sc_conv_b, attn_norm, sc_norm, w_out, ffn_norm, w_up,
              ffn_conv_w, ffn_conv_b, w_down, final_norm):
    for i in range(DEPTH):
        hn = rmsnorm(x, mix_norm[i])
        proj = hn @ w_in[i]
        q, k, v, z, xbc, dt, gate_b, gate_c, hc = jnp.split(proj, SPLIT_POINTS, axis=-1)
        att = group_rmsnorm(dilated_attention(q, k, v), attn_norm[i], ATT_HEADS, x.dtype)
        ssm = ssd_mixer(z, xbc, dt, ssd_conv_w[i], ssd_conv_b[i], ssd_dt_bias[i],
                        ssd_a_log[i], ssd_d[i], ssd_norm[i], x.dtype)
        sc = group_rmsnorm(short_conv_mixer(gate_b, gate_c, hc, sc_conv_w[i], sc_conv_b[i]),
                           sc_norm[i], CONV_GROUPS, x.dtype)
        x = x + jnp.concatenate([att, ssm, sc], axis=-1) @ w_out[i]
        x = x + conv_ffn(rmsnorm(x, ffn_norm[i]), w_up[i], ffn_conv_w[i], ffn_conv_b[i], w_down[i])
    return rmsnorm(x, final_norm)
```

```python
import math
import numpy as np
from contextlib import ExitStack
import concourse.bass as bass
import concourse.mybir as mybir
from concourse.bass_utils import run_bass_kernel_spmd

F32 = mybir.dt.float32
BF16 = mybir.dt.bfloat16
I32 = mybir.dt.int32
AF = mybir.ActivationFunctionType
ALU = mybir.AluOpType

S = 4096
D = 1024
DEPTH = 2
NT = S // 128
D_IN = 4624
D_MIX = 1536
D_FF = 2816
EPS = 1e-6
PADH = 2
PADK = 1024
MASKV = -30000.0
DIL = (1, 4, 16)
C_Q, C_K, C_V, C_Z, C_XBC, C_DT, C_GB, C_GC, C_HC = 0, 512, 1024, 1536, 2048, 3072, 3088, 3600, 4112
FM_COLS = ([C_Q + 128 * i for i in range(4)] + [C_K + 128 * i for i in range(4)] + [C_V + 128 * i for i in range(4)]
           + [C_XBC + 128 * i for i in range(8)] + [C_GB + 128 * i for i in range(4)]
           + [C_GC + 128 * i for i in range(4)] + [C_HC + 128 * i for i in range(4)])
FM_Q, FM_K, FM_V, FM_XBC, FM_GB, FM_GC, FM_HC = 0, 4, 8, 12, 20, 24, 28


def sl(st, n, d):
    return slice(st, st + (n - 1) * d + 1, d)


class R:
    __slots__ = ("w", "rs", "name")

    def __init__(self, name=""):
        self.w = None
        self.rs = []
        self.name = name


class T:
    def __init__(self, t, name=""):
        self.t = t
        self.r = R(name)


class Ring:
    def __init__(self, items):
        self.items = items
        self.i = 0

    def next(self):
        it = self.items[self.i % len(self.items)]
        self.i += 1
        return it


class Eng:
    def __init__(self, cx, name, h, is_pe=False):
        self.name = name
        self.h = h
        self.sem = cx.new_sem("e_" + name)
        self.cnt = 0
        self.waited = {}
        self.is_pe = is_pe
        self.pR = []
        self.pW = []
        self.nins = 0


class DSem:
    def __init__(self, cx, name):
        self.sem = cx.new_sem(name)
        self.cnt = 0


class Cx:
    def __init__(self, nc):
        self.nc = nc
        self.stack = ExitStack()
        self.scopes = []
        self.uid = 0
        self.E = {}
        self.E["pe"] = Eng(self, "pe", nc.tensor, is_pe=True)
        self.E["dve"] = Eng(self, "dve", nc.vector)
        self.E["act"] = Eng(self, "act", nc.scalar)
        self.E["pool"] = Eng(self, "pool", nc.gpsimd)
        self.E["sp"] = Eng(self, "sp", nc.sync)
        self.marks = []
        self.dsems = []
        self.free_ds = []
        self.scope_ds = []
        self.ninst = 0

    def new_sem(self, name):
        return self.stack.enter_context(self.nc.semaphore(name))

    def dsem(self, name=None):
        if self.free_ds:
            d = self.free_ds.pop()
        else:
            self.uid += 1
            d = DSem(self, name or f"d{self.uid}")
            self.dsems.append(d)
        if self.scope_ds:
            self.scope_ds[-1].append(d)
        return d

    def _stk(self):
        return self.scopes[-1] if self.scopes else self.stack

    def sb(self, shape, dtype, name=None):
        self.uid += 1
        nm = (name or "sb") + f"_{self.uid}"
        return T(self._stk().enter_context(self.nc.sbuf_tensor(nm, list(shape), dtype)), nm)

    def ps(self, shape, dtype, name=None):
        self.uid += 1
        nm = (name or "ps") + f"_{self.uid}"
        return T(self._stk().enter_context(self.nc.psum_tensor(nm, list(shape), dtype)), nm)

    def open_scope(self):
        self.scopes.append(ExitStack())
        self.scope_ds.append([])

    def close_scope(self):
        self.barrier()
        self.scopes.pop().close()
        self.free_ds += self.scope_ds.pop()

    def _wait(self, eng, tok):
        if tok is None:
            return
        sem, val, owner = tok
        if owner is eng and eng.is_pe:
            return
        key = id(sem)
        if eng.waited.get(key, 0) >= val:
            return
        eng.h.wait_ge(sem, val)
        eng.waited[key] = val
        self.ninst += 1

    def _deps(self, eng, Rd, Wr):
        for r in Rd:
            self._wait(eng, r.w)
        for w in Wr:
            self._wait(eng, w.w)
            for t in w.rs:
                self._wait(eng, t)

    def _commit(self, tok, Rd, Wr):
        for r in Rd:
            r.rs.append(tok)
            if len(r.rs) > 48:
                best = {}
                for t in r.rs:
                    k = id(t[0])
                    if k not in best or best[k][1] < t[1]:
                        best[k] = t
                r.rs = list(best.values())
        for w in Wr:
            w.w = tok
            w.rs = []

    def op(self, en, fn, Rd=(), Wr=(), inc=True):
        eng = self.E[en]
        Rd = [x.r if isinstance(x, T) else x for x in Rd]
        Wr = [x.r if isinstance(x, T) else x for x in Wr]
        self._deps(eng, Rd, Wr)
        ins = fn(eng.h)
        self.ninst += 1
        eng.nins += 1
        if not inc:
            assert eng.is_pe
            eng.pR += Rd
            eng.pW += Wr
            return None
        eng.cnt += 1
        ins.then_inc(eng.sem, 1)
        tok = (eng.sem, eng.cnt, eng)
        self._commit(tok, list(Rd) + eng.pR, list(Wr) + eng.pW)
        eng.pR = []
        eng.pW = []
        return tok

    def dma(self, qn, out, in_, Rd=(), Wr=(), ds=None, **kw):
        eng = self.E[qn]
        Rd = [x.r if isinstance(x, T) else x for x in Rd]
        Wr = [x.r if isinstance(x, T) else x for x in Wr]
        self._deps(eng, Rd, Wr)
        ins = eng.h.dma_start(out=out, in_=in_, **kw)
        ds.cnt += 16
        ins.then_inc(ds.sem, 16)
        tok = (ds.sem, ds.cnt, ds)
        self._commit(tok, Rd, Wr)
        self.ninst += 1
        return tok

    def barrier(self):
        assert not self.E["pe"].pR and not self.E["pe"].pW
        toks = []
        for e in self.E.values():
            if e.cnt:
                toks.append((e.sem, e.cnt, e))
        for d in self.dsems:
            if d.cnt:
                toks.append((d.sem, d.cnt, d))
        for e in self.E.values():
            for t in toks:
                if t[2] is e:
                    continue
                self._wait(e, t)

    def finish(self):
        self.barrier()
        while self.scopes:
            self.scopes.pop().close()
        self.stack.close()


class Builder:
    def __init__(self, debug=(), layers=DEPTH, phases=None):
        self.debug = set(debug)
        self.layers = layers
        self.phases = phases
        nc = bass.Bass("TRN2", target_bir_lowering=False)
        self.nc = nc
        self.lp = ExitStack()
        self.lp.enter_context(nc.allow_low_precision("bf16 matmul operands, fp32 accumulation (reference tolerance)"))
        self.lp.enter_context(nc.allow_non_contiguous_dma("small gain/bias vector layouts"))
        ein = lambda n, s: nc.dram_tensor(n, list(s), F32, kind="ExternalInput").ap()
        self.x = ein("x", [S, D])
        self.mix_norm = ein("mix_norm", [DEPTH, D])
        self.w_in = ein("w_in", [DEPTH, D, D_IN])
        self.ssd_conv_w = ein("ssd_conv_w", [DEPTH, 5, 1024])
        self.ssd_conv_b = ein("ssd_conv_b", [DEPTH, 1024])
        self.ssd_dt_bias = ein("ssd_dt_bias", [DEPTH, 16])
        self.ssd_a_log = ein("ssd_a_log", [DEPTH, 16])
        self.ssd_d = ein("ssd_d", [DEPTH, 8])
        self.ssd_norm = ein("ssd_norm", [DEPTH, 512])
        self.sc_conv_w = ein("sc_conv_w", [DEPTH, 3, 512])
        self.sc_conv_b = ein("sc_conv_b", [DEPTH, 512])
        self.attn_norm = ein("attn_norm", [DEPTH, 512])
        self.sc_norm = ein("sc_norm", [DEPTH, 512])
        self.w_out = ein("w_out", [DEPTH, D_MIX, D])
        self.ffn_norm = ein("ffn_norm", [DEPTH, D])
        self.w_up = ein("w_up", [DEPTH, D, 2 * D_FF])
        self.ffn_conv_w = ein("ffn_conv_w", [DEPTH, 3, 2 * D_FF])
        self.ffn_conv_b = ein("ffn_conv_b", [DEPTH, 2 * D_FF])
        self.w_down = ein("w_down", [DEPTH, D_FF, D])
        self.final_norm = ein("final_norm", [D])
        self.y = nc.dram_tensor("y", [S, D], F32, kind="ExternalOutput").ap()

        def scr(name, shape, dt):
            kind = "ExternalOutput" if name in self.debug else "Internal"
            return nc.dram_tensor(name, list(shape), dt, kind=kind).ap()

        self.scr = scr
        self.win_fm = scr("win_fm", [DEPTH, 32, 128, 8, 128], BF16)
        self.wz_b = scr("wz_b", [DEPTH, 128, 8, 512], BF16)
        self.wdt_b = scr("wdt_b", [DEPTH, 128, 8, 16], BF16)
        self.wout_b = scr("wout_b", [DEPTH, 128, 12, 1024], BF16)
        self.wup_fm = scr("wup_fm", [DEPTH, 44, 128, 8, 128], BF16)
        self.wdown_b = scr("wdown_b", [DEPTH, 128, 22, 1024], BF16)
        self.xa = scr("xa", [S, D], F32)
        self.xb = scr("xb", [S, D], F32)
        self.mixT = scr("mixT", [D_MIX, S], BF16)
        self.z_d = scr("z_d", [S, 512], F32)
        self.dt_d = scr("dt_d", [S, 16], F32)
        self.BT_d = scr("BT_d", [256, S], BF16)
        self.CT_d = scr("CT_d", [256, S], BF16)
        self.xtok_d = scr("xtok_d", [S, 512], BF16)
        self.Btok_d = scr("Btok_d", [S, 256], BF16)
        self.h2T_d = scr("h2T_d", [D, S], BF16)
        self.cx = Cx(nc)

    def mm(self, out, lhsT, rhs, start, stop, Rd, Wr, inc):
        self.cx.op("pe", lambda e: e.matmul(out, lhsT, rhs, start=start, stop=stop), Rd, Wr, inc=inc)

    def want(self, ph):
        ok = self.phases is None or ph in self.phases
        if ok:
            self.cx.marks.append((ph, {k: e.nins for k, e in self.cx.E.items()}))
        return ok

    def consts(self):
        cx = self.cx
        nc = self.nc
        self.dqr = Ring([cx.dsem() for _ in range(24)])
        di = cx.sb([128, 128], I32, "di")
        dF = cx.sb([128, 128], F32, "dF")
        cx.op("pool", lambda e: e.iota(di.t[:], pattern=[[-1, 128]], base=0, channel_multiplier=1), [], [di])
        cx.op("dve", lambda e: e.tensor_copy(dF.t[:], di.t[:]), [di], [dF])

        def cmpmask(name, opc, dt=F32):
            m = cx.sb([128, 128], dt, name)
            cx.op("dve", lambda e: e.tensor_scalar(m.t[:], dF.t[:], 0.0, None, op0=opc), [dF], [m])
            return m

        self.U_incl = cmpmask("U_incl", ALU.is_le)
        self.L_incl = cmpmask("L_incl", ALU.is_ge)
        self.Lstrict = cmpmask("Lstrict", ALU.is_gt)
        self.Ustrict = cmpmask("Ustrict", ALU.is_lt)
        self.ident_bf = cmpmask("ident_bf", ALU.is_equal, BF16)
        self.ident_f = cmpmask("ident_f", ALU.is_equal, F32)
        self.ones_f = cx.sb([128, 128], F32, "ones_f")
        cx.op("dve", lambda e: e.memset(self.ones_f.t[:], 1.0), [], [self.ones_f])
        self.blk64 = cx.sb([128, 128], F32, "blk64")
        cx.op("dve", lambda e: e.memset(self.blk64.t[:], 0.0), [], [self.blk64])
        cx.op("dve", lambda e: e.memset(self.blk64.t[0:64, 0:64], 1.0), [], [self.blk64])
        cx.op("dve", lambda e: e.memset(self.blk64.t[64:128, 64:128], 1.0), [], [self.blk64])
        self.eps1 = cx.sb([128, 1], F32, "eps1")
        cx.op("dve", lambda e: e.memset(self.eps1.t[:], EPS), [], [self.eps1])
        self.eps64 = cx.sb([128, 1], F32, "eps64")
        cx.op("dve", lambda e: e.memset(self.eps64.t[:], 64.0 * EPS), [], [self.eps64])
        self.W65 = cx.sb([128, 64], BF16, "W65")
        cx.op("dve", lambda e: e.memset(self.W65.t[:], 1.0), [], [self.W65])
        cx.op("dve", lambda e: e.memset(self.W65.t[64:65, :], 64.0 * EPS), [], [self.W65])
        self.blk64b = cx.sb([128, 128], BF16, "blk64b")
        cx.op("dve", lambda e: e.tensor_copy(self.blk64b.t[:], self.blk64.t[:]), [self.blk64], [self.blk64b])
        absA = cx.sb([128, 128], F32, "absA")
        absB = cx.sb([128, 128], F32, "absB")
        mA = cx.sb([128, 128], F32, "mA")
        mB = cx.sb([128, 128], F32, "mB")
        for aX, sh in ((absA, -64.0), (absB, 64.0)):
            cx.op("dve", lambda e, sh=sh: e.tensor_scalar(mA.t[:], dF.t[:], sh, None, op0=ALU.add), [dF], [mA])
            cx.op("dve", lambda e, sh=sh: e.tensor_scalar(mB.t[:], dF.t[:], -1.0, -sh, op0=ALU.mult, op1=ALU.add), [dF], [mB])
            cx.op("dve", lambda e, aX=aX: e.tensor_tensor(aX.t[:], mA.t[:], mB.t[:], ALU.max), [mA, mB], [aX])
        cx.op("dve", lambda e: e.tensor_scalar(mA.t[:], absA.t[:], 64.0, MASKV, op0=ALU.is_gt, op1=ALU.mult), [absA], [mA])
        cx.op("dve", lambda e: e.tensor_scalar(mB.t[:], absB.t[:], 64.0, MASKV, op0=ALU.is_gt, op1=ALU.mult), [absB], [mB])
        self.bias = cx.sb([128, 48, 128], BF16, "attbias")
        for h in range(8):
            slope = 2.0 ** (-8.0 * (h + 1) / 8)
            for b in range(3):
                coef = -slope * DIL[b]
                for ab, (aX, mX) in enumerate(((absA, mA), (absB, mB))):
                    idx = (h * 3 + b) * 2 + ab
                    cx.op("dve", lambda e, idx=idx, aX=aX, mX=mX, coef=coef: e.scalar_tensor_tensor(
                        self.bias.t[:, idx, :], aX.t[:], coef, mX.t[:], op0=ALU.mult, op1=ALU.add),
                        [aX, mX], [self.bias])

    def conv_setup(self, engs, qs_in, qs_out):
        cx = self.cx
        self.cv_stg = Ring([cx.sb([128, 4096], F32, "wstg") for _ in range(2)])
        self.cv_obf = Ring([cx.sb([128, 4096], BF16, "wobf") for _ in range(2)])
        self.cv_din = Ring([cx.dsem() for _ in range(2)])
        self.cv_dout = Ring([cx.dsem() for _ in range(2)])
        self.cv_engs = Ring(engs)
        self.cv_qin = Ring(qs_in)
        self.cv_qout = Ring(qs_out)

    def _cv_load(self, src, kc, nb):
        cx = self.cx
        s = self.cv_stg.next()
        n = kc * nb
        cx.dma(self.cv_qin.next(), s.t[:, 0:n].rearrange("p (k n) -> p k n", k=kc), src.rearrange("(k p) n -> p k n", p=128),
               [], [s], ds=self.cv_din.next())

        def cast(perm):
            o = self.cv_obf.next()
            en = self.cv_engs.next()
            if perm:
                nchunk = nb // 128
                ov = o.t[:, 0:n].rearrange("p (c k n) -> p c k n", c=nchunk, k=kc)
                iv = s.t[:, 0:n].rearrange("p (k c n) -> p c k n", k=kc, c=nchunk)
                for c in range(nchunk):
                    if en == "act":
                        cx.op(en, lambda e, c=c: e.copy(ov[:, c], iv[:, c]), [s], [o])
                    else:
                        cx.op(en, lambda e, c=c: e.tensor_copy(ov[:, c], iv[:, c]), [s], [o])
            else:
                if en == "act":
                    cx.op(en, lambda e: e.copy(o.t[:, 0:n], s.t[:, 0:n]), [s], [o])
                else:
                    cx.op(en, lambda e: e.tensor_copy(o.t[:, 0:n], s.t[:, 0:n]), [s], [o])
            return o, n
        return cast

    def _cv_fm(self, src, kc, nchunk, dst):
        cast = self._cv_load(src, kc, nchunk * 128)

        def fin():
            o, n = cast(True)
            self.cx.dma(self.cv_qout.next(), dst.rearrange("c p k n -> p c k n"),
                        o.t[:, 0:n].rearrange("p (c k n) -> p c k n", c=nchunk, k=kc), [o], [], ds=self.cv_dout.next())
        return fin

    def _cv_r(self, src, kc, nb, dst):
        cast = self._cv_load(src, kc, nb)

        def fin():
            o, n = cast(False)
            self.cx.dma(self.cv_qout.next(), dst, o.t[:, 0:n].rearrange("p (k n) -> p k n", k=kc), [o], [], ds=self.cv_dout.next())
        return fin

    def conv_jobs(self, li, part):
        jobs = []
        if part == "in":
            w = self.w_in
            for seg in range(8):
                c0 = FM_COLS[seg * 4]
                jobs.append(lambda seg=seg, c0=c0: self._cv_fm(w[li, :, c0:c0 + 512], 8, 4, self.win_fm[li, seg * 4:seg * 4 + 4]))
            jobs.append(lambda: self._cv_r(w[li, :, C_Z:C_Z + 512], 8, 512, self.wz_b[li]))
            jobs.append(lambda: self._cv_r(w[li, :, C_DT:C_DT + 16], 8, 16, self.wdt_b[li]))
        else:
            for j in range(4):
                jobs.append(lambda j=j: self._cv_r(self.w_out[li, :, j * 256:(j + 1) * 256], 12, 256, self.wout_b[li, :, :, j * 256:(j + 1) * 256]))
            for j in range(11):
                jobs.append(lambda j=j: self._cv_fm(self.w_up[li, :, j * 512:(j + 1) * 512], 8, 4, self.wup_fm[li, j * 4:j * 4 + 4]))
            for j in range(8):
                jobs.append(lambda j=j: self._cv_r(self.w_down[li, :, j * 128:(j + 1) * 128], 22, 128, self.wdown_b[li, :, :, j * 128:(j + 1) * 128]))
        return jobs

    def conv_tick(self):
        nxt = self.cv_jobs.pop(0)() if self.cv_jobs else None
        if self.cv_pending is not None:
            self.cv_pending()
        self.cv_pending = nxt

    def conv_flush(self):
        while self.cv_jobs or self.cv_pending is not None:
            self.conv_tick()

    def norm_setup(self):
        cx = self.cx
        self.n_tp = Ring([cx.ps([128, 8, 128], BF16, "n_tp") for _ in range(1)])
        self.gstg = Ring([cx.sb([64, 128], F32, "gstg") for _ in range(2)])

    def norm_bufs(self, nx=3, with_norm=True):
        cx = self.cx
        self.n_xt = Ring([cx.sb([128, D], F32, "n_xt") for _ in range(nx)])
        self.n_dx = Ring([cx.dsem() for _ in range(nx + 1)])
        if with_norm:
            self.n_junk = cx.sb([128, D], BF16, "n_junk")
            self.n_ss = Ring([cx.sb([128, 2], F32, "n_ss") for _ in range(3)])
            self.n_xn = Ring([cx.sb([128, D], BF16, "n_xn") for _ in range(2)])
            self.n_tp2 = Ring([self.n_tp.items[0], cx.ps([128, 8, 128], BF16, "n_tp2")])

    def _row_T(self, src_row, nchunk, dst_ap, dst_T, mult=1.0):
        cx = self.cx
        stg = self.gstg.next()
        cx.dma("sp", stg.t[0:nchunk, :], src_row.rearrange("(c p) -> c p", p=128), [], [stg], ds=self.dqr.next())
        tp = self.n_tp.next()
        pv = tp.t[:].rearrange("p a b -> p (a b)").bitcast(F32)
        self.mm(pv[:, 0:nchunk], stg.t[0:nchunk, :], self.ident_f.t[0:nchunk, 0:nchunk], True, True, [stg, self.ident_f], [tp], True)
        cx.op("dve", lambda e: e.tensor_scalar(dst_ap, pv[:, 0:nchunk], mult, None, op0=ALU.mult), [tp], [dst_T])

    def load_gT(self, src_row, nchunk, name, mult=1.0):
        g = self.cx.sb([128, nchunk], F32, name)
        self._row_T(src_row, nchunk, g.t[:, :], g, mult)
        return g

    def rstd_of(self, x_ap, xT, ss, n):
        cx = self.cx
        cx.op("dve", lambda e: e.scalar_tensor_tensor(self.n_junk.t[:, 0:n], x_ap, 1.0, x_ap, op0=ALU.mult, op1=ALU.mult,
                                                      accum_out=ss.t[:, 0:1]), [xT], [self.n_junk, ss])
        cx.op("act", lambda e: e.activation(ss.t[:, 1:2], ss.t[:, 0:1], AF.Ln, bias=self.eps1.t[:, 0:1], scale=1.0 / n), [ss, self.eps1], [ss])
        cx.op("act", lambda e: e.activation(ss.t[:, 1:2], ss.t[:, 1:2], AF.Exp, scale=-0.5), [ss], [ss])

    def norm_tile_a(self, xt):
        cx = self.cx
        ss = self.n_ss.next()
        xn = self.n_xn.next()
        tp = self.n_tp2.next()
        self.rstd_of(xt.t[:], xt, ss, D)
        cx.op("act", lambda e: e.activation(xn.t[:], xt.t[:], AF.Copy, scale=ss.t[:, 1:2]), [xt, ss], [xn])
        for j in range(8):
            cx.op("pe", lambda e, j=j: e.transpose(tp.t[:, j, :], xn.t[:, j * 128:(j + 1) * 128], self.ident_bf.t[:]),
                  [xn, self.ident_bf], [tp], inc=(j == 7))
        return tp

    def norm_tile_b(self, tp, gT, out_ap, out_R):
        self.cx.op("dve", lambda e: e.tensor_tensor(out_ap, tp.t[:], gT.t[:, :].unsqueeze(2).to_broadcast([128, 8, 128]), ALU.mult),
                   [tp, gT], [out_R])

    def phase_norm(self, x_src, gT, hT):
        cx = self.cx
        cx.open_scope()
        self.norm_bufs()
        prev = None
        for tt in range(NT):
            xt = self.n_xt.next()
            cx.dma("sp", xt.t[:], x_src[tt * 128:(tt + 1) * 128, :], [], [xt], ds=self.n_dx.next())
            tp = self.norm_tile_a(xt)
            if prev is not None:
                self.norm_tile_b(prev[0], gT, hT.t[:, :, PADH + prev[1] * 128:PADH + (prev[1] + 1) * 128], hT)
            prev = (tp, tt)
            if tt % 3 == 0:
                self.conv_tick()
        self.norm_tile_b(prev[0], gT, hT.t[:, :, PADH + prev[1] * 128:PADH + (prev[1] + 1) * 128], hT)
        cx.close_scope()

    def phase_attn(self, li, hT):
        cx = self.cx
        cx.open_scope()
        qT = cx.sb([128, S], BF16, "qT")
        kTs = [cx.sb([128, S + 2 * PADK], BF16, "kT0"), cx.sb([128, S + 2 * PADK], BF16, "kT1")]
        vT = cx.sb([128, S + 2 * PADK], BF16, "vT")
        NV = 117
        V = cx.sb([128, NV, 2, 65], BF16, "Vaug")
        acc = cx.sb([65, 1, S], F32, "acc")
        wq = cx.sb([128, 8, 128], BF16, "wq")
        wk = cx.sb([128, 8, 128], BF16, "wk")
        wv = cx.sb([128, 8, 128], BF16, "wv")
        dw = [cx.dsem() for _ in range(3)]
        pT = Ring([cx.sb([128, 8, 128], BF16, "pT") for _ in range(2)])
        nrm_a = Ring([cx.sb([64, 512], F32, "nrm_a") for _ in range(2)])
        nrm_b = Ring([cx.sb([64, 512], F32, "nrm_b") for _ in range(2)])
        nrm_c = Ring([cx.sb([65, 512], BF16, "nrm_c") for _ in range(2)])
        nrm_o = Ring([cx.sb([64, 512], BF16, "nrm_o") for _ in range(2)])
        d_o = Ring([cx.dsem() for _ in range(2)])
        banks7 = [cx.ps([128, 512], F32, "att_ps") for _ in range(7)]
        ps_proj = Ring(banks7)
        ps_s = Ring(banks7[0:4])
        ps_o = Ring(banks7[4:7])
        cx.op("pool", lambda e: e.memset(kTs[0].t[64:128, :], 0.0), [], [kTs[0]])
        cx.op("pool", lambda e: e.memset(kTs[1].t[0:64, :], 0.0), [], [kTs[1]])
        cx.op("pool", lambda e: e.memset(kTs[0].t[0:64, 0:PADK], 0.0), [], [kTs[0]])
        cx.op("pool", lambda e: e.memset(kTs[0].t[0:64, PADK + S:], 0.0), [], [kTs[0]])
        cx.op("pool", lambda e: e.memset(kTs[1].t[64:128, 0:PADK], 0.0), [], [kTs[1]])
        cx.op("pool", lambda e: e.memset(kTs[1].t[64:128, PADK + S:], 0.0), [], [kTs[1]])
        cx.op("pool", lambda e: e.memset(vT.t[:, 0:PADK], 0.0), [], [vT])
        cx.op("pool", lambda e: e.memset(vT.t[:, PADK + S:], 0.0), [], [vT])
        cx.op("pool", lambda e: e.memset(V.t[:, :, :, 64:65], 1.0), [], [V])
        voff = []
        o = 0
        for b in range(3):
            voff.append(o)
            o += DIL[b] * (S // DIL[b] // 128 + 1)
        assert o == NV
        for b in range(3):
            d = DIL[b]
            ntq = S // d // 128
            for c in range(d):
                i0 = voff[b] + c * (ntq + 1)
                cx.op("pool", lambda e, i0=i0: e.memset(V.t[0:64, i0, :, 64:65], 0.0), [], [V])
                cx.op("pool", lambda e, i1=i0 + ntq: e.memset(V.t[64:128, i1, :, 64:65], 0.0), [], [V])

        def vps(ps):
            return ps.t[:].rearrange("p (a b) -> p a b", a=4)

        evac = Ring(["act", "dve"])
        for hp in range(4):
            for wt, fm0, ds in ((wq, FM_Q, dw[0]), (wk, FM_K, dw[1]), (wv, FM_V, dw[2])):
                cx.dma("sp", wt.t[:], self.win_fm[li, fm0 + hp], [], [wt], ds=ds)
            for which, wt in enumerate((wq, wk, wv)):
                for tb in range(8):
                    ps = ps_proj.next()
                    for kc in range(8):
                        self.mm(ps.t[:], wt.t[:, kc, :], hT.t[:, kc, PADH + tb * 512:PADH + (tb + 1) * 512],
                                kc == 0, kc == 7, [wt, hT], [ps], kc == 7)
                    en = evac.next()
                    if which == 0:
                        outs = [(qT, qT.t[:, tb * 512:(tb + 1) * 512], ps.t[:], 0.125)]
                    elif which == 1:
                        cs_ = slice(PADK + tb * 512, PADK + (tb + 1) * 512)
                        outs = [(kTs[0], kTs[0].t[0:64, cs_], ps.t[0:64, :], 1.0), (kTs[1], kTs[1].t[64:128, cs_], ps.t[64:128, :], 1.0)]
                    else:
                        outs = [(vT, vT.t[:, PADK + tb * 512:PADK + (tb + 1) * 512], ps.t[:], 1.0)]
                    for (dstT, dap, sap, scale) in outs:
                        if en == "act":
                            cx.op("act", lambda e, dap=dap, sap=sap, scale=scale: e.activation(dap, sap, AF.Copy, scale=scale), [ps], [dstT])
                        else:
                            cx.op("dve", lambda e, dap=dap, sap=sap, scale=scale: e.tensor_scalar(dap, sap, scale, None, op0=ALU.mult), [ps], [dstT])
            for b in range(3):
                d = DIL[b]
                ntq = S // d // 128
                for c in range(d):
                    m = 0
                    while m < ntq + 1:
                        g = min(4, ntq + 1 - m)
                        ps = ps_proj.next()
                        pv = ps.t[:].bitcast(BF16)
                        for j in range(g):
                            st = PADK + d * (128 * (m + j) - 64) + c
                            cx.op("pe", lambda e, j=j, st=st, d=d, pv=pv: e.transpose(
                                pv[:, j * 128:(j + 1) * 128], vT.t[:, sl(st, 128, d)], self.ident_bf.t[:]),
                                [vT, self.ident_bf], [ps], inc=(j == g - 1))
                        i0 = voff[b] + c * (ntq + 1) + m
                        en = evac.next()
                        src = pv[:, 0:g * 128].rearrange("p (g h f) -> p g h f", g=g, h=2)
                        dstap = V.t[:, i0:i0 + g, :, 0:64]
                        if en == "act":
                            cx.op("act", lambda e, src=src, dstap=dstap: e.copy(dstap, src), [ps], [V])
                        else:
                            cx.op("dve", lambda e, src=src, dstap=dstap: e.tensor_copy(dstap, src), [ps], [V])
                        m += g
            for hh in range(2):
                h = hp * 2 + hh
                kT = kTs[hh]
                groups = []
                for b in range(3):
                    d = DIL[b]
                    ntq = S // d // 128
                    G = min(4, ntq)
                    for c in range(d):
                        for j0 in range(0, ntq, G):
                            groups.append((b, d, ntq, G, c, j0))

                def emit_S(grp):
                    b, d, ntq, G, c, j0 = grp
                    banks = [ps_s.next() for _ in range((2 * G + 3) // 4)]
                    for jl in range(G):
                        j = j0 + jl
                        for ab in range(2):
                            slot = jl * 2 + ab
                            bank = banks[slot // 4]
                            ks = 128 * j - 64 + 128 * ab
                            kc0 = PADK + d * ks + c
                            qc0 = d * 128 * j + c
                            oap = vps(bank)[:, slot % 4, :]
                            self.mm(oap, kT.t[:, sl(kc0, 128, d)], qT.t[:, sl(qc0, 128, d)],
                                    slot % 4 == 0, False, [kT, qT], [bank], False)
                            if slot % 4 == 3:
                                i0 = (h * 3 + b) * 2
                                self.mm(bank.t[:], self.ident_bf.t[:],
                                        self.bias.t[:, i0:i0 + 2, :].unsqueeze(1).to_broadcast([128, 2, 2, 128]),
                                        False, True, [self.ident_bf, self.bias], [bank], True)
                    return banks

                def emit_rest(grp, banks):
                    b, d, ntq, G, c, j0 = grp
                    pt = pT.next()
                    for bi, bank in enumerate(banks):
                        ns = min(4, 2 * G - bi * 4)
                        cx.op("act", lambda e, bank=bank, bi=bi, ns=ns, pt=pt: e.activation(
                            pt.t[:, bi * 4:bi * 4 + ns, :], vps(bank)[:, 0:ns, :], AF.Exp), [bank], [pt])
                    po = ps_o.next()
                    for jl in range(G):
                        j = j0 + jl
                        for ab in range(2):
                            vi = voff[b] + c * (ntq + 1) + j + ab
                            self.mm(po.t[0:65, jl * 128:(jl + 1) * 128], V.t[:, vi, hh, :], pt.t[:, jl * 2 + ab, :],
                                    ab == 0, ab == 1, [V, pt], [po], (jl == G - 1 and ab == 1))
                    t0 = d * 128 * j0 + c
                    aap = acc.t[0:65, 0, sl(t0, G * 128, d)]
                    if b == 0:
                        cx.op("dve", lambda e, aap=aap, po=po, G=G: e.tensor_copy(aap, po.t[0:65, 0:G * 128]), [po], [acc])
                    else:
                        cx.op("dve", lambda e, aap=aap, po=po, G=G: e.tensor_tensor(aap, po.t[0:65, 0:G * 128], aap, ALU.add),
                              [po, acc], [acc])

                prev = None
                for grp in groups:
                    bk = emit_S(grp)
                    if prev is not None:
                        emit_rest(*prev)
                    prev = (grp, bk)
                emit_rest(*prev)
                def n1(tb):
                    cs = slice(tb * 512, (tb + 1) * 512)
                    rc = nrm_c.next()
                    cx.op("dve", lambda e: e.tensor_tensor(rc.t[:], acc.t[0:65, 0, cs], acc.t[0:65, 0, cs], ALU.mult), [acc], [rc])
                    ps2 = ps_proj.next()
                    self.mm(ps2.t[0:64, :], self.W65.t[0:65, :], rc.t[:], True, True, [self.W65, rc], [ps2], True)
                    return (cs, ps2)

                def n2(st, hh=hh, hp=hp):
                    cs, ps2 = st
                    ra = nrm_a.next()
                    cx.op("act", lambda e: e.activation(ra.t[:], ps2.t[0:64, :], AF.Ln), [ps2], [ra])
                    ra2 = nrm_b.next()
                    cx.op("act", lambda e: e.activation(ra2.t[:], ra.t[:], AF.Exp, scale=-0.5), [ra], [ra2])
                    ro = nrm_o.next()
                    cx.op("dve", lambda e: e.scalar_tensor_tensor(
                        ro.t[:], acc.t[0:64, 0, cs], self.g8c.t[:, hp * 2 + hh:hp * 2 + hh + 1], ra2.t[:], op0=ALU.mult, op1=ALU.mult), [acc, ra2, self.g8c], [ro])
                    row0 = (hp * 2 + hh) * 64
                    cx.dma("act", self.mixT[row0:row0 + 64, cs], ro.t[:], [ro], [], ds=d_o.next())

                pv_ = None
                for tb in range(8):
                    cur_ = n1(tb)
                    if pv_ is not None:
                        n2(pv_)
                    pv_ = cur_
                n2(pv_)
        cx.close_scope()

    def load_bcast(self, src_row, n, name):
        cx = self.cx
        t = cx.sb([128, n], F32, name)
        cx.dma("sp", t.t[:, :], src_row.unsqueeze(0).partition_broadcast(128)[:, 0, :], [], [t], ds=self.dqr.next())
        return t

    def load_cw(self, src, k, nchunk, name):
        t = self.cx.sb([128, nchunk, k], F32, name)
        for kk in range(k):
            self._row_T(src[kk, :], nchunk, t.t[:, :, kk], t)
        return t

    def phase_zdt(self, li, hT):
        cx = self.cx
        cx.open_scope()
        wz = cx.sb([128, 8, 512], BF16, "wz")
        wdt = cx.sb([128, 8, 16], BF16, "wdt")
        cx.dma("sp", wz.t[:], self.wz_b[li], [], [wz], ds=cx.dsem())
        cx.dma("sp", wdt.t[:], self.wdt_b[li], [], [wdt], ds=cx.dsem())
        zs = Ring([cx.sb([128, 512], F32, "zs") for _ in range(2)])
        dts = Ring([cx.sb([128, 16], F32, "dts") for _ in range(2)])
        dz = Ring([cx.dsem() for _ in range(2)])
        dd = Ring([cx.dsem() for _ in range(2)])
        psz = Ring([cx.ps([128, 512], F32, "psz") for _ in range(2)])
        psd = Ring([cx.ps([128, 512], F32, "psd") for _ in range(2)])
        for tt in range(NT):
            pz = psz.next()
            pd = psd.next()
            tok = slice(PADH + tt * 128, PADH + (tt + 1) * 128)
            for kc in range(8):
                self.mm(pz.t[:], hT.t[:, kc, tok], wz.t[:, kc, :], kc == 0, kc == 7, [hT, wz], [pz], kc == 7)
            for kc in range(8):
                self.mm(pd.t[:, 0:16], hT.t[:, kc, tok], wdt.t[:, kc, :], kc == 0, kc == 7, [hT, wdt], [pd], kc == 7)
            z = zs.next()
            cx.op("act", lambda e, z=z, pz=pz: e.copy(z.t[:], pz.t[:]), [pz], [z])
            cx.dma("act", self.z_d[tt * 128:(tt + 1) * 128, :], z.t[:], [z], [], ds=dz.next())
            dt = dts.next()
            cx.op("dve", lambda e, dt=dt, pd=pd: e.tensor_copy(dt.t[:], pd.t[:, 0:16]), [pd], [dt])
            cx.dma("act", self.dt_d[tt * 128:(tt + 1) * 128, :], dt.t[:], [dt], [], ds=dd.next())
        cx.close_scope()

    def phase_xbc(self, li, hT):
        cx = self.cx
        cx.open_scope()
        cw = self.load_cw(self.ssd_conv_w[li], 5, 8, "xbc_cw")
        cb = self.load_gT(self.ssd_conv_b[li], 8, "xbc_cb")
        wts = Ring([cx.sb([128, 8, 128], BF16, "xbc_w") for _ in range(2)])
        dws = Ring([cx.dsem() for _ in range(2)])
        rows = Ring([cx.sb([128, S], BF16, "xbc_row") for _ in range(2)])
        drow = Ring([cx.dsem() for _ in range(2)])
        accs = Ring([cx.sb([128, 512], F32, "xbc_acc") for _ in range(2)])
        toks = Ring([cx.sb([128, 32, 128], BF16, "xbc_tok") for _ in range(2)])
        dtok = Ring([cx.dsem() for _ in range(2)])
        pss = Ring([cx.ps([128, 512], F32, "xbc_ps") for _ in range(3)])
        pst = Ring([cx.ps([128, 4, 128], BF16, "xbc_pst") for _ in range(2)])
        W = 508
        for fc in range(8):
            wt = wts.next()
            cx.dma("sp", wt.t[:], self.win_fm[li, FM_XBC + fc], [], [wt], ds=dws.next())
            row = rows.next()
            for t0 in range(0, S, W):
                w = min(W, S - t0)
                n = w + 4
                ps = pss.next()
                for kc in range(8):
                    self.mm(ps.t[:, 0:n], wt.t[:, kc, :], hT.t[:, kc, PADH + t0 - 2:PADH + t0 - 2 + n], kc == 0, kc == 7, [wt, hT], [ps], kc == 7)
                acc = accs.next()
                cx.op("act", lambda e, acc=acc, ps=ps, w=w, fc=fc: e.activation(acc.t[:, 0:w], ps.t[:, 2:2 + w], AF.Identity,
                      bias=cb.t[:, fc:fc + 1], scale=cw.t[:, fc, 2:3]), [ps, cb, cw], [acc])
                for k in (0, 1, 3, 4):
                    cx.op("dve", lambda e, acc=acc, ps=ps, w=w, fc=fc, k=k: e.scalar_tensor_tensor(
                        acc.t[:, 0:w], ps.t[:, k:k + w], cw.t[:, fc, k:k + 1], acc.t[:, 0:w], op0=ALU.mult, op1=ALU.add), [ps, cw, acc], [acc])
                cx.op("act", lambda e, acc=acc, row=row, t0=t0, w=w: e.activation(row.t[:, t0:t0 + w], acc.t[:, 0:w], AF.Silu), [acc], [row])
            if fc >= 4:
                dst = self.BT_d if fc < 6 else self.CT_d
                r0 = ((fc - 4) % 2) * 128
                cx.dma("act", dst[r0:r0 + 128, :], row.t[:], [row], [], ds=drow.next())
            if fc < 6:
                tokb = toks.next()
                for tq in range(8):
                    pt = pst.next()
                    for j in range(4):
                        tt = tq * 4 + j
                        cx.op("pe", lambda e, pt=pt, j=j, tt=tt, row=row: e.transpose(pt.t[:, j, :], row.t[:, tt * 128:(tt + 1) * 128], self.ident_bf.t[:]),
                              [row, self.ident_bf], [pt], inc=(j == 3))
                    cx.op("act", lambda e, pt=pt, tokb=tokb, tq=tq: e.copy(tokb.t[:, tq * 4:tq * 4 + 4, :], pt.t[:]), [pt], [tokb])
                if fc < 4:
                    dst = self.xtok_d[:, fc * 128:(fc + 1) * 128]
                else:
                    dst = self.Btok_d[:, (fc - 4) * 128:(fc - 3) * 128]
                dv = dst.rearrange("(t p) f -> p t f", p=128)
                dk = dtok.next()
                for q4 in range(4):
                    cx.dma("act", dv[:, q4 * 8:(q4 + 1) * 8, :], tokb.t[:, q4 * 8:(q4 + 1) * 8, :], [tokb], [], ds=dk)
        cx.close_scope()

    def phase_sc(self, li, hT):
        cx = self.cx
        cx.open_scope()
        cw = self.load_cw(self.sc_conv_w[li], 3, 4, "sc_cw")
        cb = self.load_gT(self.sc_conv_b[li], 4, "sc_cb")
        g8 = self.load_gT(self.sc_norm[li], 4, "sc_g8", mult=8.0)
        wts = [Ring([cx.sb([128, 8, 128], BF16, "sc_w") for _ in range(2)]) for _ in range(3)]
        dws = [Ring([cx.dsem() for _ in range(2)]) for _ in range(3)]
        pss = [Ring([cx.ps([128, 512], F32, "sc_ps") for _ in range(2)]) for _ in range(3)]
        psn = cx.ps([128, 512], F32, "sc_psn")
        gcs = Ring([cx.sb([128, 512], F32, "sc_gcs") for _ in range(2)])
        tts = Ring([cx.sb([128, 512], F32, "sc_tt") for _ in range(2)])
        accs = Ring([cx.sb([128, 512], F32, "sc_acc") for _ in range(2)])
        yvs = Ring([cx.sb([128, 512], F32, "sc_yv") for _ in range(2)])
        ysq = Ring([cx.sb([128, 512], BF16, "sc_ysq") for _ in range(2)])
        rrs = Ring([cx.sb([128, 512], F32, "sc_rr") for _ in range(2)])
        outs = Ring([cx.sb([128, 512], BF16, "sc_out") for _ in range(2)])
        douts = Ring([cx.dsem() for _ in range(2)])
        W = 510
        for c4 in range(4):
            ws = []
            for i, fm0 in enumerate((FM_GB, FM_GC, FM_HC)):
                wt = wts[i].next()
                cx.dma("sp", wt.t[:], self.win_fm[li, fm0 + c4], [], [wt], ds=dws[i].next())
                ws.append(wt)
            def sc_proj(t0, ws=ws):
                w = min(W, S - t0)
                n = w + 2
                pp = []
                for i in range(3):
                    ps = pss[i].next()
                    for kc in range(8):
                        self.mm(ps.t[:, 0:n], ws[i].t[:, kc, :], hT.t[:, kc, PADH + t0 - 1:PADH + t0 - 1 + n], kc == 0, kc == 7, [ws[i], hT], [ps], kc == 7)
                    pp.append(ps)
                return (t0, w, n, pp)

            def sc_rest(st, c4=c4):
                t0, w, n, pp = st
                pgb, pgc, phc = pp
                gc = gcs.next()
                cx.op("act", lambda e, gc=gc, pgc=pgc, n=n: e.copy(gc.t[:, 0:n], pgc.t[:, 0:n]), [pgc], [gc])
                tt = tts.next()
                cx.op("dve", lambda e, tt=tt, phc=phc, gc=gc, n=n: e.tensor_tensor(tt.t[:, 0:n], phc.t[:, 0:n], gc.t[:, 0:n], ALU.mult), [phc, gc], [tt])
                acc = accs.next()
                cx.op("act", lambda e, acc=acc, tt=tt, w=w, c4=c4: e.activation(acc.t[:, 0:w], tt.t[:, 1:1 + w], AF.Identity,
                      bias=cb.t[:, c4:c4 + 1], scale=cw.t[:, c4, 1:2]), [tt, cb, cw], [acc])
                for k in (0, 2):
                    cx.op("dve", lambda e, acc=acc, tt=tt, w=w, c4=c4, k=k: e.scalar_tensor_tensor(
                        acc.t[:, 0:w], tt.t[:, k:k + w], cw.t[:, c4, k:k + 1], acc.t[:, 0:w], op0=ALU.mult, op1=ALU.add), [tt, cw, acc], [acc])
                yv = yvs.next()
                cx.op("dve", lambda e, yv=yv, pgb=pgb, acc=acc, w=w: e.tensor_tensor(yv.t[:, 0:w], pgb.t[:, 1:1 + w], acc.t[:, 0:w], ALU.mult), [pgb, acc], [yv])
                yq = ysq.next()
                cx.op("dve", lambda e, yq=yq, yv=yv, w=w: e.tensor_tensor(yq.t[:, 0:w], yv.t[:, 0:w], yv.t[:, 0:w], ALU.mult), [yv], [yq])
                self.mm(psn.t[:, 0:w], self.blk64b.t[:], yq.t[:, 0:w], True, True, [self.blk64b, yq], [psn], True)
                rr = rrs.next()
                cx.op("act", lambda e, rr=rr, w=w: e.activation(rr.t[:, 0:w], psn.t[:, 0:w], AF.Ln, bias=self.eps64.t[:, 0:1]), [psn, self.eps64], [rr])
                cx.op("act", lambda e, rr=rr, w=w: e.activation(rr.t[:, 0:w], rr.t[:, 0:w], AF.Exp, scale=-0.5), [rr], [rr])
                ob = outs.next()
                cx.op("dve", lambda e, ob=ob, yv=yv, rr=rr, w=w, c4=c4: e.scalar_tensor_tensor(
                    ob.t[:, 0:w], yv.t[:, 0:w], g8.t[:, c4:c4 + 1], rr.t[:, 0:w], op0=ALU.mult, op1=ALU.mult), [yv, g8, rr], [ob])
                cx.dma("act", self.mixT[1024 + c4 * 128:1024 + (c4 + 1) * 128, t0:t0 + w], ob.t[:, 0:w], [ob], [], ds=douts.next())

            prev = None
            for t0 in range(0, S, W):
                cur = sc_proj(t0)
                if prev is not None:
                    sc_rest(prev)
                prev = cur
            sc_rest(prev)
        cx.close_scope()

    def phase_ssd(self, li):
        cx = self.cx
        cx.open_scope()
        bias16 = self.load_bcast(self.ssd_dt_bias[li], 16, "ssd_bias16")
        a16 = self.load_bcast(self.ssd_a_log[li], 16, "ssd_a16")
        cx.op("act", lambda e: e.activation(a16.t[:], a16.t[:], AF.Exp), [a16], [a16])
        cx.op("dve", lambda e: e.tensor_scalar(a16.t[:], a16.t[:], -1.0, None, op0=ALU.mult), [a16], [a16])
        d8 = self.load_bcast(self.ssd_d[li], 8, "ssd_d8")
        Dfull = cx.sb([128, 8, 64], F32, "ssd_Dfull")
        cx.op("dve", lambda e: e.tensor_copy(Dfull.t[:], d8.t[:, :].unsqueeze(2).to_broadcast([128, 8, 64])), [d8], [Dfull])
        gS = self.load_gT(self.ssd_norm[li], 4, "ssd_gS")
        prevB = cx.sb([128, NT, 512], BF16, "ssd_prevB")
        state_f = cx.sb([128, 512], F32, "ssd_state_f")
        state_b = cx.sb([128, 512], F32, "ssd_state_b")
        stf_bf = cx.sb([128, 512], BF16, "ssd_stf_bf")
        cx.op("pool", lambda e: e.memset(state_f.t[:], 0.0), [], [state_f])
        cx.op("pool", lambda e: e.memset(state_b.t[:], 0.0), [], [state_b])
        cx.op("pool", lambda e: e.memset(stf_bf.t[:], 0.0), [], [stf_bf])
        dtrs = Ring([cx.sb([128, 16], F32, "ssd_dtr") for _ in range(4)])
        xts = Ring([cx.sb([128, 512], BF16, "ssd_xt") for _ in range(4)])
        bts = Ring([cx.sb([128, 256], BF16, "ssd_bt") for _ in range(4)])
        BTs = Ring([cx.sb([128, 2, 128], BF16, "ssd_BT") for _ in range(4)])
        CTs = Ring([cx.sb([128, 2, 128], BF16, "ssd_CT") for _ in range(4)])
        zts = Ring([cx.sb([128, 512], F32, "ssd_zt") for _ in range(4)])
        dl = [Ring([cx.dsem() for _ in range(5)]) for _ in range(6)]
        t16 = Ring([cx.sb([128, 16], F32, "ssd_t16") for _ in range(8)])
        dts_ = Ring([cx.sb([128, 16], F32, "ssd_dt") for _ in range(4)])
        acs = Ring([cx.sb([128, 16], F32, "ssd_ac") for _ in range(4)])
        Es = Ring([cx.sb([128, 32], F32, "ssd_E") for _ in range(4)])
        wdts = Ring([cx.sb([128, 8], F32, "ssd_wdt") for _ in range(4)])
        xdtf = Ring([cx.sb([128, 512], BF16, "ssd_xdtf") for _ in range(2)])
        xdtb = Ring([cx.sb([128, 512], BF16, "ssd_xdtb") for _ in range(2)])
        xwf = Ring([cx.sb([128, 512], BF16, "ssd_xwf") for _ in range(2)])
        xDs = Ring([cx.sb([128, 512], BF16, "ssd_xD") for _ in range(2)])
        Gmf = Ring([cx.sb([128, 2, 128], F32, "ssd_Gmf") for _ in range(2)])
        Gmb = Ring([cx.sb([128, 2, 128], F32, "ssd_Gmb") for _ in range(2)])
        lhss = Ring([cx.sb([128, 128], F32, "ssd_lhs") for _ in range(8)])
        expds = Ring([cx.sb([128, 4, 128], F32, "ssd_expd") for _ in range(2)])
        MTs = [Ring([cx.sb([128, 8, 128], BF16, "ssd_MT") for _ in range(2)]) for _ in range(2)]
        y1s = Ring([cx.sb([128, 512], F32, "ssd_y1") for _ in range(2)])
        tmps = Ring([cx.sb([128, 512], F32, "ssd_tmp") for _ in range(2)])
        szs = Ring([cx.sb([128, 512], F32, "ssd_sz") for _ in range(2)])
        ss2 = Ring([cx.sb([128, 4], F32, "ssd_ss2") for _ in range(2)])
        yns = Ring([cx.sb([128, 512], BF16, "ssd_yn") for _ in range(2)])
        sTs = Ring([cx.sb([128, 4, 512], BF16, "ssd_sT") for _ in range(2)])
        dsT = Ring([cx.dsem() for _ in range(2)])
        junk = cx.sb([128, 256], F32, "ssd_junk")
        psA = cx.ps([128, 512], F32, "ssd_psA")
        RpsG = psA.r
        psS_T = cx.ps([128, 512], F32, "ssd_psS")
        RpsS = psS_T.r
        psS = psS_T.t
        diffs = Ring([cx.ps([128, 4, 128], F32, "ssd_diff") for _ in range(2)])
        psy = cx.ps([128, 512], F32, "ssd_psy")
        psyo1 = cx.ps([128, 512], F32, "ssd_psyo")
        psyo = [psyo1, psyo1]
        pscs = cx.ps([128, 512], F32, "ssd_pscs")
        if self.phases is None or "conv" in self.phases:
            self.conv_setup(["act"], ["sp"], ["act"])
            self.cv_jobs = self.conv_jobs(li, "rest") + (self.conv_jobs(li + 1, "in") if li + 1 < self.layers else [])
        BTv = self.BT_d.rearrange("(g n) t -> n g t", g=2)
        CTv = self.CT_d.rearrange("(g n) t -> n g t", g=2)

        def bc8(ap8):
            return ap8.unsqueeze(2).to_broadcast([128, 8, 64])

        def v3(ap):
            return ap.rearrange("p (h f) -> p h f", h=8)

        def softplus(dst, src_ap, bias_ap, n, Rsrc):
            ta = t16.next()
            tb = t16.next()
            cx.op("dve", lambda e: e.tensor_tensor(ta.t[:, 0:n], src_ap, bias_ap, ALU.add), [Rsrc, bias16], [ta])
            cx.op("act", lambda e: e.activation(tb.t[:, 0:n], ta.t[:, 0:n], AF.Exp), [ta], [tb])
            cx.op("act", lambda e: e.activation(dst, tb.t[:, 0:n], AF.Ln, bias=1.0), [tb], [])

        for c in range(NT - 1, -1, -1):
            cx.op("act", lambda e, c=c: e.copy(prevB.t[:, c, :], state_b.t[:]), [state_b], [prevB])
            if c == 0:
                break
            if c % 2 == 0:
                self.conv_tick()
            tok = slice(c * 128, (c + 1) * 128)
            dtr = dtrs.next()
            cx.dma("sp", dtr.t[:], self.dt_d[tok, :], [], [dtr], ds=dl[0].next())
            xt = xts.next()
            cx.dma("sp", xt.t[:], self.xtok_d[tok, :], [], [xt], ds=dl[1].next())
            bt = bts.next()
            cx.dma("sp", bt.t[:], self.Btok_d[tok, :], [], [bt], ds=dl[2].next())
            dt = dts_.next()
            ta = t16.next()
            tb = t16.next()
            cx.op("dve", lambda e, ta=ta, dtr=dtr: e.tensor_tensor(ta.t[:, 0:8], dtr.t[:, 8:16], bias16.t[:, 8:16], ALU.add), [dtr, bias16], [ta])
            cx.op("act", lambda e, ta=ta, tb=tb: e.activation(tb.t[:, 0:8], ta.t[:, 0:8], AF.Exp), [ta], [tb])
            cx.op("act", lambda e, dt=dt, tb=tb: e.activation(dt.t[:, 0:8], tb.t[:, 0:8], AF.Ln, bias=1.0), [tb], [dt])
            ac = acs.next()
            cx.op("dve", lambda e, ac=ac, dt=dt: e.tensor_tensor(ac.t[:, 0:8], dt.t[:, 0:8], a16.t[:, 8:16], ALU.mult), [dt, a16], [ac])
            self.mm(psS[:, 0:8], self.Ustrict.t[:], ac.t[:, 0:8], True, True, [self.Ustrict, ac], [RpsS], False)
            self.mm(psS[:, 8:16], self.ones_f.t[:], ac.t[:, 0:8], True, True, [self.ones_f, ac], [RpsS], True)
            E = Es.next()
            cx.op("act", lambda e, E=E: e.activation(E.t[:, 0:16], psS[:, 0:16], AF.Exp), [RpsS], [E])
            wdt = wdts.next()
            cx.op("dve", lambda e, wdt=wdt, dt=dt, E=E: e.tensor_tensor(wdt.t[:], dt.t[:, 0:8], E.t[:, 0:8], ALU.mult), [dt, E], [wdt])
            xw = xwf.next()
            cx.op("dve", lambda e, xw=xw, xt=xt, wdt=wdt: e.tensor_tensor(v3(xw.t[:]), v3(xt.t[:]), bc8(wdt.t[:, :]), ALU.mult), [xt, wdt], [xw])
            for g in range(2):
                self.mm(pscs.t[:, g * 256:(g + 1) * 256], bt.t[:, g * 128:(g + 1) * 128], xw.t[:, g * 256:(g + 1) * 256], True, True, [bt, xw], [pscs], g == 1)
            cx.op("dve", lambda e, E=E: e.tensor_tensor(v3(state_b.t[:]), v3(state_b.t[:]), bc8(E.t[:, 8:16]), ALU.mult), [state_b, E], [state_b])
            cx.op("dve", lambda e: e.tensor_tensor(state_b.t[:], state_b.t[:], pscs.t[:], ALU.add), [state_b, pscs], [state_b])

        lhs_eng = Ring(["act", "dve"])
        sT_box = [None]

        def stageA0(c):
            tok = slice(c * 128, (c + 1) * 128)
            dtr = dtrs.next()
            cx.dma("sp", dtr.t[:], self.dt_d[tok, :], [], [dtr], ds=dl[0].next())
            xt = xts.next()
            cx.dma("sp", xt.t[:], self.xtok_d[tok, :], [], [xt], ds=dl[1].next())
            bt = bts.next()
            cx.dma("sp", bt.t[:], self.Btok_d[tok, :], [], [bt], ds=dl[2].next())
            BTc = BTs.next()
            cx.dma("sp", BTc.t[:], BTv[:, :, tok], [], [BTc], ds=dl[3].next())
            CTc = CTs.next()
            cx.dma("sp", CTc.t[:], CTv[:, :, tok], [], [CTc], ds=dl[4].next())
            zt = zts.next()
            cx.dma("sp", zt.t[:], self.z_d[tok, :], [], [zt], ds=dl[5].next())
            dt = dts_.next()
            ta = t16.next()
            tb = t16.next()
            cx.op("dve", lambda e, ta=ta, dtr=dtr: e.tensor_tensor(ta.t[:], dtr.t[:], bias16.t[:], ALU.add), [dtr, bias16], [ta])
            cx.op("act", lambda e, ta=ta, tb=tb: e.activation(tb.t[:], ta.t[:], AF.Exp), [ta], [tb])
            cx.op("act", lambda e, dt=dt, tb=tb: e.activation(dt.t[:], tb.t[:], AF.Ln, bias=1.0), [tb], [dt])
            ac = acs.next()
            cx.op("dve", lambda e, ac=ac, dt=dt: e.tensor_tensor(ac.t[:], dt.t[:], a16.t[:], ALU.mult), [dt, a16], [ac])
            self.mm(psS[:, 0:8], self.U_incl.t[:], ac.t[:, 0:8], True, True, [self.U_incl, ac], [RpsS], False)
            self.mm(psS[:, 8:16], self.L_incl.t[:], ac.t[:, 8:16], True, True, [self.L_incl, ac], [RpsS], False)
            self.mm(psS[:, 16:24], self.Lstrict.t[:], ac.t[:, 0:8], True, True, [self.Lstrict, ac], [RpsS], False)
            self.mm(psS[:, 24:32], self.ones_f.t[:], ac.t[:, 0:8], True, True, [self.ones_f, ac], [RpsS], True)
            E = Es.next()
            cx.op("act", lambda e, E=E: e.activation(E.t[:, 0:32], psS[:, 0:32], AF.Exp), [RpsS], [E])
            wdt = wdts.next()
            cx.op("dve", lambda e, wdt=wdt, dt=dt, E=E: e.tensor_tensor(wdt.t[:], dt.t[:, 0:8], E.t[:, 16:24], ALU.mult), [dt, E], [wdt])
            return dict(c=c, tok=tok, xt=xt, bt=bt, BTc=BTc, CTc=CTc, zt=zt, dt=dt, ac=ac, E=E, wdt=wdt)

        def stageA1(s0):
            c = s0["c"]; tok = s0["tok"]; xt = s0["xt"]; bt = s0["bt"]; BTc = s0["BTc"]; CTc = s0["CTc"]; zt = s0["zt"]
            dt = s0["dt"]; ac = s0["ac"]; E = s0["E"]; wdt = s0["wdt"]
            xf = xdtf.next()
            cx.op("dve", lambda e, xf=xf, xt=xt, dt=dt: e.tensor_tensor(v3(xf.t[:]), v3(xt.t[:]), bc8(dt.t[:, 0:8]), ALU.mult), [xt, dt], [xf])
            xb_ = xdtb.next()
            cx.op("dve", lambda e, xb_=xb_, xt=xt, dt=dt: e.tensor_tensor(v3(xb_.t[:]), v3(xt.t[:]), bc8(dt.t[:, 8:16]), ALU.mult), [xt, dt], [xb_])
            xw = xwf.next()
            cx.op("dve", lambda e, xw=xw, xt=xt, wdt=wdt: e.tensor_tensor(v3(xw.t[:]), v3(xt.t[:]), bc8(wdt.t[:, :]), ALU.mult), [xt, wdt], [xw])
            xD = xDs.next()
            cx.op("dve", lambda e, xD=xD, xt=xt: e.tensor_tensor(v3(xD.t[:]), v3(xt.t[:]), Dfull.t[:], ALU.mult), [xt, Dfull], [xD])
            for g in range(2):
                self.mm(psA.t[:, 128 + g * 128:256 + g * 128], BTc.t[:, g, :], CTc.t[:, g, :], True, True, [BTc, CTc], [RpsG], g == 1)
            gmf = Gmf.next()
            gmb = Gmb.next()
            pg = psA.t[:, 128:384].rearrange("p (g l) -> p g l", g=2)
            cx.op("dve", lambda e, gmf=gmf, pg=pg: e.tensor_tensor(gmf.t[:], pg, self.U_incl.t[:, :].unsqueeze(1).to_broadcast([128, 2, 128]), ALU.mult), [RpsG, self.U_incl], [gmf])
            cx.op("dve", lambda e, gmb=gmb, pg=pg: e.tensor_tensor(gmb.t[:], pg, self.L_incl.t[:, :].unsqueeze(1).to_broadcast([128, 2, 128]), ALU.mult), [RpsG, self.L_incl], [gmb])
            MT = [MTs[0].next(), MTs[1].next()]
            for dr in range(2):
                smask = self.Lstrict if dr == 0 else self.Ustrict
                cmask = self.U_incl if dr == 0 else self.L_incl
                gm = gmf if dr == 0 else gmb
                for g in range(2):
                    bank = diffs.next()
                    for hh in range(4):
                        j = dr * 8 + g * 4 + hh
                        lh = lhss.next()
                        en = lhs_eng.next()
                        if en == "act":
                            cx.op("act", lambda e, lh=lh, smask=smask, ac=ac, j=j: e.activation(lh.t[:], smask.t[:], AF.Copy, scale=ac.t[:, j:j + 1]), [smask, ac], [lh])
                        else:
                            cx.op("dve", lambda e, lh=lh, smask=smask, ac=ac, j=j: e.tensor_scalar(lh.t[:], smask.t[:], ac.t[:, j:j + 1], None, op0=ALU.mult), [smask, ac], [lh])
                        self.mm(bank.t[:, hh, :], lh.t[:], cmask.t[:], True, True, [lh, cmask], [bank], hh == 3)
                    ex = expds.next()
                    cx.op("act", lambda e, ex=ex, bank=bank: e.activation(ex.t[:], bank.t[:], AF.Exp), [bank], [ex])
                    cx.op("dve", lambda e, ex=ex, gm=gm, g=g, dr=dr: e.tensor_tensor(
                        MT[dr].t[:, g * 4:(g + 1) * 4, :], ex.t[:], gm.t[:, g:g + 1, :].to_broadcast([128, 4, 128]), ALU.mult), [ex, gm], [MT[dr]])
            return dict(c=c, tok=tok, xt=xt, bt=bt, CTc=CTc, zt=zt, E=E, xf=xf, xb_=xb_, xw=xw, xD=xD, MT=MT)

        def stageB(st):
            c = st["c"]; xt = st["xt"]; bt = st["bt"]; CTc = st["CTc"]; zt = st["zt"]; E = st["E"]
            xf = st["xf"]; xb_ = st["xb_"]; xw = st["xw"]; xD = st["xD"]; MT = st["MT"]
            self.mm(psy.t[:], self.ident_bf.t[:], xD.t[:], True, False, [self.ident_bf, xD], [psy], False)
            for h in range(8):
                hs = slice(h * 64, (h + 1) * 64)
                self.mm(psy.t[:, hs], MT[0].t[:, h, :], xf.t[:, hs], False, False, [MT[0], xf], [psy], False)
                self.mm(psy.t[:, hs], MT[1].t[:, h, :], xb_.t[:, hs], False, h == 7, [MT[1], xb_], [psy], h == 7)
            for g in range(2):
                gs = slice(g * 256, (g + 1) * 256)
                self.mm(psyo1.t[:, gs], CTc.t[:, g, :], stf_bf.t[:, gs], True, True, [CTc, stf_bf], [psyo1], g == 1)
            for g in range(2):
                self.mm(pscs.t[:, g * 256:(g + 1) * 256], bt.t[:, g * 128:(g + 1) * 128], xw.t[:, g * 256:(g + 1) * 256], True, True, [bt, xw], [pscs], g == 1)
            tmf = tmps.next()
            cx.op("dve", lambda e: e.tensor_tensor(v3(tmf.t[:]), v3(psyo1.t[:]), bc8(E.t[:, 0:8]), ALU.mult), [psyo1, E], [tmf])
            cx.op("dve", lambda e: e.tensor_tensor(v3(state_f.t[:]), v3(state_f.t[:]), bc8(E.t[:, 24:32]), ALU.mult), [state_f, E], [state_f])
            cx.op("dve", lambda e: e.tensor_tensor(state_f.t[:], state_f.t[:], pscs.t[:], ALU.add), [state_f, pscs], [state_f])
            cx.op("act", lambda e: e.copy(stf_bf.t[:], state_f.t[:]), [state_f], [stf_bf])
            y1 = y1s.next()
            cx.op("act", lambda e: e.copy(y1.t[:], psy.t[:]), [psy], [y1])
            cx.op("dve", lambda e: e.tensor_tensor(y1.t[:], y1.t[:], tmf.t[:], ALU.add), [y1, tmf], [y1])
            for g in range(2):
                gs = slice(g * 256, (g + 1) * 256)
                self.mm(psyo1.t[:, gs], CTc.t[:, g, :], prevB.t[:, c, gs], True, True, [CTc, prevB], [psyo1], g == 1)
            tmb = tmps.next()
            cx.op("dve", lambda e: e.tensor_tensor(v3(tmb.t[:]), v3(psyo1.t[:]), bc8(E.t[:, 8:16]), ALU.mult), [psyo1, E], [tmb])
            cx.op("dve", lambda e: e.tensor_tensor(y1.t[:], y1.t[:], tmb.t[:], ALU.add), [y1, tmb], [y1])
            return (c, y1, zt)

        def stageBt(sb):
            c, y1, zt = sb
            sz = szs.next()
            cx.op("act", lambda e: e.activation(sz.t[:], zt.t[:], AF.Silu), [zt], [sz])
            cx.op("dve", lambda e: e.tensor_tensor(y1.t[:], y1.t[:], sz.t[:], ALU.mult), [y1, sz], [y1])
            s2 = ss2.next()
            for g in range(2):
                cx.op("dve", lambda e, g=g: e.scalar_tensor_tensor(junk.t[:], y1.t[:, g * 256:(g + 1) * 256], 1.0, y1.t[:, g * 256:(g + 1) * 256],
                      op0=ALU.mult, op1=ALU.mult, accum_out=s2.t[:, g:g + 1]), [y1], [junk, s2])
            cx.op("act", lambda e: e.activation(s2.t[:, 2:4], s2.t[:, 0:2], AF.Ln, bias=self.eps1.t[:, 0:1], scale=1.0 / 256), [s2, self.eps1], [s2])
            cx.op("act", lambda e: e.activation(s2.t[:, 2:4], s2.t[:, 2:4], AF.Exp, scale=-0.5), [s2], [s2])
            yn = yns.next()
            cx.op("dve", lambda e: e.tensor_tensor(
                yn.t[:].rearrange("p (g f) -> p g f", g=2), y1.t[:].rearrange("p (g f) -> p g f", g=2),
                s2.t[:, 2:4].unsqueeze(2).to_broadcast([128, 2, 256]), ALU.mult), [y1, s2], [yn])
            return (c, yn)

        def stageC(stc):
            c, yn = stc
            tp = self.n_tp.next()
            for j in range(4):
                cx.op("pe", lambda e, j=j: e.transpose(tp.t[:, j, :], yn.t[:, j * 128:(j + 1) * 128], self.ident_bf.t[:]),
                      [yn, self.ident_bf], [tp], inc=(j == 3))
            if c % 4 == 0:
                sT_box[0] = sTs.next()
            sT = sT_box[0]
            q = c % 4
            cx.op("dve", lambda e: e.tensor_tensor(sT.t[:, :, q * 128:(q + 1) * 128], tp.t[:, 0:4, :],
                  gS.t[:, :].unsqueeze(2).to_broadcast([128, 4, 128]), ALU.mult), [tp, gS], [sT])
            if q == 3:
                cb4 = c // 4
                cx.dma("act", self.mixT[512:1024, cb4 * 512:(cb4 + 1) * 512].rearrange("(ch p) t -> p ch t", p=128), sT.t[:], [sT], [], ds=dsT.next())

        s0 = {}
        for k in range(min(3, NT)):
            s0[k] = stageA0(k)
        stA = stageA1(s0.pop(0))
        stC = None
        for c in range(NT):
            sb = stageB(stA)
            stA = stageA1(s0.pop(c + 1)) if c + 1 < NT else None
            if c + 3 < NT:
                s0[c + 3] = stageA0(c + 3)
            cur = stageBt(sb)
            if stC is not None:
                stageC(stC)
            stC = cur
            if c % 2 == 1:
                self.conv_tick()
        stageC(stC)
        self.conv_flush()
        cx.close_scope()

    def phase_wout(self, li, x_src):
        cx = self.cx
        cx.open_scope()
        self.norm_bufs(nx=3, with_norm=False)
        wo = cx.sb([128, 12, 1024], BF16, "wo")
        cx.dma("sp", wo.t[:], self.wout_b[li], [], [wo], ds=cx.dsem())
        mts = Ring([cx.sb([128, 12, 512], BF16, "wo_mt") for _ in range(2)])
        dmt = Ring([cx.dsem() for _ in range(2)])
        xos = Ring([cx.sb([128, D], F32, "wo_xo") for _ in range(2)])
        dxo = Ring([cx.dsem() for _ in range(2)])
        pss = Ring([cx.ps([128, 512], F32, "wo_ps") for _ in range(4)])
        mv = self.mixT.rearrange("(k p) t -> p k t", p=128)
        for tb in range(8):
            mt = mts.next()
            cx.dma("sp", mt.t[:], mv[:, :, tb * 512:(tb + 1) * 512], [], [mt], ds=dmt.next())
            for t4 in range(4):
                tt = tb * 4 + t4
                xt = self.n_xt.next()
                cx.dma("sp", xt.t[:], x_src[tt * 128:(tt + 1) * 128, :], [], [xt], ds=self.n_dx.next())
                xo = xos.next()
                for half in range(2):
                    ps = pss.next()
                    hs = slice(half * 512, (half + 1) * 512)
                    for kc in range(12):
                        self.mm(ps.t[:], mt.t[:, kc, t4 * 128:(t4 + 1) * 128], wo.t[:, kc, hs], kc == 0, kc == 11, [mt, wo], [ps], kc == 11)
                    cx.op("dve", lambda e, xo=xo, ps=ps, xt=xt, hs=hs: e.tensor_tensor(xo.t[:, hs], ps.t[:], xt.t[:, hs], ALU.add), [ps, xt], [xo])
                cx.dma("act", self.xa[tt * 128:(tt + 1) * 128, :], xo.t[:], [xo], [], ds=dxo.next())
        cx.close_scope()

    def phase_ffn_norm(self, li):
        cx = self.cx
        cx.open_scope()
        self.norm_bufs()
        g2 = self.load_gT(self.ffn_norm[li], 8, "gT_ffn")
        stgs = Ring([cx.sb([128, 8, 512], BF16, "fn_stg") for _ in range(2)])
        dst = Ring([cx.dsem() for _ in range(2)])
        hv = self.h2T_d.rearrange("(k p) t -> p k t", p=128)
        prev = None

        def fin(pv):
            tp, stg, t4, tb = pv
            self.norm_tile_b(tp, g2, stg.t[:, :, t4 * 128:(t4 + 1) * 128], stg)
            if t4 == 3:
                cx.dma("act", hv[:, :, tb * 512:(tb + 1) * 512], stg.t[:], [stg], [], ds=dst.next())

        for tb in range(8):
            stg = stgs.next()
            for t4 in range(4):
                tt = tb * 4 + t4
                xt = self.n_xt.next()
                cx.dma("sp", xt.t[:], self.xa[tt * 128:(tt + 1) * 128, :], [], [xt], ds=self.n_dx.next())
                tp = self.norm_tile_a(xt)
                if prev is not None:
                    fin(prev)
                prev = (tp, stg, t4, tb)
        fin(prev)
        cx.close_scope()

    def phase_ffn(self, li, last):
        cx = self.cx
        cx.open_scope()
        self.norm_bufs(nx=3, with_norm=False)
        self.n_junk = cx.sb([128, D], BF16, "n_junk")
        wd = cx.sb([128, 22, 1024], BF16, "wd")
        cx.dma("sp", wd.t[:], self.wdown_b[li], [], [wd], ds=cx.dsem())
        cw = self.load_cw(self.ffn_conv_w[li], 3, 44, "ffn_cw")
        cb = self.load_gT(self.ffn_conv_b[li], 44, "ffn_cb")
        if last:
            gfin = self.load_bcast(self.final_norm, D, "gfin")
            fss = Ring([cx.sb([128, 2], F32, "fin_ss") for _ in range(2)])
        hbs = Ring([cx.sb([128, 8, 1026], BF16, "ffn_hb") for _ in range(2)])
        dhb = Ring([cx.dsem() for _ in range(2)])
        aT = cx.sb([128, 22, 1024], BF16, "ffn_aT")
        wus = Ring([cx.sb([128, 2, 8, 128], BF16, "ffn_wu") for _ in range(3)])
        dwu = Ring([cx.dsem() for _ in range(3)])
        accg = Ring([cx.sb([128, 512], F32, "ffn_accg") for _ in range(2)])
        accu = Ring([cx.sb([128, 512], F32, "ffn_accu") for _ in range(2)])
        sgs = Ring([cx.sb([128, 512], F32, "ffn_sg") for _ in range(2)])
        xos = Ring([cx.sb([128, D], F32, "ffn_xo") for _ in range(2)])
        dxo = Ring([cx.dsem() for _ in range(2)])
        psu = Ring([cx.ps([128, 512], F32, "ffn_psu") for _ in range(4)])
        psd = Ring([cx.ps([128, 512], F32, "ffn_psd") for _ in range(2)])
        hv = self.h2T_d.rearrange("(k p) t -> p k t", p=128)
        subs = ((0, 342), (342, 342), (684, 340))
        for bk in range(4):
            hb = hbs.next()
            lo = max(0, bk * 1024 - 1)
            hi = min(S, bk * 1024 + 1025)
            o0 = lo - (bk * 1024 - 1)
            if bk == 0:
                cx.op("pool", lambda e, hb=hb: e.memset(hb.t[:, :, 0:1], 0.0), [], [hb])
            if bk == 3:
                cx.op("pool", lambda e, hb=hb: e.memset(hb.t[:, :, 1025:1026], 0.0), [], [hb])
            cx.dma("sp", hb.t[:, :, o0:o0 + hi - lo], hv[:, :, lo:hi], [], [hb], ds=dhb.next())
            for fc in range(22):
                wu = wus.next()
                dd = dwu.next()
                cx.dma("sp", wu.t[:, 0], self.wup_fm[li, fc], [], [wu], ds=dd)
                cx.dma("sp", wu.t[:, 1], self.wup_fm[li, 22 + fc], [], [wu], ds=dd)
                for (s0, w) in subs:
                    n = w + 2
                    accs = []
                    for which in range(2):
                        ps = psu.next()
                        for kc in range(8):
                            self.mm(ps.t[:, 0:n], wu.t[:, which, kc, :], hb.t[:, kc, s0:s0 + n], kc == 0, kc == 7, [wu, hb], [ps], kc == 7)
                        ch = fc + 22 * which
                        acc = (accg if which == 0 else accu).next()
                        cx.op("act", lambda e, acc=acc, ps=ps, w=w, ch=ch: e.activation(acc.t[:, 0:w], ps.t[:, 1:1 + w], AF.Identity,
                              bias=cb.t[:, ch:ch + 1], scale=cw.t[:, ch, 1:2]), [ps, cb, cw], [acc])
                        for k in (0, 2):
                            cx.op("dve", lambda e, acc=acc, ps=ps, w=w, ch=ch, k=k: e.scalar_tensor_tensor(
                                acc.t[:, 0:w], ps.t[:, k:k + w], cw.t[:, ch, k:k + 1], acc.t[:, 0:w], op0=ALU.mult, op1=ALU.add), [ps, cw, acc], [acc])
                        accs.append(acc)
                    sg = sgs.next()
                    cx.op("act", lambda e, sg=sg, a=accs[0], w=w: e.activation(sg.t[:, 0:w], a.t[:, 0:w], AF.Silu), [accs[0]], [sg])
                    cx.op("dve", lambda e, sg=sg, a=accs[1], w=w, fc=fc, s0=s0: e.tensor_tensor(aT.t[:, fc, s0:s0 + w], sg.t[:, 0:w], a.t[:, 0:w], ALU.mult), [sg, accs[1]], [aT])
            for t8 in range(8):
                tt = bk * 8 + t8
                xt = self.n_xt.next()
                cx.dma("sp", xt.t[:], self.xa[tt * 128:(tt + 1) * 128, :], [], [xt], ds=self.n_dx.next())
                xo = xos.next()
                for half in range(2):
                    ps = psd.next()
                    hs = slice(half * 512, (half + 1) * 512)
                    for kc in range(22):
                        self.mm(ps.t[:], aT.t[:, kc, t8 * 128:(t8 + 1) * 128], wd.t[:, kc, hs], kc == 0, kc == 21, [aT, wd], [ps], kc == 21)
                    cx.op("dve", lambda e, xo=xo, ps=ps, xt=xt, hs=hs: e.tensor_tensor(xo.t[:, hs], ps.t[:], xt.t[:, hs], ALU.add), [ps, xt], [xo])
                if not last:
                    cx.dma("act", self.xb[tt * 128:(tt + 1) * 128, :], xo.t[:], [xo], [], ds=dxo.next())
                else:
                    ss = fss.next()
                    self.rstd_of(xo.t[:], xo, ss, D)
                    cx.op("dve", lambda e, xo=xo, ss=ss: e.scalar_tensor_tensor(xo.t[:], xo.t[:], ss.t[:, 1:2], gfin.t[:], op0=ALU.mult, op1=ALU.mult), [xo, ss, gfin], [xo])
                    cx.dma("act", self.y[tt * 128:(tt + 1) * 128, :], xo.t[:], [xo], [], ds=dxo.next())
        cx.close_scope()

    def build(self):
        cx = self.cx
        self.consts()
        self.cv_jobs = []
        self.cv_pending = None
        self.early_conv = self.want("conv")
        if self.early_conv and self.phases is not None:
            cx.open_scope()
            self.conv_setup(["dve", "act"], ["sp"], ["act"])
            for j in self.conv_jobs(0, "in"):
                j()()
            if "ssd" not in self.phases:
                for j in self.conv_jobs(0, "rest"):
                    j()()
            cx.close_scope()
            self.early_conv = False
        cx.open_scope()
        self.norm_setup()
        for li in range(self.layers):
            x_src = self.x if li == 0 else self.xb
            cx.open_scope()
            hT = cx.sb([128, 8, S + 2 * PADH], BF16, "hT")
            cx.op("pool", lambda e: e.memset(hT.t[:, :, 0:PADH], 0.0), [], [hT])
            cx.op("pool", lambda e: e.memset(hT.t[:, :, PADH + S:], 0.0), [], [hT])
            gT = self.load_gT(self.mix_norm[li], 8, "gT_mix")
            self.g8h = []
            g8c = cx.sb([64, 8], F32, "g8c")
            stg = self.gstg.next()
            cx.dma("sp", stg.t[0:8, 0:64], self.attn_norm[li].rearrange("(c p) -> c p", p=64), [], [stg], ds=self.dqr.next())
            tp = self.n_tp.next()
            pv = tp.t[:].rearrange("p a b -> p (a b)").bitcast(F32)
            self.mm(pv[0:64, 0:8], stg.t[0:8, 0:64], self.ident_f.t[0:8, 0:8], True, True, [stg, self.ident_f], [tp], True)
            cx.op("dve", lambda e: e.tensor_scalar(g8c.t[:, :], pv[0:64, 0:8], 8.0, None, op0=ALU.mult), [tp], [g8c])
            self.g8c = g8c
            ovl = self.early_conv and li == 0
            if ovl:
                cx.open_scope()
                self.conv_setup(["act", "dve"], ["sp"], ["act"])
                self.cv_jobs = self.conv_jobs(0, "in")
            if self.want("norm"):
                self.phase_norm(x_src, gT, hT)
            if ovl:
                self.conv_flush()
                cx.close_scope()
            if "hT" in self.debug:
                dd = cx.dsem()
                cx.dma("sp", self.scr("hT", [128, 8, S + 2 * PADH], BF16), hT.t[:], [hT], [], ds=dd)
            if self.want("attn"):
                self.phase_attn(li, hT)
            if self.want("zdt"):
                self.phase_zdt(li, hT)
            if self.want("xbc"):
                self.phase_xbc(li, hT)
            if self.want("sc"):
                self.phase_sc(li, hT)
            cx.close_scope()
            if self.want("ssd"):
                self.phase_ssd(li)
            if self.want("wout"):
                self.phase_wout(li, x_src)
            if self.want("ffn"):
                self.phase_ffn_norm(li)
                self.phase_ffn(li, li == DEPTH - 1)
        cx.close_scope()
        cx.finish()
        self.lp.close()
        return self.nc


_CACHE = {}


def kernel(**inputs):
    if "nc" not in _CACHE:
        _CACHE["nc"] = Builder().build()
    nc = _CACHE["nc"]
    names = ["mix_norm", "w_in", "ssd_conv_w", "ssd_conv_b", "ssd_dt_bias", "ssd_a_log", "ssd_d", "ssd_norm",
             "sc_conv_w", "sc_conv_b", "attn_norm", "sc_norm", "w_out", "ffn_norm", "w_up", "ffn_conv_w",
             "ffn_conv_b", "w_down", "final_norm"]
    shared = {}
    for n in names:
        a = np.ascontiguousarray(np.asarray(inputs[n], dtype=np.float32))
        if n in ("ssd_dt_bias", "ssd_a_log"):
            a = a.reshape(DEPTH, 16)
        shared[n] = a
    x = np.asarray(inputs["x"], dtype=np.float32)
    in_maps = [dict(shared, x=np.ascontiguousarray(x[b])) for b in range(8)]
    res = run_bass_kernel_spmd(nc, in_maps, core_ids=list(range(8)))
    return np.stack([res.results[b]["y"] for b in range(8)], axis=0).astype(np.float32)
```

```python
import math
import numpy as np
from contextlib import ExitStack
import concourse.bass as bass
import concourse.mybir as mybir
from concourse.bass_utils import run_bass_kernel_spmd

F32 = mybir.dt.float32
BF16 = mybir.dt.bfloat16
I32 = mybir.dt.int32
AF = mybir.ActivationFunctionType
ALU = mybir.AluOpType

S = 4096
D = 1024
DEPTH = 2
NT = S // 128
D_IN = 4624
D_MIX = 1536
D_FF = 2816
EPS = 1e-6
PADH = 2
PADK = 1024
MASKV = -30000.0
DIL = (1, 4, 16)
C_Q, C_K, C_V, C_Z, C_XBC, C_DT, C_GB, C_GC, C_HC = 0, 512, 1024, 1536, 2048, 3072, 3088, 3600, 4112
FM_COLS = ([C_Q + 128 * i for i in range(4)] + [C_K + 128 * i for i in range(4)] + [C_V + 128 * i for i in range(4)]
           + [C_XBC + 128 * i for i in range(8)] + [C_GB + 128 * i for i in range(4)]
           + [C_GC + 128 * i for i in range(4)] + [C_HC + 128 * i for i in range(4)])
FM_Q, FM_K, FM_V, FM_XBC, FM_GB, FM_GC, FM_HC = 0, 4, 8, 12, 20, 24, 28


def sl(st, n, d):
    return slice(st, st + (n - 1) * d + 1, d)


class R:
    __slots__ = ("w", "rs", "name")

    def __init__(self, name=""):
        self.w = None
        self.rs = []
        self.name = name


class T:
    def __init__(self, t, name=""):
        self.t = t
        self.r = R(name)


class Ring:
    def __init__(self, items):
        self.items = items
        self.i = 0

    def next(self):
        it = self.items[self.i % len(self.items)]
        self.i += 1
        return it


class Eng:
    def __init__(self, cx, name, h, is_pe=False):
        self.name = name
        self.h = h
        self.sem = cx.new_sem("e_" + name)
        self.cnt = 0
        self.waited = {}
        self.is_pe = is_pe
        self.pR = []
        self.pW = []
        self.nins = 0


class DSem:
    def __init__(self, cx, name):
        self.sem = cx.new_sem(name)
        self.cnt = 0


class Cx:
    def __init__(self, nc):
        self.nc = nc
        self.stack = ExitStack()
        self.scopes = []
        self.uid = 0
        self.E = {}
        self.E["pe"] = Eng(self, "pe", nc.tensor, is_pe=True)
        self.E["dve"] = Eng(self, "dve", nc.vector)
        self.E["act"] = Eng(self, "act", nc.scalar)
        self.E["pool"] = Eng(self, "pool", nc.gpsimd)
        self.E["sp"] = Eng(self, "sp", nc.sync)
        self.marks = []
        self.dsems = []
        self.free_ds = []
        self.scope_ds = []
        self.ninst = 0

    def new_sem(self, name):
        return self.stack.enter_context(self.nc.semaphore(name))

    def dsem(self, name=None):
        if self.free_ds:
            d = self.free_ds.pop()
        else:
            self.uid += 1
            d = DSem(self, name or f"d{self.uid}")
            self.dsems.append(d)
        if self.scope_ds:
            self.scope_ds[-1].append(d)
        return d

    def _stk(self):
        return self.scopes[-1] if self.scopes else self.stack

    def sb(self, shape, dtype, name=None):
        self.uid += 1
        nm = (name or "sb") + f"_{self.uid}"
        return T(self._stk().enter_context(self.nc.sbuf_tensor(nm, list(shape), dtype)), nm)

    def ps(self, shape, dtype, name=None):
        self.uid += 1
        nm = (name or "ps") + f"_{self.uid}"
        return T(self._stk().enter_context(self.nc.psum_tensor(nm, list(shape), dtype)), nm)

    def open_scope(self):
        self.scopes.append(ExitStack())
        self.scope_ds.append([])

    def close_scope(self):
        self.barrier()
        self.scopes.pop().close()
        self.free_ds += self.scope_ds.pop()

    def _wait(self, eng, tok):
        if tok is None:
            return
        sem, val, owner = tok
        if owner is eng and eng.is_pe:
            return
        key = id(sem)
        if eng.waited.get(key, 0) >= val:
            return
        eng.h.wait_ge(sem, val)
        eng.waited[key] = val
        self.ninst += 1

    def _deps(self, eng, Rd, Wr):
        for r in Rd:
            self._wait(eng, r.w)
        for w in Wr:
            self._wait(eng, w.w)
            for t in w.rs:
                self._wait(eng, t)

    def _commit(self, tok, Rd, Wr):
        for r in Rd:
            r.rs.append(tok)
            if len(r.rs) > 48:
                best = {}
                for t in r.rs:
                    k = id(t[0])
                    if k not in best or best[k][1] < t[1]:
                        best[k] = t
                r.rs = list(best.values())
        for w in Wr:
            w.w = tok
            w.rs = []

    def op(self, en, fn, Rd=(), Wr=(), inc=True):
        eng = self.E[en]
        Rd = [x.r if isinstance(x, T) else x for x in Rd]
        Wr = [x.r if isinstance(x, T) else x for x in Wr]
        self._deps(eng, Rd, Wr)
        ins = fn(eng.h)
        self.ninst += 1
        eng.nins += 1
        if not inc:
            assert eng.is_pe
            eng.pR += Rd
            eng.pW += Wr
            return None
        eng.cnt += 1
        ins.then_inc(eng.sem, 1)
        tok = (eng.sem, eng.cnt, eng)
        self._commit(tok, list(Rd) + eng.pR, list(Wr) + eng.pW)
        eng.pR = []
        eng.pW = []
        return tok

    def dma(self, qn, out, in_, Rd=(), Wr=(), ds=None, **kw):
        eng = self.E[qn]
        Rd = [x.r if isinstance(x, T) else x for x in Rd]
        Wr = [x.r if isinstance(x, T) else x for x in Wr]
        self._deps(eng, Rd, Wr)
        ins = eng.h.dma_start(out=out, in_=in_, **kw)
        ds.cnt += 16
        ins.then_inc(ds.sem, 16)
        tok = (ds.sem, ds.cnt, ds)
        self._commit(tok, Rd, Wr)
        self.ninst += 1
        return tok

    def barrier(self):
        assert not self.E["pe"].pR and not self.E["pe"].pW
        toks = []
        for e in self.E.values():
            if e.cnt:
                toks.append((e.sem, e.cnt, e))
        for d in self.dsems:
            if d.cnt:
                toks.append((d.sem, d.cnt, d))
        for e in self.E.values():
            for t in toks:
                if t[2] is e:
                    continue
                self._wait(e, t)

    def finish(self):
        self.barrier()
        while self.scopes:
            self.scopes.pop().close()
        self.stack.close()


class Builder:
    def __init__(self, debug=(), layers=DEPTH, phases=None):
        self.debug = set(debug)
        self.layers = layers
        self.phases = phases
        nc = bass.Bass("TRN2", target_bir_lowering=False)
        self.nc = nc
        self.lp = ExitStack()
        self.lp.enter_context(nc.allow_low_precision("bf16 matmul operands, fp32 accumulation (reference tolerance)"))
        self.lp.enter_context(nc.allow_non_contiguous_dma("small gain/bias vector layouts"))
        ein = lambda n, s: nc.dram_tensor(n, list(s), F32, kind="ExternalInput").ap()
        self.x = ein("x", [S, D])
        self.mix_norm = ein("mix_norm", [DEPTH, D])
        self.w_in = ein("w_in", [DEPTH, D, D_IN])
        self.ssd_conv_w = ein("ssd_conv_w", [DEPTH, 5, 1024])
        self.ssd_conv_b = ein("ssd_conv_b", [DEPTH, 1024])
        self.ssd_dt_bias = ein("ssd_dt_bias", [DEPTH, 16])
        self.ssd_a_log = ein("ssd_a_log", [DEPTH, 16])
        self.ssd_d = ein("ssd_d", [DEPTH, 8])
        self.ssd_norm = ein("ssd_norm", [DEPTH, 512])
        self.sc_conv_w = ein("sc_conv_w", [DEPTH, 3, 512])
        self.sc_conv_b = ein("sc_conv_b", [DEPTH, 512])
        self.attn_norm = ein("attn_norm", [DEPTH, 512])
        self.sc_norm = ein("sc_norm", [DEPTH, 512])
        self.w_out = ein("w_out", [DEPTH, D_MIX, D])
        self.ffn_norm = ein("ffn_norm", [DEPTH, D])
        self.w_up = ein("w_up", [DEPTH, D, 2 * D_FF])
        self.ffn_conv_w = ein("ffn_conv_w", [DEPTH, 3, 2 * D_FF])
        self.ffn_conv_b = ein("ffn_conv_b", [DEPTH, 2 * D_FF])
        self.w_down = ein("w_down", [DEPTH, D_FF, D])
        self.final_norm = ein("final_norm", [D])
        self.y = nc.dram_tensor("y", [S, D], F32, kind="ExternalOutput").ap()

        def scr(name, shape, dt):
            kind = "ExternalOutput" if name in self.debug else "Internal"
            return nc.dram_tensor(name, list(shape), dt, kind=kind).ap()

        self.scr = scr
        self.win_fm = scr("win_fm", [DEPTH, 32, 128, 8, 128], BF16)
        self.wz_b = scr("wz_b", [DEPTH, 128, 8, 512], BF16)
        self.wdt_b = scr("wdt_b", [DEPTH, 128, 8, 16], BF16)
        self.wout_b = scr("wout_b", [DEPTH, 128, 12, 1024], BF16)
        self.wup_fm = scr("wup_fm", [DEPTH, 44, 128, 8, 128], BF16)
        self.wdown_b = scr("wdown_b", [DEPTH, 128, 22, 1024], BF16)
        self.xa = scr("xa", [S, D], F32)
        self.xb = scr("xb", [S, D], F32)
        self.mixT = scr("mixT", [D_MIX, S], BF16)
        self.z_d = scr("z_d", [S, 512], F32)
        self.dt_d = scr("dt_d", [S, 16], F32)
        self.BT_d = scr("BT_d", [256, S], BF16)
        self.CT_d = scr("CT_d", [256, S], BF16)
        self.xtok_d = scr("xtok_d", [S, 512], BF16)
        self.Btok_d = scr("Btok_d", [S, 256], BF16)
        self.h2T_d = scr("h2T_d", [D, S], BF16)
        self.cx = Cx(nc)

    def mm(self, out, lhsT, rhs, start, stop, Rd, Wr, inc):
        self.cx.op("pe", lambda e: e.matmul(out, lhsT, rhs, start=start, stop=stop), Rd, Wr, inc=inc)

    def want(self, ph):
        ok = self.phases is None or ph in self.phases
        if ok:
            self.cx.marks.append((ph, {k: e.nins for k, e in self.cx.E.items()}))
        return ok

    def consts(self):
        cx = self.cx
        nc = self.nc
        self.dqr = Ring([cx.dsem() for _ in range(24)])
        di = cx.sb([128, 128], I32, "di")
        dF = cx.sb([128, 128], F32, "dF")
        cx.op("pool", lambda e: e.iota(di.t[:], pattern=[[-1, 128]], base=0, channel_multiplier=1), [], [di])
        cx.op("dve", lambda e: e.tensor_copy(dF.t[:], di.t[:]), [di], [dF])

        def cmpmask(name, opc, dt=F32):
            m = cx.sb([128, 128], dt, name)
            cx.op("dve", lambda e: e.tensor_scalar(m.t[:], dF.t[:], 0.0, None, op0=opc), [dF], [m])
            return m

        self.U_incl = cmpmask("U_incl", ALU.is_le)
        self.L_incl = cmpmask("L_incl", ALU.is_ge)
        self.Lstrict = cmpmask("Lstrict", ALU.is_gt)
        self.Ustrict = cmpmask("Ustrict", ALU.is_lt)
        self.ident_bf = cmpmask("ident_bf", ALU.is_equal, BF16)
        self.ident_f = cmpmask("ident_f", ALU.is_equal, F32)
        self.ones_f = cx.sb([128, 128], F32, "ones_f")
        cx.op("dve", lambda e: e.memset(self.ones_f.t[:], 1.0), [], [self.ones_f])
        self.blk64 = cx.sb([128, 128], F32, "blk64")
        cx.op("dve", lambda e: e.memset(self.blk64.t[:], 0.0), [], [self.blk64])
        cx.op("dve", lambda e: e.memset(self.blk64.t[0:64, 0:64], 1.0), [], [self.blk64])
        cx.op("dve", lambda e: e.memset(self.blk64.t[64:128, 64:128], 1.0), [], [self.blk64])
        self.eps1 = cx.sb([128, 1], F32, "eps1")
        cx.op("dve", lambda e: e.memset(self.eps1.t[:], EPS), [], [self.eps1])
        self.eps64 = cx.sb([128, 1], F32, "eps64")
        cx.op("dve", lambda e: e.memset(self.eps64.t[:], 64.0 * EPS), [], [self.eps64])
        self.W65 = cx.sb([128, 64], BF16, "W65")
        cx.op("dve", lambda e: e.memset(self.W65.t[:], 1.0), [], [self.W65])
        cx.op("dve", lambda e: e.memset(self.W65.t[64:65, :], 64.0 * EPS), [], [self.W65])
        self.blk64b = cx.sb([128, 128], BF16, "blk64b")
        cx.op("dve", lambda e: e.tensor_copy(self.blk64b.t[:], self.blk64.t[:]), [self.blk64], [self.blk64b])
        absA = cx.sb([128, 128], F32, "absA")
        absB = cx.sb([128, 128], F32, "absB")
        mA = cx.sb([128, 128], F32, "mA")
        mB = cx.sb([128, 128], F32, "mB")
        for aX, sh in ((absA, -64.0), (absB, 64.0)):
            cx.op("dve", lambda e, sh=sh: e.tensor_scalar(mA.t[:], dF.t[:], sh, None, op0=ALU.add), [dF], [mA])
            cx.op("dve", lambda e, sh=sh: e.tensor_scalar(mB.t[:], dF.t[:], -1.0, -sh, op0=ALU.mult, op1=ALU.add), [dF], [mB])
            cx.op("dve", lambda e, aX=aX: e.tensor_tensor(aX.t[:], mA.t[:], mB.t[:], ALU.max), [mA, mB], [aX])
        cx.op("dve", lambda e: e.tensor_scalar(mA.t[:], absA.t[:], 64.0, MASKV, op0=ALU.is_gt, op1=ALU.mult), [absA], [mA])
        cx.op("dve", lambda e: e.tensor_scalar(mB.t[:], absB.t[:], 64.0, MASKV, op0=ALU.is_gt, op1=ALU.mult), [absB], [mB])
        self.bias = cx.sb([128, 48, 128], BF16, "attbias")
        for h in range(8):
            slope = 2.0 ** (-8.0 * (h + 1) / 8)
            for b in range(3):
                coef = -slope * DIL[b]
                for ab, (aX, mX) in enumerate(((absA, mA), (absB, mB))):
                    idx = (h * 3 + b) * 2 + ab
                    cx.op("dve", lambda e, idx=idx, aX=aX, mX=mX, coef=coef: e.scalar_tensor_tensor(
                        self.bias.t[:, idx, :], aX.t[:], coef, mX.t[:], op0=ALU.mult, op1=ALU.add),
                        [aX, mX], [self.bias])

    def conv_setup(self, engs, qs_in, qs_out):
        cx = self.cx
        self.cv_stg = Ring([cx.sb([128, 4096], F32, "wstg") for _ in range(2)])
        self.cv_obf = Ring([cx.sb([128, 4096], BF16, "wobf") for _ in range(2)])
        self.cv_din = Ring([cx.dsem() for _ in range(2)])
        self.cv_dout = Ring([cx.dsem() for _ in range(2)])
        self.cv_engs = Ring(engs)
        self.cv_qin = Ring(qs_in)
        self.cv_qout = Ring(qs_out)

    def _cv_load(self, src, kc, nb):
        cx = self.cx
        s = self.cv_stg.next()
        n = kc * nb
        cx.dma(self.cv_qin.next(), s.t[:, 0:n].rearrange("p (k n) -> p k n", k=kc), src.rearrange("(k p) n -> p k n", p=128),
               [], [s], ds=self.cv_din.next())

        def cast(perm):
            o = self.cv_obf.next()
            en = self.cv_engs.next()
            if perm:
                nchunk = nb // 128
                ov = o.t[:, 0:n].rearrange("p (c k n) -> p c k n", c=nchunk, k=kc)
                iv = s.t[:, 0:n].rearrange("p (k c n) -> p c k n", k=kc, c=nchunk)
                for c in range(nchunk):
                    if en == "act":
                        cx.op(en, lambda e, c=c: e.copy(ov[:, c], iv[:, c]), [s], [o])
                    else:
                        cx.op(en, lambda e, c=c: e.tensor_copy(ov[:, c], iv[:, c]), [s], [o])
            else:
                if en == "act":
                    cx.op(en, lambda e: e.copy(o.t[:, 0:n], s.t[:, 0:n]), [s], [o])
                else:
                    cx.op(en, lambda e: e.tensor_copy(o.t[:, 0:n], s.t[:, 0:n]), [s], [o])
            return o, n
        return cast

    def _cv_fm(self, src, kc, nchunk, dst):
        cast = self._cv_load(src, kc, nchunk * 128)

        def fin():
            o, n = cast(True)
            self.cx.dma(self.cv_qout.next(), dst.rearrange("c p k n -> p c k n"),
                        o.t[:, 0:n].rearrange("p (c k n) -> p c k n", c=nchunk, k=kc), [o], [], ds=self.cv_dout.next())
        return fin

    def _cv_r(self, src, kc, nb, dst):
        cast = self._cv_load(src, kc, nb)

        def fin():
            o, n = cast(False)
            self.cx.dma(self.cv_qout.next(), dst, o.t[:, 0:n].rearrange("p (k n) -> p k n", k=kc), [o], [], ds=self.cv_dout.next())
        return fin

    def conv_jobs(self, li, part):
        jobs = []
        if part == "in":
            w = self.w_in
            for seg in range(8):
                c0 = FM_COLS[seg * 4]
                jobs.append(lambda seg=seg, c0=c0: self._cv_fm(w[li, :, c0:c0 + 512], 8, 4, self.win_fm[li, seg * 4:seg * 4 + 4]))
            jobs.append(lambda: self._cv_r(w[li, :, C_Z:C_Z + 512], 8, 512, self.wz_b[li]))
            jobs.append(lambda: self._cv_r(w[li, :, C_DT:C_DT + 16], 8, 16, self.wdt_b[li]))
        else:
            for j in range(4):
                jobs.append(lambda j=j: self._cv_r(self.w_out[li, :, j * 256:(j + 1) * 256], 12, 256, self.wout_b[li, :, :, j * 256:(j + 1) * 256]))
            for j in range(11):
                jobs.append(lambda j=j: self._cv_fm(self.w_up[li, :, j * 512:(j + 1) * 512], 8, 4, self.wup_fm[li, j * 4:j * 4 + 4]))
            for j in range(8):
                jobs.append(lambda j=j: self._cv_r(self.w_down[li, :, j * 128:(j + 1) * 128], 22, 128, self.wdown_b[li, :, :, j * 128:(j + 1) * 128]))
        return jobs

    def conv_tick(self):
        nxt = self.cv_jobs.pop(0)() if self.cv_jobs else None
        if self.cv_pending is not None:
            self.cv_pending()
        self.cv_pending = nxt

    def conv_flush(self):
        while self.cv_jobs or self.cv_pending is not None:
            self.conv_tick()

    def norm_setup(self):
        cx = self.cx
        self.n_tp = Ring([cx.ps([128, 8, 128], BF16, "n_tp") for _ in range(1)])
        self.gstg = Ring([cx.sb([64, 128], F32, "gstg") for _ in range(2)])

    def norm_bufs(self, nx=3, with_norm=True):
        cx = self.cx
        self.n_xt = Ring([cx.sb([128, D], F32, "n_xt") for _ in range(nx)])
        self.n_dx = Ring([cx.dsem() for _ in range(nx + 1)])
        if with_norm:
            self.n_junk = cx.sb([128, D], BF16, "n_junk")
            self.n_ss = Ring([cx.sb([128, 2], F32, "n_ss") for _ in range(3)])
            self.n_xn = Ring([cx.sb([128, D], BF16, "n_xn") for _ in range(2)])
            self.n_tp2 = Ring([self.n_tp.items[0], cx.ps([128, 8, 128], BF16, "n_tp2")])

    def _row_T(self, src_row, nchunk, dst_ap, dst_T, mult=1.0):
        cx = self.cx
        stg = self.gstg.next()
        cx.dma("sp", stg.t[0:nchunk, :], src_row.rearrange("(c p) -> c p", p=128), [], [stg], ds=self.dqr.next())
        tp = self.n_tp.next()
        pv = tp.t[:].rearrange("p a b -> p (a b)").bitcast(F32)
        self.mm(pv[:, 0:nchunk], stg.t[0:nchunk, :], self.ident_f.t[0:nchunk, 0:nchunk], True, True, [stg, self.ident_f], [tp], True)
        cx.op("dve", lambda e: e.tensor_scalar(dst_ap, pv[:, 0:nchunk], mult, None, op0=ALU.mult), [tp], [dst_T])

    def load_gT(self, src_row, nchunk, name, mult=1.0):
        g = self.cx.sb([128, nchunk], F32, name)
        self._row_T(src_row, nchunk, g.t[:, :], g, mult)
        return g

    def rstd_of(self, x_ap, xT, ss, n):
        cx = self.cx
        cx.op("dve", lambda e: e.scalar_tensor_tensor(self.n_junk.t[:, 0:n], x_ap, 1.0, x_ap, op0=ALU.mult, op1=ALU.mult,
                                                      accum_out=ss.t[:, 0:1]), [xT], [self.n_junk, ss])
        cx.op("act", lambda e: e.activation(ss.t[:, 1:2], ss.t[:, 0:1], AF.Ln, bias=self.eps1.t[:, 0:1], scale=1.0 / n), [ss, self.eps1], [ss])
        cx.op("act", lambda e: e.activation(ss.t[:, 1:2], ss.t[:, 1:2], AF.Exp, scale=-0.5), [ss], [ss])

    def norm_tile_a(self, xt):
        cx = self.cx
        ss = self.n_ss.next()
        xn = self.n_xn.next()
        tp = self.n_tp2.next()
        self.rstd_of(xt.t[:], xt, ss, D)
        cx.op("act", lambda e: e.activation(xn.t[:], xt.t[:], AF.Copy, scale=ss.t[:, 1:2]), [xt, ss], [xn])
        for j in range(8):
            cx.op("pe", lambda e, j=j: e.transpose(tp.t[:, j, :], xn.t[:, j * 128:(j + 1) * 128], self.ident_bf.t[:]),
                  [xn, self.ident_bf], [tp], inc=(j == 7))
        return tp

    def norm_tile_b(self, tp, gT, out_ap, out_R):
        self.cx.op("dve", lambda e: e.tensor_tensor(out_ap, tp.t[:], gT.t[:, :].unsqueeze(2).to_broadcast([128, 8, 128]), ALU.mult),
                   [tp, gT], [out_R])

    def phase_norm(self, x_src, gT, hT):
        cx = self.cx
        cx.open_scope()
        self.norm_bufs()
        prev = None
        for tt in range(NT):
            xt = self.n_xt.next()
            cx.dma("sp", xt.t[:], x_src[tt * 128:(tt + 1) * 128, :], [], [xt], ds=self.n_dx.next())
            tp = self.norm_tile_a(xt)
            if prev is not None:
                self.norm_tile_b(prev[0], gT, hT.t[:, :, PADH + prev[1] * 128:PADH + (prev[1] + 1) * 128], hT)
            prev = (tp, tt)
            if tt % 3 == 0:
                self.conv_tick()
        self.norm_tile_b(prev[0], gT, hT.t[:, :, PADH + prev[1] * 128:PADH + (prev[1] + 1) * 128], hT)
        cx.close_scope()

    def phase_attn(self, li, hT):
        cx = self.cx
        cx.open_scope()
        qT = cx.sb([128, S], BF16, "qT")
        kTs = [cx.sb([128, S + 2 * PADK], BF16, "kT0"), cx.sb([128, S + 2 * PADK], BF16, "kT1")]
        vT = cx.sb([128, S + 2 * PADK], BF16, "vT")
        NV = 117
        V = cx.sb([128, NV, 2, 65], BF16, "Vaug")
        acc = cx.sb([65, 1, S], F32, "acc")
        wq = cx.sb([128, 8, 128], BF16, "wq")
        wk = cx.sb([128, 8, 128], BF16, "wk")
        wv = cx.sb([128, 8, 128], BF16, "wv")
        dw = [cx.dsem() for _ in range(3)]
        pT = Ring([cx.sb([128, 8, 128], BF16, "pT") for _ in range(2)])
        nrm_a = Ring([cx.sb([64, 512], F32, "nrm_a") for _ in range(2)])
        nrm_b = Ring([cx.sb([64, 512], F32, "nrm_b") for _ in range(2)])
        nrm_c = Ring([cx.sb([65, 512], BF16, "nrm_c") for _ in range(2)])
        nrm_o = Ring([cx.sb([64, 512], BF16, "nrm_o") for _ in range(2)])
        d_o = Ring([cx.dsem() for _ in range(2)])
        banks7 = [cx.ps([128, 512], F32, "att_ps") for _ in range(7)]
        ps_proj = Ring(banks7)
        ps_s = Ring(banks7[0:4])
        ps_o = Ring(banks7[4:7])
        cx.op("pool", lambda e: e.memset(kTs[0].t[64:128, :], 0.0), [], [kTs[0]])
        cx.op("pool", lambda e: e.memset(kTs[1].t[0:64, :], 0.0), [], [kTs[1]])
        cx.op("pool", lambda e: e.memset(kTs[0].t[0:64, 0:PADK], 0.0), [], [kTs[0]])
        cx.op("pool", lambda e: e.memset(kTs[0].t[0:64, PADK + S:], 0.0), [], [kTs[0]])
        cx.op("pool", lambda e: e.memset(kTs[1].t[64:128, 0:PADK], 0.0), [], [kTs[1]])
        cx.op("pool", lambda e: e.memset(kTs[1].t[64:128, PADK + S:], 0.0), [], [kTs[1]])
        cx.op("pool", lambda e: e.memset(vT.t[:, 0:PADK], 0.0), [], [vT])
        cx.op("pool", lambda e: e.memset(vT.t[:, PADK + S:], 0.0), [], [vT])
        cx.op("pool", lambda e: e.memset(V.t[:, :, :, 64:65], 1.0), [], [V])
        voff = []
        o = 0
        for b in range(3):
            voff.append(o)
            o += DIL[b] * (S // DIL[b] // 128 + 1)
        assert o == NV
        for b in range(3):
            d = DIL[b]
            ntq = S // d // 128
            for c in range(d):
                i0 = voff[b] + c * (ntq + 1)
                cx.op("pool", lambda e, i0=i0: e.memset(V.t[0:64, i0, :, 64:65], 0.0), [], [V])
                cx.op("pool", lambda e, i1=i0 + ntq: e.memset(V.t[64:128, i1, :, 64:65], 0.0), [], [V])

        def vps(ps):
            return ps.t[:].rearrange("p (a b) -> p a b", a=4)

        evac = Ring(["act", "dve"])
        for hp in range(4):
            for wt, fm0, ds in ((wq, FM_Q, dw[0]), (wk, FM_K, dw[1]), (wv, FM_V, dw[2])):
                cx.dma("sp", wt.t[:], self.win_fm[li, fm0 + hp], [], [wt], ds=ds)
            for which, wt in enumerate((wq, wk, wv)):
                for tb in range(8):
                    ps = ps_proj.next()
                    for kc in range(8):
                        self.mm(ps.t[:], wt.t[:, kc, :], hT.t[:, kc, PADH + tb * 512:PADH + (tb + 1) * 512],
                                kc == 0, kc == 7, [wt, hT], [ps], kc == 7)
                    en = evac.next()
                    if which == 0:
                        outs = [(qT, qT.t[:, tb * 512:(tb + 1) * 512], ps.t[:], 0.125)]
                    elif which == 1:
                        cs_ = slice(PADK + tb * 512, PADK + (tb + 1) * 512)
                        outs = [(kTs[0], kTs[0].t[0:64, cs_], ps.t[0:64, :], 1.0), (kTs[1], kTs[1].t[64:128, cs_], ps.t[64:128, :], 1.0)]
                    else:
                        outs = [(vT, vT.t[:, PADK + tb * 512:PADK + (tb + 1) * 512], ps.t[:], 1.0)]
                    for (dstT, dap, sap, scale) in outs:
                        if en == "act":
                            cx.op("act", lambda e, dap=dap, sap=sap, scale=scale: e.activation(dap, sap, AF.Copy, scale=scale), [ps], [dstT])
                        else:
                            cx.op("dve", lambda e, dap=dap, sap=sap, scale=scale: e.tensor_scalar(dap, sap, scale, None, op0=ALU.mult), [ps], [dstT])
            for b in range(3):
                d = DIL[b]
                ntq = S // d // 128
                for c in range(d):
                    m = 0
                    while m < ntq + 1:
                        g = min(4, ntq + 1 - m)
                        ps = ps_proj.next()
                        pv = ps.t[:].bitcast(BF16)
                        for j in range(g):
                            st = PADK + d * (128 * (m + j) - 64) + c
                            cx.op("pe", lambda e, j=j, st=st, d=d, pv=pv: e.transpose(
                                pv[:, j * 128:(j + 1) * 128], vT.t[:, sl(st, 128, d)], self.ident_bf.t[:]),
                                [vT, self.ident_bf], [ps], inc=(j == g - 1))
                        i0 = voff[b] + c * (ntq + 1) + m
                        en = evac.next()
                        src = pv[:, 0:g * 128].rearrange("p (g h f) -> p g h f", g=g, h=2)
                        dstap = V.t[:, i0:i0 + g, :, 0:64]
                        if en == "act":
                            cx.op("act", lambda e, src=src, dstap=dstap: e.copy(dstap, src), [ps], [V])
                        else:
                            cx.op("dve", lambda e, src=src, dstap=dstap: e.tensor_copy(dstap, src), [ps], [V])
                        m += g
            for hh in range(2):
                h = hp * 2 + hh
                kT = kTs[hh]
                groups = []
                for b in range(3):
                    d = DIL[b]
                    ntq = S // d // 128
                    G = min(4, ntq)
                    for c in range(d):
                        for j0 in range(0, ntq, G):
                            groups.append((b, d, ntq, G, c, j0))

                def emit_S(grp):
                    b, d, ntq, G, c, j0 = grp
                    banks = [ps_s.next() for _ in range((2 * G + 3) // 4)]
                    for jl in range(G):
                        j = j0 + jl
                        for ab in range(2):
                            slot = jl * 2 + ab
                            bank = banks[slot // 4]
                            ks = 128 * j - 64 + 128 * ab
                            kc0 = PADK + d * ks + c
                            qc0 = d * 128 * j + c
                            oap = vps(bank)[:, slot % 4, :]
                            self.mm(oap, kT.t[:, sl(kc0, 128, d)], qT.t[:, sl(qc0, 128, d)],
                                    slot % 4 == 0, False, [kT, qT], [bank], False)
                            if slot % 4 == 3:
                                i0 = (h * 3 + b) * 2
                                self.mm(bank.t[:], self.ident_bf.t[:],
                                        self.bias.t[:, i0:i0 + 2, :].unsqueeze(1).to_broadcast([128, 2, 2, 128]),
                                        False, True, [self.ident_bf, self.bias], [bank], True)
                    return banks

                def emit_rest(grp, banks):
                    b, d, ntq, G, c, j0 = grp
                    pt = pT.next()
                    for bi, bank in enumerate(banks):
                        ns = min(4, 2 * G - bi * 4)
                        cx.op("act", lambda e, bank=bank, bi=bi, ns=ns, pt=pt: e.activation(
                            pt.t[:, bi * 4:bi * 4 + ns, :], vps(bank)[:, 0:ns, :], AF.Exp), [bank], [pt])
                    po = ps_o.next()
                    for jl in range(G):
                        j = j0 + jl
                        for ab in range(2):
                            vi = voff[b] + c * (ntq + 1) + j + ab
                            self.mm(po.t[0:65, jl * 128:(jl + 1) * 128], V.t[:, vi, hh, :], pt.t[:, jl * 2 + ab, :],
                                    ab == 0, ab == 1, [V, pt], [po], (jl == G - 1 and ab == 1))
                    t0 = d * 128 * j0 + c
                    aap = acc.t[0:65, 0, sl(t0, G * 128, d)]
                    if b == 0:
                        cx.op("dve", lambda e, aap=aap, po=po, G=G: e.tensor_copy(aap, po.t[0:65, 0:G * 128]), [po], [acc])
                    else:
                        cx.op("dve", lambda e, aap=aap, po=po, G=G: e.tensor_tensor(aap, po.t[0:65, 0:G * 128], aap, ALU.add),
                              [po, acc], [acc])

                prev = None
                for grp in groups:
                    bk = emit_S(grp)
                    if prev is not None:
                        emit_rest(*prev)
                    prev = (grp, bk)
                emit_rest(*prev)
                def n1(tb):
                    cs = slice(tb * 512, (tb + 1) * 512)
                    rc = nrm_c.next()
                    cx.op("dve", lambda e: e.tensor_tensor(rc.t[:], acc.t[0:65, 0, cs], acc.t[0:65, 0, cs], ALU.mult), [acc], [rc])
                    ps2 = ps_proj.next()
                    self.mm(ps2.t[0:64, :], self.W65.t[0:65, :], rc.t[:], True, True, [self.W65, rc], [ps2], True)
                    return (cs, ps2)

                def n2(st, hh=hh, hp=hp):
                    cs, ps2 = st
                    ra = nrm_a.next()
                    cx.op("act", lambda e: e.activation(ra.t[:], ps2.t[0:64, :], AF.Ln), [ps2], [ra])
                    ra2 = nrm_b.next()
                    cx.op("act", lambda e: e.activation(ra2.t[:], ra.t[:], AF.Exp, scale=-0.5), [ra], [ra2])
                    ro = nrm_o.next()
                    cx.op("dve", lambda e: e.scalar_tensor_tensor(
                        ro.t[:], acc.t[0:64, 0, cs], self.g8c.t[:, hp * 2 + hh:hp * 2 + hh + 1], ra2.t[:], op0=ALU.mult, op1=ALU.mult), [acc, ra2, self.g8c], [ro])
                    row0 = (hp * 2 + hh) * 64
                    cx.dma("act", self.mixT[row0:row0 + 64, cs], ro.t[:], [ro], [], ds=d_o.next())

                pv_ = None
                for tb in range(8):
                    cur_ = n1(tb)
                    if pv_ is not None:
                        n2(pv_)
                    pv_ = cur_
                n2(pv_)
        cx.close_scope()

    def load_bcast(self, src_row, n, name):
        cx = self.cx
        t = cx.sb([128, n], F32, name)
        cx.dma("sp", t.t[:, :], src_row.unsqueeze(0).partition_broadcast(128)[:, 0, :], [], [t], ds=self.dqr.next())
        return t

    def load_cw(self, src, k, nchunk, name):
        t = self.cx.sb([128, nchunk, k], F32, name)
        for kk in range(k):
            self._row_T(src[kk, :], nchunk, t.t[:, :, kk], t)
        return t

    def phase_zdt(self, li, hT):
        cx = self.cx
        cx.open_scope()
        wz = cx.sb([128, 8, 512], BF16, "wz")
        wdt = cx.sb([128, 8, 16], BF16, "wdt")
        cx.dma("sp", wz.t[:], self.wz_b[li], [], [wz], ds=cx.dsem())
        cx.dma("sp", wdt.t[:], self.wdt_b[li], [], [wdt], ds=cx.dsem())
        zs = Ring([cx.sb([128, 512], F32, "zs") for _ in range(2)])
        dts = Ring([cx.sb([128, 16], F32, "dts") for _ in range(2)])
        dz = Ring([cx.dsem() for _ in range(2)])
        dd = Ring([cx.dsem() for _ in range(2)])
        psz = Ring([cx.ps([128, 512], F32, "psz") for _ in range(2)])
        psd = Ring([cx.ps([128, 512], F32, "psd") for _ in range(2)])
        for tt in range(NT):
            pz = psz.next()
            pd = psd.next()
            tok = slice(PADH + tt * 128, PADH + (tt + 1) * 128)
            for kc in range(8):
                self.mm(pz.t[:], hT.t[:, kc, tok], wz.t[:, kc, :], kc == 0, kc == 7, [hT, wz], [pz], kc == 7)
            for kc in range(8):
                self.mm(pd.t[:, 0:16], hT.t[:, kc, tok], wdt.t[:, kc, :], kc == 0, kc == 7, [hT, wdt], [pd], kc == 7)
            z = zs.next()
            cx.op("act", lambda e, z=z, pz=pz: e.copy(z.t[:], pz.t[:]), [pz], [z])
            cx.dma("act", self.z_d[tt * 128:(tt + 1) * 128, :], z.t[:], [z], [], ds=dz.next())
            dt = dts.next()
            cx.op("dve", lambda e, dt=dt, pd=pd: e.tensor_copy(dt.t[:], pd.t[:, 0:16]), [pd], [dt])
            cx.dma("act", self.dt_d[tt * 128:(tt + 1) * 128, :], dt.t[:], [dt], [], ds=dd.next())
        cx.close_scope()

    def phase_xbc(self, li, hT):
        cx = self.cx
        cx.open_scope()
        cw = self.load_cw(self.ssd_conv_w[li], 5, 8, "xbc_cw")
        cb = self.load_gT(self.ssd_conv_b[li], 8, "xbc_cb")
        wts = Ring([cx.sb([128, 8, 128], BF16, "xbc_w") for _ in range(2)])
        dws = Ring([cx.dsem() for _ in range(2)])
        rows = Ring([cx.sb([128, S], BF16, "xbc_row") for _ in range(2)])
        drow = Ring([cx.dsem() for _ in range(2)])
        accs = Ring([cx.sb([128, 512], F32, "xbc_acc") for _ in range(2)])
        toks = Ring([cx.sb([128, 32, 128], BF16, "xbc_tok") for _ in range(2)])
        dtok = Ring([cx.dsem() for _ in range(2)])
        pss = Ring([cx.ps([128, 512], F32, "xbc_ps") for _ in range(3)])
        pst = Ring([cx.ps([128, 4, 128], BF16, "xbc_pst") for _ in range(2)])
        W = 508
        for fc in range(8):
            wt = wts.next()
            cx.dma("sp", wt.t[:], self.win_fm[li, FM_XBC + fc], [], [wt], ds=dws.next())
            row = rows.next()
            for t0 in range(0, S, W):
                w = min(W, S - t0)
                n = w + 4
                ps = pss.next()
                for kc in range(8):
                    self.mm(ps.t[:, 0:n], wt.t[:, kc, :], hT.t[:, kc, PADH + t0 - 2:PADH + t0 - 2 + n], kc == 0, kc == 7, [wt, hT], [ps], kc == 7)
                acc = accs.next()
                cx.op("act", lambda e, acc=acc, ps=ps, w=w, fc=fc: e.activation(acc.t[:, 0:w], ps.t[:, 2:2 + w], AF.Identity,
                      bias=cb.t[:, fc:fc + 1], scale=cw.t[:, fc, 2:3]), [ps, cb, cw], [acc])
                for k in (0, 1, 3, 4):
                    cx.op("dve", lambda e, acc=acc, ps=ps, w=w, fc=fc, k=k: e.scalar_tensor_tensor(
                        acc.t[:, 0:w], ps.t[:, k:k + w], cw.t[:, fc, k:k + 1], acc.t[:, 0:w], op0=ALU.mult, op1=ALU.add), [ps, cw, acc], [acc])
                cx.op("act", lambda e, acc=acc, row=row, t0=t0, w=w: e.activation(row.t[:, t0:t0 + w], acc.t[:, 0:w], AF.Silu), [acc], [row])
            if fc >= 4:
                dst = self.BT_d if fc < 6 else self.CT_d
                r0 = ((fc - 4) % 2) * 128
                cx.dma("act", dst[r0:r0 + 128, :], row.t[:], [row], [], ds=drow.next())
            if fc < 6:
                tokb = toks.next()
                for tq in range(8):
                    pt = pst.next()
                    for j in range(4):
                        tt = tq * 4 + j
                        cx.op("pe", lambda e, pt=pt, j=j, tt=tt, row=row: e.transpose(pt.t[:, j, :], row.t[:, tt * 128:(tt + 1) * 128], self.ident_bf.t[:]),
                              [row, self.ident_bf], [pt], inc=(j == 3))
                    cx.op("act", lambda e, pt=pt, tokb=tokb, tq=tq: e.copy(tokb.t[:, tq * 4:tq * 4 + 4, :], pt.t[:]), [pt], [tokb])
                if fc < 4:
                    dst = self.xtok_d[:, fc * 128:(fc + 1) * 128]
                else:
                    dst = self.Btok_d[:, (fc - 4) * 128:(fc - 3) * 128]
                dv = dst.rearrange("(t p) f -> p t f", p=128)
                dk = dtok.next()
                for q4 in range(4):
                    cx.dma("act", dv[:, q4 * 8:(q4 + 1) * 8, :], tokb.t[:, q4 * 8:(q4 + 1) * 8, :], [tokb], [], ds=dk)
        cx.close_scope()

    def phase_sc(self, li, hT):
        cx = self.cx
        cx.open_scope()
        cw = self.load_cw(self.sc_conv_w[li], 3, 4, "sc_cw")
        cb = self.load_gT(self.sc_conv_b[li], 4, "sc_cb")
        g8 = self.load_gT(self.sc_norm[li], 4, "sc_g8", mult=8.0)
        wts = [Ring([cx.sb([128, 8, 128], BF16, "sc_w") for _ in range(2)]) for _ in range(3)]
        dws = [Ring([cx.dsem() for _ in range(2)]) for _ in range(3)]
        pss = [Ring([cx.ps([128, 512], F32, "sc_ps") for _ in range(2)]) for _ in range(3)]
        psn = cx.ps([128, 512], F32, "sc_psn")
        gcs = Ring([cx.sb([128, 512], F32, "sc_gcs") for _ in range(2)])
        tts = Ring([cx.sb([128, 512], F32, "sc_tt") for _ in range(2)])
        accs = Ring([cx.sb([128, 512], F32, "sc_acc") for _ in range(2)])
        yvs = Ring([cx.sb([128, 512], F32, "sc_yv") for _ in range(2)])
        ysq = Ring([cx.sb([128, 512], BF16, "sc_ysq") for _ in range(2)])
        rrs = Ring([cx.sb([128, 512], F32, "sc_rr") for _ in range(2)])
        outs = Ring([cx.sb([128, 512], BF16, "sc_out") for _ in range(2)])
        douts = Ring([cx.dsem() for _ in range(2)])
        W = 510
        for c4 in range(4):
            ws = []
            for i, fm0 in enumerate((FM_GB, FM_GC, FM_HC)):
                wt = wts[i].next()
                cx.dma("sp", wt.t[:], self.win_fm[li, fm0 + c4], [], [wt], ds=dws[i].next())
                ws.append(wt)
            def sc_proj(t0, ws=ws):
                w = min(W, S - t0)
                n = w + 2
                pp = []
                for i in range(3):
                    ps = pss[i].next()
                    for kc in range(8):
                        self.mm(ps.t[:, 0:n], ws[i].t[:, kc, :], hT.t[:, kc, PADH + t0 - 1:PADH + t0 - 1 + n], kc == 0, kc == 7, [ws[i], hT], [ps], kc == 7)
                    pp.append(ps)
                return (t0, w, n, pp)

            def sc_rest(st, c4=c4):
                t0, w, n, pp = st
                pgb, pgc, phc = pp
                gc = gcs.next()
                cx.op("act", lambda e, gc=gc, pgc=pgc, n=n: e.copy(gc.t[:, 0:n], pgc.t[:, 0:n]), [pgc], [gc])
                tt = tts.next()
                cx.op("dve", lambda e, tt=tt, phc=phc, gc=gc, n=n: e.tensor_tensor(tt.t[:, 0:n], phc.t[:, 0:n], gc.t[:, 0:n], ALU.mult), [phc, gc], [tt])
                acc = accs.next()
                cx.op("act", lambda e, acc=acc, tt=tt, w=w, c4=c4: e.activation(acc.t[:, 0:w], tt.t[:, 1:1 + w], AF.Identity,
                      bias=cb.t[:, c4:c4 + 1], scale=cw.t[:, c4, 1:2]), [tt, cb, cw], [acc])
                for k in (0, 2):
                    cx.op("dve", lambda e, acc=acc, tt=tt, w=w, c4=c4, k=k: e.scalar_tensor_tensor(
                        acc.t[:, 0:w], tt.t[:, k:k + w], cw.t[:, c4, k:k + 1], acc.t[:, 0:w], op0=ALU.mult, op1=ALU.add), [tt, cw, acc], [acc])
                yv = yvs.next()
                cx.op("dve", lambda e, yv=yv, pgb=pgb, acc=acc, w=w: e.tensor_tensor(yv.t[:, 0:w], pgb.t[:, 1:1 + w], acc.t[:, 0:w], ALU.mult), [pgb, acc], [yv])
                yq = ysq.next()
                cx.op("dve", lambda e, yq=yq, yv=yv, w=w: e.tensor_tensor(yq.t[:, 0:w], yv.t[:, 0:w], yv.t[:, 0:w], ALU.mult), [yv], [yq])
                self.mm(psn.t[:, 0:w], self.blk64b.t[:], yq.t[:, 0:w], True, True, [self.blk64b, yq], [psn], True)
                rr = rrs.next()
                cx.op("act", lambda e, rr=rr, w=w: e.activation(rr.t[:, 0:w], psn.t[:, 0:w], AF.Ln, bias=self.eps64.t[:, 0:1]), [psn, self.eps64], [rr])
                cx.op("act", lambda e, rr=rr, w=w: e.activation(rr.t[:, 0:w], rr.t[:, 0:w], AF.Exp, scale=-0.5), [rr], [rr])
                ob = outs.next()
                cx.op("dve", lambda e, ob=ob, yv=yv, rr=rr, w=w, c4=c4: e.scalar_tensor_tensor(
                    ob.t[:, 0:w], yv.t[:, 0:w], g8.t[:, c4:c4 + 1], rr.t[:, 0:w], op0=ALU.mult, op1=ALU.mult), [yv, g8, rr], [ob])
                cx.dma("act", self.mixT[1024 + c4 * 128:1024 + (c4 + 1) * 128, t0:t0 + w], ob.t[:, 0:w], [ob], [], ds=douts.next())

            prev = None
            for t0 in range(0, S, W):
                cur = sc_proj(t0)
                if prev is not None:
                    sc_rest(prev)
                prev = cur
            sc_rest(prev)
        cx.close_scope()

    def phase_ssd(self, li):
        cx = self.cx
        cx.open_scope()
        bias16 = self.load_bcast(self.ssd_dt_bias[li], 16, "ssd_bias16")
        a16 = self.load_bcast(self.ssd_a_log[li], 16, "ssd_a16")
        cx.op("act", lambda e: e.activation(a16.t[:], a16.t[:], AF.Exp), [a16], [a16])
        cx.op("dve", lambda e: e.tensor_scalar(a16.t[:], a16.t[:], -1.0, None, op0=ALU.mult), [a16], [a16])
        d8 = self.load_bcast(self.ssd_d[li], 8, "ssd_d8")
        Dfull = cx.sb([128, 8, 64], F32, "ssd_Dfull")
        cx.op("dve", lambda e: e.tensor_copy(Dfull.t[:], d8.t[:, :].unsqueeze(2).to_broadcast([128, 8, 64])), [d8], [Dfull])
        gS = self.load_gT(self.ssd_norm[li], 4, "ssd_gS")
        prevB = cx.sb([128, NT, 512], BF16, "ssd_prevB")
        state_f = cx.sb([128, 512], F32, "ssd_state_f")
        state_b = cx.sb([128, 512], F32, "ssd_state_b")
        stf_bf = cx.sb([128, 512], BF16, "ssd_stf_bf")
        cx.op("pool", lambda e: e.memset(state_f.t[:], 0.0), [], [state_f])
        cx.op("pool", lambda e: e.memset(state_b.t[:], 0.0), [], [state_b])
        cx.op("pool", lambda e: e.memset(stf_bf.t[:], 0.0), [], [stf_bf])
        dtrs = Ring([cx.sb([128, 16], F32, "ssd_dtr") for _ in range(4)])
        xts = Ring([cx.sb([128, 512], BF16, "ssd_xt") for _ in range(4)])
        bts = Ring([cx.sb([128, 256], BF16, "ssd_bt") for _ in range(4)])
        BTs = Ring([cx.sb([128, 2, 128], BF16, "ssd_BT") for _ in range(4)])
        CTs = Ring([cx.sb([128, 2, 128], BF16, "ssd_CT") for _ in range(4)])
        zts = Ring([cx.sb([128, 512], F32, "ssd_zt") for _ in range(4)])
        dl = [Ring([cx.dsem() for _ in range(5)]) for _ in range(6)]
        t16 = Ring([cx.sb([128, 16], F32, "ssd_t16") for _ in range(8)])
        dts_ = Ring([cx.sb([128, 16], F32, "ssd_dt") for _ in range(4)])
        acs = Ring([cx.sb([128, 16], F32, "ssd_ac") for _ in range(4)])
        Es = Ring([cx.sb([128, 32], F32, "ssd_E") for _ in range(4)])
        wdts = Ring([cx.sb([128, 8], F32, "ssd_wdt") for _ in range(4)])
        xdtf = Ring([cx.sb([128, 512], BF16, "ssd_xdtf") for _ in range(2)])
        xdtb = Ring([cx.sb([128, 512], BF16, "ssd_xdtb") for _ in range(2)])
        xwf = Ring([cx.sb([128, 512], BF16, "ssd_xwf") for _ in range(2)])
        xDs = Ring([cx.sb([128, 512], BF16, "ssd_xD") for _ in range(2)])
        Gmf = Ring([cx.sb([128, 2, 128], F32, "ssd_Gmf") for _ in range(2)])
        Gmb = Ring([cx.sb([128, 2, 128], F32, "ssd_Gmb") for _ in range(2)])
        lhss = Ring([cx.sb([128, 128], F32, "ssd_lhs") for _ in range(32)])
        expds = Ring([cx.sb([128, 4, 128], F32, "ssd_expd") for _ in range(2)])
        MTs = [Ring([cx.sb([128, 8, 128], BF16, "ssd_MT") for _ in range(2)]) for _ in range(2)]
        y1s = Ring([cx.sb([128, 512], F32, "ssd_y1") for _ in range(2)])
        tmps = Ring([cx.sb([128, 512], F32, "ssd_tmp") for _ in range(2)])
        szs = Ring([cx.sb([128, 512], F32, "ssd_sz") for _ in range(2)])
        ss2 = Ring([cx.sb([128, 4], F32, "ssd_ss2") for _ in range(2)])
        yns = Ring([cx.sb([128, 512], BF16, "ssd_yn") for _ in range(2)])
        sTs = Ring([cx.sb([128, 4, 512], BF16, "ssd_sT") for _ in range(2)])
        dsT = Ring([cx.dsem() for _ in range(2)])
        junk = cx.sb([128, 256], F32, "ssd_junk")
        psA = cx.ps([128, 512], F32, "ssd_psA")
        RpsG = psA.r
        psS_T = cx.ps([128, 512], F32, "ssd_psS")
        RpsS = psS_T.r
        psS = psS_T.t
        diffs = Ring([cx.ps([128, 4, 128], F32, "ssd_diff") for _ in range(2)])
        psy = cx.ps([128, 512], F32, "ssd_psy")
        psyo1 = cx.ps([128, 512], F32, "ssd_psyo")
        psyo = [psyo1, psyo1]
        pscs = cx.ps([128, 512], F32, "ssd_pscs")
        if self.phases is None or "conv" in self.phases:
            self.conv_setup(["act"], ["sp"], ["act"])
            self.cv_jobs = self.conv_jobs(li, "rest") + (self.conv_jobs(li + 1, "in") if li + 1 < self.layers else [])
        BTv = self.BT_d.rearrange("(g n) t -> n g t", g=2)
        CTv = self.CT_d.rearrange("(g n) t -> n g t", g=2)

        def bc8(ap8):
            return ap8.unsqueeze(2).to_broadcast([128, 8, 64])

        def v3(ap):
            return ap.rearrange("p (h f) -> p h f", h=8)

        def softplus(dst, src_ap, bias_ap, n, Rsrc):
            ta = t16.next()
            tb = t16.next()
            cx.op("dve", lambda e: e.tensor_tensor(ta.t[:, 0:n], src_ap, bias_ap, ALU.add), [Rsrc, bias16], [ta])
            cx.op("act", lambda e: e.activation(tb.t[:, 0:n], ta.t[:, 0:n], AF.Exp), [ta], [tb])
            cx.op("act", lambda e: e.activation(dst, tb.t[:, 0:n], AF.Ln, bias=1.0), [tb], [])

        for c in range(NT - 1, -1, -1):
            cx.op("act", lambda e, c=c: e.copy(prevB.t[:, c, :], state_b.t[:]), [state_b], [prevB])
            if c == 0:
                break
            if c % 2 == 0:
                self.conv_tick()
            tok = slice(c * 128, (c + 1) * 128)
            dtr = dtrs.next()
            cx.dma("sp", dtr.t[:], self.dt_d[tok, :], [], [dtr], ds=dl[0].next())
            xt = xts.next()
            cx.dma("sp", xt.t[:], self.xtok_d[tok, :], [], [xt], ds=dl[1].next())
            bt = bts.next()
            cx.dma("sp", bt.t[:], self.Btok_d[tok, :], [], [bt], ds=dl[2].next())
            dt = dts_.next()
            ta = t16.next()
            tb = t16.next()
            cx.op("dve", lambda e, ta=ta, dtr=dtr: e.tensor_tensor(ta.t[:, 0:8], dtr.t[:, 8:16], bias16.t[:, 8:16], ALU.add), [dtr, bias16], [ta])
            cx.op("act", lambda e, ta=ta, tb=tb: e.activation(tb.t[:, 0:8], ta.t[:, 0:8], AF.Exp), [ta], [tb])
            cx.op("act", lambda e, dt=dt, tb=tb: e.activation(dt.t[:, 0:8], tb.t[:, 0:8], AF.Ln, bias=1.0), [tb], [dt])
            ac = acs.next()
            cx.op("dve", lambda e, ac=ac, dt=dt: e.tensor_tensor(ac.t[:, 0:8], dt.t[:, 0:8], a16.t[:, 8:16], ALU.mult), [dt, a16], [ac])
            self.mm(psS[:, 0:8], self.Ustrict.t[:], ac.t[:, 0:8], True, True, [self.Ustrict, ac], [RpsS], False)
            self.mm(psS[:, 8:16], self.ones_f.t[:], ac.t[:, 0:8], True, True, [self.ones_f, ac], [RpsS], True)
            E = Es.next()
            cx.op("act", lambda e, E=E: e.activation(E.t[:, 0:16], psS[:, 0:16], AF.Exp), [RpsS], [E])
            wdt = wdts.next()
            cx.op("dve", lambda e, wdt=wdt, dt=dt, E=E: e.tensor_tensor(wdt.t[:], dt.t[:, 0:8], E.t[:, 0:8], ALU.mult), [dt, E], [wdt])
            xw = xwf.next()
            cx.op("dve", lambda e, xw=xw, xt=xt, wdt=wdt: e.tensor_tensor(v3(xw.t[:]), v3(xt.t[:]), bc8(wdt.t[:, :]), ALU.mult), [xt, wdt], [xw])
            for g in range(2):
                self.mm(pscs.t[:, g * 256:(g + 1) * 256], bt.t[:, g * 128:(g + 1) * 128], xw.t[:, g * 256:(g + 1) * 256], True, True, [bt, xw], [pscs], g == 1)
            cx.op("dve", lambda e, E=E: e.tensor_tensor(v3(state_b.t[:]), v3(state_b.t[:]), bc8(E.t[:, 8:16]), ALU.mult), [state_b, E], [state_b])
            cx.op("dve", lambda e: e.tensor_tensor(state_b.t[:], state_b.t[:], pscs.t[:], ALU.add), [state_b, pscs], [state_b])

        lhs_eng = Ring(["act", "dve"])
        sT_box = [None]

        def stageA0(c):
            tok = slice(c * 128, (c + 1) * 128)
            dtr = dtrs.next()
            cx.dma("sp", dtr.t[:], self.dt_d[tok, :], [], [dtr], ds=dl[0].next())
            xt = xts.next()
            cx.dma("sp", xt.t[:], self.xtok_d[tok, :], [], [xt], ds=dl[1].next())
            bt = bts.next()
            cx.dma("sp", bt.t[:], self.Btok_d[tok, :], [], [bt], ds=dl[2].next())
            BTc = BTs.next()
            cx.dma("sp", BTc.t[:], BTv[:, :, tok], [], [BTc], ds=dl[3].next())
            CTc = CTs.next()
            cx.dma("sp", CTc.t[:], CTv[:, :, tok], [], [CTc], ds=dl[4].next())
            zt = zts.next()
            cx.dma("sp", zt.t[:], self.z_d[tok, :], [], [zt], ds=dl[5].next())
            dt = dts_.next()
            ta = t16.next()
            tb = t16.next()
            cx.op("dve", lambda e, ta=ta, dtr=dtr: e.tensor_tensor(ta.t[:], dtr.t[:], bias16.t[:], ALU.add), [dtr, bias16], [ta])
            cx.op("act", lambda e, ta=ta, tb=tb: e.activation(tb.t[:], ta.t[:], AF.Exp), [ta], [tb])
            cx.op("act", lambda e, dt=dt, tb=tb: e.activation(dt.t[:], tb.t[:], AF.Ln, bias=1.0), [tb], [dt])
            ac = acs.next()
            cx.op("dve", lambda e, ac=ac, dt=dt: e.tensor_tensor(ac.t[:], dt.t[:], a16.t[:], ALU.mult), [dt, a16], [ac])
            self.mm(psS[:, 0:8], self.U_incl.t[:], ac.t[:, 0:8], True, True, [self.U_incl, ac], [RpsS], False)
            self.mm(psS[:, 8:16], self.L_incl.t[:], ac.t[:, 8:16], True, True, [self.L_incl, ac], [RpsS], False)
            self.mm(psS[:, 16:24], self.Lstrict.t[:], ac.t[:, 0:8], True, True, [self.Lstrict, ac], [RpsS], False)
            self.mm(psS[:, 24:32], self.ones_f.t[:], ac.t[:, 0:8], True, True, [self.ones_f, ac], [RpsS], True)
            E = Es.next()
            cx.op("act", lambda e, E=E: e.activation(E.t[:, 0:32], psS[:, 0:32], AF.Exp), [RpsS], [E])
            wdt = wdts.next()
            cx.op("dve", lambda e, wdt=wdt, dt=dt, E=E: e.tensor_tensor(wdt.t[:], dt.t[:, 0:8], E.t[:, 16:24], ALU.mult), [dt, E], [wdt])
            return dict(c=c, tok=tok, xt=xt, bt=bt, BTc=BTc, CTc=CTc, zt=zt, dt=dt, ac=ac, E=E, wdt=wdt)

        def stageBuild(s0):
            ac = s0["ac"]
            lst = []
            for j in range(16):
                smask = self.Lstrict if j < 8 else self.Ustrict
                lh = lhss.next()
                en = lhs_eng.next()
                if en == "act":
                    cx.op("act", lambda e, lh=lh, smask=smask, j=j: e.activation(lh.t[:], smask.t[:], AF.Copy, scale=ac.t[:, j:j + 1]), [smask, ac], [lh])
                else:
                    cx.op("dve", lambda e, lh=lh, smask=smask, j=j: e.tensor_scalar(lh.t[:], smask.t[:], ac.t[:, j:j + 1], None, op0=ALU.mult), [smask, ac], [lh])
                lst.append(lh)
            s0["lhs"] = lst

        def stageA1(s0):
            c = s0["c"]; tok = s0["tok"]; xt = s0["xt"]; bt = s0["bt"]; BTc = s0["BTc"]; CTc = s0["CTc"]; zt = s0["zt"]
            dt = s0["dt"]; ac = s0["ac"]; E = s0["E"]; wdt = s0["wdt"]
            xf = xdtf.next()
            cx.op("dve", lambda e, xf=xf, xt=xt, dt=dt: e.tensor_tensor(v3(xf.t[:]), v3(xt.t[:]), bc8(dt.t[:, 0:8]), ALU.mult), [xt, dt], [xf])
            xb_ = xdtb.next()
            cx.op("dve", lambda e, xb_=xb_, xt=xt, dt=dt: e.tensor_tensor(v3(xb_.t[:]), v3(xt.t[:]), bc8(dt.t[:, 8:16]), ALU.mult), [xt, dt], [xb_])
            xw = xwf.next()
            cx.op("dve", lambda e, xw=xw, xt=xt, wdt=wdt: e.tensor_tensor(v3(xw.t[:]), v3(xt.t[:]), bc8(wdt.t[:, :]), ALU.mult), [xt, wdt], [xw])
            xD = xDs.next()
            cx.op("dve", lambda e, xD=xD, xt=xt: e.tensor_tensor(v3(xD.t[:]), v3(xt.t[:]), Dfull.t[:], ALU.mult), [xt, Dfull], [xD])
            for g in range(2):
                self.mm(psA.t[:, 128 + g * 128:256 + g * 128], BTc.t[:, g, :], CTc.t[:, g, :], True, True, [BTc, CTc], [RpsG], g == 1)
            gmf = Gmf.next()
            gmb = Gmb.next()
            pg = psA.t[:, 128:384].rearrange("p (g l) -> p g l", g=2)
            cx.op("dve", lambda e, gmf=gmf, pg=pg: e.tensor_tensor(gmf.t[:], pg, self.U_incl.t[:, :].unsqueeze(1).to_broadcast([128, 2, 128]), ALU.mult), [RpsG, self.U_incl], [gmf])
            cx.op("dve", lambda e, gmb=gmb, pg=pg: e.tensor_tensor(gmb.t[:], pg, self.L_incl.t[:, :].unsqueeze(1).to_broadcast([128, 2, 128]), ALU.mult), [RpsG, self.L_incl], [gmb])
            MT = [MTs[0].next(), MTs[1].next()]
            for dr in range(2):
                smask = self.Lstrict if dr == 0 else self.Ustrict
                cmask = self.U_incl if dr == 0 else self.L_incl
                gm = gmf if dr == 0 else gmb
                for g in range(2):
                    bank = diffs.next()
                    for hh in range(4):
                        j = dr * 8 + g * 4 + hh
                        lh = s0["lhs"][j]
                        self.mm(bank.t[:, hh, :], lh.t[:], cmask.t[:], True, True, [lh, cmask], [bank], hh == 3)
                    ex = expds.next()
                    cx.op("act", lambda e, ex=ex, bank=bank: e.activation(ex.t[:], bank.t[:], AF.Exp), [bank], [ex])
                    cx.op("dve", lambda e, ex=ex, gm=gm, g=g, dr=dr: e.tensor_tensor(
                        MT[dr].t[:, g * 4:(g + 1) * 4, :], ex.t[:], gm.t[:, g:g + 1, :].to_broadcast([128, 4, 128]), ALU.mult), [ex, gm], [MT[dr]])
            return dict(c=c, tok=tok, xt=xt, bt=bt, CTc=CTc, zt=zt, E=E, xf=xf, xb_=xb_, xw=xw, xD=xD, MT=MT)

        def stageB(st):
            c = st["c"]; xt = st["xt"]; bt = st["bt"]; CTc = st["CTc"]; zt = st["zt"]; E = st["E"]
            xf = st["xf"]; xb_ = st["xb_"]; xw = st["xw"]; xD = st["xD"]; MT = st["MT"]
            self.mm(psy.t[:], self.ident_bf.t[:], xD.t[:], True, False, [self.ident_bf, xD], [psy], False)
            for h in range(8):
                hs = slice(h * 64, (h + 1) * 64)
                self.mm(psy.t[:, hs], MT[0].t[:, h, :], xf.t[:, hs], False, False, [MT[0], xf], [psy], False)
                self.mm(psy.t[:, hs], MT[1].t[:, h, :], xb_.t[:, hs], False, h == 7, [MT[1], xb_], [psy], h == 7)
            for g in range(2):
                gs = slice(g * 256, (g + 1) * 256)
                self.mm(psyo1.t[:, gs], CTc.t[:, g, :], stf_bf.t[:, gs], True, True, [CTc, stf_bf], [psyo1], g == 1)
            for g in range(2):
                self.mm(pscs.t[:, g * 256:(g + 1) * 256], bt.t[:, g * 128:(g + 1) * 128], xw.t[:, g * 256:(g + 1) * 256], True, True, [bt, xw], [pscs], g == 1)
            tmf = tmps.next()
            cx.op("dve", lambda e: e.tensor_tensor(v3(tmf.t[:]), v3(psyo1.t[:]), bc8(E.t[:, 0:8]), ALU.mult), [psyo1, E], [tmf])
            cx.op("dve", lambda e: e.tensor_tensor(v3(state_f.t[:]), v3(state_f.t[:]), bc8(E.t[:, 24:32]), ALU.mult), [state_f, E], [state_f])
            cx.op("dve", lambda e: e.tensor_tensor(state_f.t[:], state_f.t[:], pscs.t[:], ALU.add), [state_f, pscs], [state_f])
            cx.op("act", lambda e: e.copy(stf_bf.t[:], state_f.t[:]), [state_f], [stf_bf])
            y1 = y1s.next()
            cx.op("act", lambda e: e.copy(y1.t[:], psy.t[:]), [psy], [y1])
            cx.op("dve", lambda e: e.tensor_tensor(y1.t[:], y1.t[:], tmf.t[:], ALU.add), [y1, tmf], [y1])
            for g in range(2):
                gs = slice(g * 256, (g + 1) * 256)
                self.mm(psyo1.t[:, gs], CTc.t[:, g, :], prevB.t[:, c, gs], True, True, [CTc, prevB], [psyo1], g == 1)
            tmb = tmps.next()
            cx.op("dve", lambda e: e.tensor_tensor(v3(tmb.t[:]), v3(psyo1.t[:]), bc8(E.t[:, 8:16]), ALU.mult), [psyo1, E], [tmb])
            cx.op("dve", lambda e: e.tensor_tensor(y1.t[:], y1.t[:], tmb.t[:], ALU.add), [y1, tmb], [y1])
            return (c, y1, zt)

        def stageBt(sb):
            c, y1, zt = sb
            sz = szs.next()
            cx.op("act", lambda e: e.activation(sz.t[:], zt.t[:], AF.Silu), [zt], [sz])
            cx.op("dve", lambda e: e.tensor_tensor(y1.t[:], y1.t[:], sz.t[:], ALU.mult), [y1, sz], [y1])
            s2 = ss2.next()
            for g in range(2):
                cx.op("dve", lambda e, g=g: e.scalar_tensor_tensor(junk.t[:], y1.t[:, g * 256:(g + 1) * 256], 1.0, y1.t[:, g * 256:(g + 1) * 256],
                      op0=ALU.mult, op1=ALU.mult, accum_out=s2.t[:, g:g + 1]), [y1], [junk, s2])
            cx.op("act", lambda e: e.activation(s2.t[:, 2:4], s2.t[:, 0:2], AF.Ln, bias=self.eps1.t[:, 0:1], scale=1.0 / 256), [s2, self.eps1], [s2])
            cx.op("act", lambda e: e.activation(s2.t[:, 2:4], s2.t[:, 2:4], AF.Exp, scale=-0.5), [s2], [s2])
            yn = yns.next()
            cx.op("dve", lambda e: e.tensor_tensor(
                yn.t[:].rearrange("p (g f) -> p g f", g=2), y1.t[:].rearrange("p (g f) -> p g f", g=2),
                s2.t[:, 2:4].unsqueeze(2).to_broadcast([128, 2, 256]), ALU.mult), [y1, s2], [yn])
            return (c, yn)

        def stageC(stc):
            c, yn = stc
            tp = self.n_tp.next()
            for j in range(4):
                cx.op("pe", lambda e, j=j: e.transpose(tp.t[:, j, :], yn.t[:, j * 128:(j + 1) * 128], self.ident_bf.t[:]),
                      [yn, self.ident_bf], [tp], inc=(j == 3))
            if c % 4 == 0:
                sT_box[0] = sTs.next()
            sT = sT_box[0]
            q = c % 4
            cx.op("dve", lambda e: e.tensor_tensor(sT.t[:, :, q * 128:(q + 1) * 128], tp.t[:, 0:4, :],
                  gS.t[:, :].unsqueeze(2).to_broadcast([128, 4, 128]), ALU.mult), [tp, gS], [sT])
            if q == 3:
                cb4 = c // 4
                cx.dma("act", self.mixT[512:1024, cb4 * 512:(cb4 + 1) * 512].rearrange("(ch p) t -> p ch t", p=128), sT.t[:], [sT], [], ds=dsT.next())

        s0 = {}
        for k in range(min(3, NT)):
            s0[k] = stageA0(k)
        stageBuild(s0[0])
        if NT > 1:
            stageBuild(s0[1])
        stA = stageA1(s0.pop(0))
        stC = None
        for c in range(NT):
            sb = stageB(stA)
            stA = stageA1(s0.pop(c + 1)) if c + 1 < NT else None
            if c + 3 < NT:
                s0[c + 3] = stageA0(c + 3)
            if c + 2 < NT:
                stageBuild(s0[c + 2])
            cur = stageBt(sb)
            if stC is not None:
                stageC(stC)
            stC = cur
            if c % 2 == 1:
                self.conv_tick()
        stageC(stC)
        self.conv_flush()
        cx.close_scope()

    def phase_wout(self, li, x_src):
        cx = self.cx
        cx.open_scope()
        self.norm_bufs(nx=3, with_norm=False)
        wo = cx.sb([128, 12, 1024], BF16, "wo")
        cx.dma("sp", wo.t[:], self.wout_b[li], [], [wo], ds=cx.dsem())
        mts = Ring([cx.sb([128, 12, 512], BF16, "wo_mt") for _ in range(2)])
        dmt = Ring([cx.dsem() for _ in range(2)])
        xos = Ring([cx.sb([128, D], F32, "wo_xo") for _ in range(2)])
        dxo = Ring([cx.dsem() for _ in range(2)])
        pss = Ring([cx.ps([128, 512], F32, "wo_ps") for _ in range(4)])
        mv = self.mixT.rearrange("(k p) t -> p k t", p=128)
        for tb in range(8):
            mt = mts.next()
            cx.dma("sp", mt.t[:], mv[:, :, tb * 512:(tb + 1) * 512], [], [mt], ds=dmt.next())
            for t4 in range(4):
                tt = tb * 4 + t4
                xt = self.n_xt.next()
                cx.dma("sp", xt.t[:], x_src[tt * 128:(tt + 1) * 128, :], [], [xt], ds=self.n_dx.next())
                xo = xos.next()
                for half in range(2):
                    ps = pss.next()
                    hs = slice(half * 512, (half + 1) * 512)
                    for kc in range(12):
                        self.mm(ps.t[:], mt.t[:, kc, t4 * 128:(t4 + 1) * 128], wo.t[:, kc, hs], kc == 0, kc == 11, [mt, wo], [ps], kc == 11)
                    cx.op("dve", lambda e, xo=xo, ps=ps, xt=xt, hs=hs: e.tensor_tensor(xo.t[:, hs], ps.t[:], xt.t[:, hs], ALU.add), [ps, xt], [xo])
                cx.dma("act", self.xa[tt * 128:(tt + 1) * 128, :], xo.t[:], [xo], [], ds=dxo.next())
        cx.close_scope()

    def phase_ffn_norm(self, li):
        cx = self.cx
        cx.open_scope()
        self.norm_bufs()
        g2 = self.load_gT(self.ffn_norm[li], 8, "gT_ffn")
        stgs = Ring([cx.sb([128, 8, 512], BF16, "fn_stg") for _ in range(2)])
        dst = Ring([cx.dsem() for _ in range(2)])
        hv = self.h2T_d.rearrange("(k p) t -> p k t", p=128)
        prev = None

        def fin(pv):
            tp, stg, t4, tb = pv
            self.norm_tile_b(tp, g2, stg.t[:, :, t4 * 128:(t4 + 1) * 128], stg)
            if t4 == 3:
                cx.dma("act", hv[:, :, tb * 512:(tb + 1) * 512], stg.t[:], [stg], [], ds=dst.next())

        for tb in range(8):
            stg = stgs.next()
            for t4 in range(4):
                tt = tb * 4 + t4
                xt = self.n_xt.next()
                cx.dma("sp", xt.t[:], self.xa[tt * 128:(tt + 1) * 128, :], [], [xt], ds=self.n_dx.next())
                tp = self.norm_tile_a(xt)
                if prev is not None:
                    fin(prev)
                prev = (tp, stg, t4, tb)
        fin(prev)
        cx.close_scope()

    def phase_ffn(self, li, last):
        cx = self.cx
        cx.open_scope()
        self.norm_bufs(nx=3, with_norm=False)
        self.n_junk = cx.sb([128, D], BF16, "n_junk")
        wd = cx.sb([128, 22, 1024], BF16, "wd")
        cx.dma("sp", wd.t[:], self.wdown_b[li], [], [wd], ds=cx.dsem())
        cw = self.load_cw(self.ffn_conv_w[li], 3, 44, "ffn_cw")
        cb = self.load_gT(self.ffn_conv_b[li], 44, "ffn_cb")
        if last:
            gfin = self.load_bcast(self.final_norm, D, "gfin")
            fss = Ring([cx.sb([128, 2], F32, "fin_ss") for _ in range(2)])
        hbs = Ring([cx.sb([128, 8, 1026], BF16, "ffn_hb") for _ in range(2)])
        dhb = Ring([cx.dsem() for _ in range(2)])
        aT = cx.sb([128, 22, 1024], BF16, "ffn_aT")
        wus = Ring([cx.sb([128, 2, 8, 128], BF16, "ffn_wu") for _ in range(3)])
        dwu = Ring([cx.dsem() for _ in range(3)])
        accg = Ring([cx.sb([128, 512], F32, "ffn_accg") for _ in range(2)])
        accu = Ring([cx.sb([128, 512], F32, "ffn_accu") for _ in range(2)])
        sgs = Ring([cx.sb([128, 512], F32, "ffn_sg") for _ in range(2)])
        xos = Ring([cx.sb([128, D], F32, "ffn_xo") for _ in range(2)])
        dxo = Ring([cx.dsem() for _ in range(2)])
        psu = Ring([cx.ps([128, 512], F32, "ffn_psu") for _ in range(4)])
        psd = Ring([cx.ps([128, 512], F32, "ffn_psd") for _ in range(2)])
        hv = self.h2T_d.rearrange("(k p) t -> p k t", p=128)
        subs = ((0, 342), (342, 342), (684, 340))
        for bk in range(4):
            hb = hbs.next()
            lo = max(0, bk * 1024 - 1)
            hi = min(S, bk * 1024 + 1025)
            o0 = lo - (bk * 1024 - 1)
            if bk == 0:
                cx.op("pool", lambda e, hb=hb: e.memset(hb.t[:, :, 0:1], 0.0), [], [hb])
            if bk == 3:
                cx.op("pool", lambda e, hb=hb: e.memset(hb.t[:, :, 1025:1026], 0.0), [], [hb])
            cx.dma("sp", hb.t[:, :, o0:o0 + hi - lo], hv[:, :, lo:hi], [], [hb], ds=dhb.next())
            for fc in range(22):
                wu = wus.next()
                dd = dwu.next()
                cx.dma("sp", wu.t[:, 0], self.wup_fm[li, fc], [], [wu], ds=dd)
                cx.dma("sp", wu.t[:, 1], self.wup_fm[li, 22 + fc], [], [wu], ds=dd)
                for (s0, w) in subs:
                    n = w + 2
                    accs = []
                    for which in range(2):
                        ps = psu.next()
                        for kc in range(8):
                            self.mm(ps.t[:, 0:n], wu.t[:, which, kc, :], hb.t[:, kc, s0:s0 + n], kc == 0, kc == 7, [wu, hb], [ps], kc == 7)
                        ch = fc + 22 * which
                        acc = (accg if which == 0 else accu).next()
                        cx.op("act", lambda e, acc=acc, ps=ps, w=w, ch=ch: e.activation(acc.t[:, 0:w], ps.t[:, 1:1 + w], AF.Identity,
                              bias=cb.t[:, ch:ch + 1], scale=cw.t[:, ch, 1:2]), [ps, cb, cw], [acc])
                        for k in (0, 2):
                            cx.op("dve", lambda e, acc=acc, ps=ps, w=w, ch=ch, k=k: e.scalar_tensor_tensor(
                                acc.t[:, 0:w], ps.t[:, k:k + w], cw.t[:, ch, k:k + 1], acc.t[:, 0:w], op0=ALU.mult, op1=ALU.add), [ps, cw, acc], [acc])
                        accs.append(acc)
                    sg = sgs.next()
                    cx.op("act", lambda e, sg=sg, a=accs[0], w=w: e.activation(sg.t[:, 0:w], a.t[:, 0:w], AF.Silu), [accs[0]], [sg])
                    cx.op("dve", lambda e, sg=sg, a=accs[1], w=w, fc=fc, s0=s0: e.tensor_tensor(aT.t[:, fc, s0:s0 + w], sg.t[:, 0:w], a.t[:, 0:w], ALU.mult), [sg, accs[1]], [aT])
            for t8 in range(8):
                tt = bk * 8 + t8
                xt = self.n_xt.next()
                cx.dma("sp", xt.t[:], self.xa[tt * 128:(tt + 1) * 128, :], [], [xt], ds=self.n_dx.next())
                xo = xos.next()
                for half in range(2):
                    ps = psd.next()
                    hs = slice(half * 512, (half + 1) * 512)
                    for kc in range(22):
                        self.mm(ps.t[:], aT.t[:, kc, t8 * 128:(t8 + 1) * 128], wd.t[:, kc, hs], kc == 0, kc == 21, [aT, wd], [ps], kc == 21)
                    cx.op("dve", lambda e, xo=xo, ps=ps, xt=xt, hs=hs: e.tensor_tensor(xo.t[:, hs], ps.t[:], xt.t[:, hs], ALU.add), [ps, xt], [xo])
                if not last:
                    cx.dma("act", self.xb[tt * 128:(tt + 1) * 128, :], xo.t[:], [xo], [], ds=dxo.next())
                else:
                    ss = fss.next()
                    self.rstd_of(xo.t[:], xo, ss, D)
                    cx.op("dve", lambda e, xo=xo, ss=ss: e.scalar_tensor_tensor(xo.t[:], xo.t[:], ss.t[:, 1:2], gfin.t[:], op0=ALU.mult, op1=ALU.mult), [xo, ss, gfin], [xo])
                    cx.dma("act", self.y[tt * 128:(tt + 1) * 128, :], xo.t[:], [xo], [], ds=dxo.next())
        cx.close_scope()

    def build(self):
        cx = self.cx
        self.consts()
        self.cv_jobs = []
        self.cv_pending = None
        self.early_conv = self.want("conv")
        if self.early_conv and self.phases is not None:
            cx.open_scope()
            self.conv_setup(["dve", "act"], ["sp"], ["act"])
            for j in self.conv_jobs(0, "in"):
                j()()
            if "ssd" not in self.phases:
                for j in self.conv_jobs(0, "rest"):
                    j()()
            cx.close_scope()
            self.early_conv = False
        cx.open_scope()
        self.norm_setup()
        for li in range(self.layers):
            x_src = self.x if li == 0 else self.xb
            cx.open_scope()
            hT = cx.sb([128, 8, S + 2 * PADH], BF16, "hT")
            cx.op("pool", lambda e: e.memset(hT.t[:, :, 0:PADH], 0.0), [], [hT])
            cx.op("pool", lambda e: e.memset(hT.t[:, :, PADH + S:], 0.0), [], [hT])
            gT = self.load_gT(self.mix_norm[li], 8, "gT_mix")
            self.g8h = []
            g8c = cx.sb([64, 8], F32, "g8c")
            stg = self.gstg.next()
            cx.dma("sp", stg.t[0:8, 0:64], self.attn_norm[li].rearrange("(c p) -> c p", p=64), [], [stg], ds=self.dqr.next())
            tp = self.n_tp.next()
            pv = tp.t[:].rearrange("p a b -> p (a b)").bitcast(F32)
            self.mm(pv[0:64, 0:8], stg.t[0:8, 0:64], self.ident_f.t[0:8, 0:8], True, True, [stg, self.ident_f], [tp], True)
            cx.op("dve", lambda e: e.tensor_scalar(g8c.t[:, :], pv[0:64, 0:8], 8.0, None, op0=ALU.mult), [tp], [g8c])
            self.g8c = g8c
            ovl = self.early_conv and li == 0
            if ovl:
                cx.open_scope()
                self.conv_setup(["act", "dve"], ["sp"], ["act"])
                self.cv_jobs = self.conv_jobs(0, "in")
            if self.want("norm"):
                self.phase_norm(x_src, gT, hT)
            if ovl:
                self.conv_flush()
                cx.close_scope()
            if "hT" in self.debug:
                dd = cx.dsem()
                cx.dma("sp", self.scr("hT", [128, 8, S + 2 * PADH], BF16), hT.t[:], [hT], [], ds=dd)
            if self.want("attn"):
                self.phase_attn(li, hT)
            if self.want("zdt"):
                self.phase_zdt(li, hT)
            if self.want("xbc"):
                self.phase_xbc(li, hT)
            if self.want("sc"):
                self.phase_sc(li, hT)
            cx.close_scope()
            if self.want("ssd"):
                self.phase_ssd(li)
            if self.want("wout"):
                self.phase_wout(li, x_src)
            if self.want("ffn"):
                self.phase_ffn_norm(li)
                self.phase_ffn(li, li == DEPTH - 1)
        cx.close_scope()
        cx.finish()
        self.lp.close()
        return self.nc


_CACHE = {}


def kernel(**inputs):
    if "nc" not in _CACHE:
        _CACHE["nc"] = Builder().build()
    nc = _CACHE["nc"]
    names = ["mix_norm", "w_in", "ssd_conv_w", "ssd_conv_b", "ssd_dt_bias", "ssd_a_log", "ssd_d", "ssd_norm",
             "sc_conv_w", "sc_conv_b", "attn_norm", "sc_norm", "w_out", "ffn_norm", "w_up", "ffn_conv_w",
             "ffn_conv_b", "w_down", "final_norm"]
    shared = {}
    for n in names:
        a = np.ascontiguousarray(np.asarray(inputs[n], dtype=np.float32))
        if n in ("ssd_dt_bias", "ssd_a_log"):
            a = a.reshape(DEPTH, 16)
        shared[n] = a
    x = np.asarray(inputs["x"], dtype=np.float32)
    in_maps = [dict(shared, x=np.ascontiguousarray(x[b])) for b in range(8)]
    res = run_bass_kernel_spmd(nc, in_maps, core_ids=list(range(8)))
    return np.stack([res.results[b]["y"] for b in range(8)], axis=0).astype(np.float32)
```

```python
import math
import numpy as np
from contextlib import ExitStack
import concourse.bass as bass
import concourse.mybir as mybir
from concourse.bass_utils import run_bass_kernel_spmd

F32 = mybir.dt.float32
BF16 = mybir.dt.bfloat16
I32 = mybir.dt.int32
AF = mybir.ActivationFunctionType
ALU = mybir.AluOpType

S = 4096
D = 1024
DEPTH = 2
NT = S // 128
D_IN = 4624
D_MIX = 1536
D_FF = 2816
EPS = 1e-6
PADH = 2
PADK = 1024
MASKV = -30000.0
DIL = (1, 4, 16)
C_Q, C_K, C_V, C_Z, C_XBC, C_DT, C_GB, C_GC, C_HC = 0, 512, 1024, 1536, 2048, 3072, 3088, 3600, 4112
FM_COLS = ([C_Q + 128 * i for i in range(4)] + [C_K + 128 * i for i in range(4)] + [C_V + 128 * i for i in range(4)]
           + [C_XBC + 128 * i for i in range(8)] + [C_GB + 128 * i for i in range(4)]
           + [C_GC + 128 * i for i in range(4)] + [C_HC + 128 * i for i in range(4)])
FM_Q, FM_K, FM_V, FM_XBC, FM_GB, FM_GC, FM_HC = 0, 4, 8, 12, 20, 24, 28


def sl(st, n, d):
    return slice(st, st + (n - 1) * d + 1, d)


class R:
    __slots__ = ("w", "rs", "name")

    def __init__(self, name=""):
        self.w = None
        self.rs = []
        self.name = name


class T:
    def __init__(self, t, name=""):
        self.t = t
        self.r = R(name)


class Ring:
    def __init__(self, items):
        self.items = items
        self.i = 0

    def next(self):
        it = self.items[self.i % len(self.items)]
        self.i += 1
        return it


class Eng:
    def __init__(self, cx, name, h, is_pe=False):
        self.name = name
        self.h = h
        self.sem = cx.new_sem("e_" + name)
        self.cnt = 0
        self.waited = {}
        self.is_pe = is_pe
        self.pR = []
        self.pW = []
        self.nins = 0


class DSem:
    def __init__(self, cx, name):
        self.sem = cx.new_sem(name)
        self.cnt = 0


class Cx:
    def __init__(self, nc):
        self.nc = nc
        self.stack = ExitStack()
        self.scopes = []
        self.uid = 0
        self.E = {}
        self.E["pe"] = Eng(self, "pe", nc.tensor, is_pe=True)
        self.E["dve"] = Eng(self, "dve", nc.vector)
        self.E["act"] = Eng(self, "act", nc.scalar)
        self.E["pool"] = Eng(self, "pool", nc.gpsimd)
        self.E["sp"] = Eng(self, "sp", nc.sync)
        self.marks = []
        self.dsems = []
        self.free_ds = []
        self.scope_ds = []
        self.ninst = 0

    def new_sem(self, name):
        return self.stack.enter_context(self.nc.semaphore(name))

    def dsem(self, name=None):
        if self.free_ds:
            d = self.free_ds.pop()
        else:
            self.uid += 1
            d = DSem(self, name or f"d{self.uid}")
            self.dsems.append(d)
        if self.scope_ds:
            self.scope_ds[-1].append(d)
        return d

    def _stk(self):
        return self.scopes[-1] if self.scopes else self.stack

    def sb(self, shape, dtype, name=None):
        self.uid += 1
        nm = (name or "sb") + f"_{self.uid}"
        return T(self._stk().enter_context(self.nc.sbuf_tensor(nm, list(shape), dtype)), nm)

    def ps(self, shape, dtype, name=None):
        self.uid += 1
        nm = (name or "ps") + f"_{self.uid}"
        return T(self._stk().enter_context(self.nc.psum_tensor(nm, list(shape), dtype)), nm)

    def open_scope(self):
        self.scopes.append(ExitStack())
        self.scope_ds.append([])

    def close_scope(self):
        self.barrier()
        self.scopes.pop().close()
        self.free_ds += self.scope_ds.pop()

    def _wait(self, eng, tok):
        if tok is None:
            return
        sem, val, owner = tok
        if owner is eng and eng.is_pe:
            return
        key = id(sem)
        if eng.waited.get(key, 0) >= val:
            return
        eng.h.wait_ge(sem, val)
        eng.waited[key] = val
        self.ninst += 1

    def _deps(self, eng, Rd, Wr):
        for r in Rd:
            self._wait(eng, r.w)
        for w in Wr:
            self._wait(eng, w.w)
            for t in w.rs:
                self._wait(eng, t)

    def _commit(self, tok, Rd, Wr):
        for r in Rd:
            r.rs.append(tok)
            if len(r.rs) > 48:
                best = {}
                for t in r.rs:
                    k = id(t[0])
                    if k not in best or best[k][1] < t[1]:
                        best[k] = t
                r.rs = list(best.values())
        for w in Wr:
            w.w = tok
            w.rs = []

    def op(self, en, fn, Rd=(), Wr=(), inc=True):
        eng = self.E[en]
        Rd = [x.r if isinstance(x, T) else x for x in Rd]
        Wr = [x.r if isinstance(x, T) else x for x in Wr]
        self._deps(eng, Rd, Wr)
        ins = fn(eng.h)
        self.ninst += 1
        eng.nins += 1
        if not inc:
            assert eng.is_pe
            eng.pR += Rd
            eng.pW += Wr
            return None
        eng.cnt += 1
        ins.then_inc(eng.sem, 1)
        tok = (eng.sem, eng.cnt, eng)
        self._commit(tok, list(Rd) + eng.pR, list(Wr) + eng.pW)
        eng.pR = []
        eng.pW = []
        return tok

    def dma(self, qn, out, in_, Rd=(), Wr=(), ds=None, **kw):
        eng = self.E[qn]
        Rd = [x.r if isinstance(x, T) else x for x in Rd]
        Wr = [x.r if isinstance(x, T) else x for x in Wr]
        self._deps(eng, Rd, Wr)
        ins = eng.h.dma_start(out=out, in_=in_, **kw)
        ds.cnt += 16
        ins.then_inc(ds.sem, 16)
        tok = (ds.sem, ds.cnt, ds)
        self._commit(tok, Rd, Wr)
        self.ninst += 1
        return tok

    def barrier(self):
        assert not self.E["pe"].pR and not self.E["pe"].pW
        toks = []
        for e in self.E.values():
            if e.cnt:
                toks.append((e.sem, e.cnt, e))
        for d in self.dsems:
            if d.cnt:
                toks.append((d.sem, d.cnt, d))
        for e in self.E.values():
            for t in toks:
                if t[2] is e:
                    continue
                self._wait(e, t)

    def finish(self):
        self.barrier()
        while self.scopes:
            self.scopes.pop().close()
        self.stack.close()


class Builder:
    def __init__(self, debug=(), layers=DEPTH, phases=None):
        self.debug = set(debug)
        self.layers = layers
        self.phases = phases
        nc = bass.Bass("TRN2", target_bir_lowering=False)
        self.nc = nc
        self.lp = ExitStack()
        self.lp.enter_context(nc.allow_low_precision("bf16 matmul operands, fp32 accumulation (reference tolerance)"))
        self.lp.enter_context(nc.allow_non_contiguous_dma("small gain/bias vector layouts"))
        ein = lambda n, s: nc.dram_tensor(n, list(s), F32, kind="ExternalInput").ap()
        self.x = ein("x", [S, D])
        self.mix_norm = ein("mix_norm", [DEPTH, D])
        self.w_in = ein("w_in", [DEPTH, D, D_IN])
        self.ssd_conv_w = ein("ssd_conv_w", [DEPTH, 5, 1024])
        self.ssd_conv_b = ein("ssd_conv_b", [DEPTH, 1024])
        self.ssd_dt_bias = ein("ssd_dt_bias", [DEPTH, 16])
        self.ssd_a_log = ein("ssd_a_log", [DEPTH, 16])
        self.ssd_d = ein("ssd_d", [DEPTH, 8])
        self.ssd_norm = ein("ssd_norm", [DEPTH, 512])
        self.sc_conv_w = ein("sc_conv_w", [DEPTH, 3, 512])
        self.sc_conv_b = ein("sc_conv_b", [DEPTH, 512])
        self.attn_norm = ein("attn_norm", [DEPTH, 512])
        self.sc_norm = ein("sc_norm", [DEPTH, 512])
        self.w_out = ein("w_out", [DEPTH, D_MIX, D])
        self.ffn_norm = ein("ffn_norm", [DEPTH, D])
        self.w_up = ein("w_up", [DEPTH, D, 2 * D_FF])
        self.ffn_conv_w = ein("ffn_conv_w", [DEPTH, 3, 2 * D_FF])
        self.ffn_conv_b = ein("ffn_conv_b", [DEPTH, 2 * D_FF])
        self.w_down = ein("w_down", [DEPTH, D_FF, D])
        self.final_norm = ein("final_norm", [D])
        self.y = nc.dram_tensor("y", [S, D], F32, kind="ExternalOutput").ap()

        def scr(name, shape, dt):
            kind = "ExternalOutput" if name in self.debug else "Internal"
            return nc.dram_tensor(name, list(shape), dt, kind=kind).ap()

        self.scr = scr
        self.win_fm = scr("win_fm", [DEPTH, 32, 128, 8, 128], BF16)
        self.wz_b = scr("wz_b", [DEPTH, 128, 8, 512], BF16)
        self.wdt_b = scr("wdt_b", [DEPTH, 128, 8, 16], BF16)
        self.wout_b = scr("wout_b", [DEPTH, 128, 12, 1024], BF16)
        self.wup_fm = scr("wup_fm", [DEPTH, 44, 128, 8, 128], BF16)
        self.wdown_b = scr("wdown_b", [DEPTH, 128, 22, 1024], BF16)
        self.xa = scr("xa", [S, D], F32)
        self.xb = scr("xb", [S, D], F32)
        self.mixT = scr("mixT", [D_MIX, S], BF16)
        self.z_d = scr("z_d", [S, 512], F32)
        self.dt_d = scr("dt_d", [S, 16], F32)
        self.BT_d = scr("BT_d", [256, S], BF16)
        self.CT_d = scr("CT_d", [256, S], BF16)
        self.xtok_d = scr("xtok_d", [S, 512], BF16)
        self.Btok_d = scr("Btok_d", [S, 256], BF16)
        self.h2T_d = scr("h2T_d", [D, S], BF16)
        self.cx = Cx(nc)

    def mm(self, out, lhsT, rhs, start, stop, Rd, Wr, inc):
        self.cx.op("pe", lambda e: e.matmul(out, lhsT, rhs, start=start, stop=stop), Rd, Wr, inc=inc)

    def want(self, ph):
        ok = self.phases is None or ph in self.phases
        if ok:
            self.cx.marks.append((ph, {k: e.nins for k, e in self.cx.E.items()}))
        return ok

    def consts(self):
        cx = self.cx
        nc = self.nc
        self.dqr = Ring([cx.dsem() for _ in range(24)])
        di = cx.sb([128, 128], I32, "di")
        dF = cx.sb([128, 128], F32, "dF")
        cx.op("pool", lambda e: e.iota(di.t[:], pattern=[[-1, 128]], base=0, channel_multiplier=1), [], [di])
        cx.op("dve", lambda e: e.tensor_copy(dF.t[:], di.t[:]), [di], [dF])

        def cmpmask(name, opc, dt=F32):
            m = cx.sb([128, 128], dt, name)
            cx.op("dve", lambda e: e.tensor_scalar(m.t[:], dF.t[:], 0.0, None, op0=opc), [dF], [m])
            return m

        self.U_incl = cmpmask("U_incl", ALU.is_le)
        self.L_incl = cmpmask("L_incl", ALU.is_ge)
        self.Lstrict = cmpmask("Lstrict", ALU.is_gt)
        self.Ustrict = cmpmask("Ustrict", ALU.is_lt)
        self.ident_bf = cmpmask("ident_bf", ALU.is_equal, BF16)
        self.ident_f = cmpmask("ident_f", ALU.is_equal, F32)
        self.ones_f = cx.sb([128, 128], F32, "ones_f")
        cx.op("dve", lambda e: e.memset(self.ones_f.t[:], 1.0), [], [self.ones_f])
        self.blk64 = cx.sb([128, 128], F32, "blk64")
        cx.op("dve", lambda e: e.memset(self.blk64.t[:], 0.0), [], [self.blk64])
        cx.op("dve", lambda e: e.memset(self.blk64.t[0:64, 0:64], 1.0), [], [self.blk64])
        cx.op("dve", lambda e: e.memset(self.blk64.t[64:128, 64:128], 1.0), [], [self.blk64])
        self.eps1 = cx.sb([128, 1], F32, "eps1")
        cx.op("dve", lambda e: e.memset(self.eps1.t[:], EPS), [], [self.eps1])
        self.eps64 = cx.sb([128, 1], F32, "eps64")
        cx.op("dve", lambda e: e.memset(self.eps64.t[:], 64.0 * EPS), [], [self.eps64])
        self.W65 = cx.sb([128, 64], BF16, "W65")
        cx.op("dve", lambda e: e.memset(self.W65.t[:], 1.0), [], [self.W65])
        cx.op("dve", lambda e: e.memset(self.W65.t[64:65, :], 64.0 * EPS), [], [self.W65])
        self.blk64b = cx.sb([128, 128], BF16, "blk64b")
        cx.op("dve", lambda e: e.tensor_copy(self.blk64b.t[:], self.blk64.t[:]), [self.blk64], [self.blk64b])
        absA = cx.sb([128, 128], F32, "absA")
        absB = cx.sb([128, 128], F32, "absB")
        mA = cx.sb([128, 128], F32, "mA")
        mB = cx.sb([128, 128], F32, "mB")
        for aX, sh in ((absA, -64.0), (absB, 64.0)):
            cx.op("dve", lambda e, sh=sh: e.tensor_scalar(mA.t[:], dF.t[:], sh, None, op0=ALU.add), [dF], [mA])
            cx.op("dve", lambda e, sh=sh: e.tensor_scalar(mB.t[:], dF.t[:], -1.0, -sh, op0=ALU.mult, op1=ALU.add), [dF], [mB])
            cx.op("dve", lambda e, aX=aX: e.tensor_tensor(aX.t[:], mA.t[:], mB.t[:], ALU.max), [mA, mB], [aX])
        cx.op("dve", lambda e: e.tensor_scalar(mA.t[:], absA.t[:], 64.0, MASKV, op0=ALU.is_gt, op1=ALU.mult), [absA], [mA])
        cx.op("dve", lambda e: e.tensor_scalar(mB.t[:], absB.t[:], 64.0, MASKV, op0=ALU.is_gt, op1=ALU.mult), [absB], [mB])
        self.bias = cx.sb([128, 48, 128], BF16, "attbias")
        for h in range(8):
            slope = 2.0 ** (-8.0 * (h + 1) / 8)
            for b in range(3):
                coef = -slope * DIL[b]
                for ab, (aX, mX) in enumerate(((absA, mA), (absB, mB))):
                    idx = (h * 3 + b) * 2 + ab
                    cx.op("dve", lambda e, idx=idx, aX=aX, mX=mX, coef=coef: e.scalar_tensor_tensor(
                        self.bias.t[:, idx, :], aX.t[:], coef, mX.t[:], op0=ALU.mult, op1=ALU.add),
                        [aX, mX], [self.bias])

    def conv_setup(self, engs, qs_in, qs_out):
        cx = self.cx
        self.cv_stg = Ring([cx.sb([128, 4096], F32, "wstg") for _ in range(2)])
        self.cv_obf = Ring([cx.sb([128, 4096], BF16, "wobf") for _ in range(2)])
        self.cv_din = Ring([cx.dsem() for _ in range(2)])
        self.cv_dout = Ring([cx.dsem() for _ in range(2)])
        self.cv_engs = Ring(engs)
        self.cv_qin = Ring(qs_in)
        self.cv_qout = Ring(qs_out)

    def _cv_load(self, src, kc, nb):
        cx = self.cx
        s = self.cv_stg.next()
        n = kc * nb
        cx.dma(self.cv_qin.next(), s.t[:, 0:n].rearrange("p (k n) -> p k n", k=kc), src.rearrange("(k p) n -> p k n", p=128),
               [], [s], ds=self.cv_din.next())

        def cast(perm):
            o = self.cv_obf.next()
            en = self.cv_engs.next()
            if perm:
                nchunk = nb // 128
                ov = o.t[:, 0:n].rearrange("p (c k n) -> p c k n", c=nchunk, k=kc)
                iv = s.t[:, 0:n].rearrange("p (k c n) -> p c k n", k=kc, c=nchunk)
                for c in range(nchunk):
                    if en == "act":
                        cx.op(en, lambda e, c=c: e.copy(ov[:, c], iv[:, c]), [s], [o])
                    else:
                        cx.op(en, lambda e, c=c: e.tensor_copy(ov[:, c], iv[:, c]), [s], [o])
            else:
                if en == "act":
                    cx.op(en, lambda e: e.copy(o.t[:, 0:n], s.t[:, 0:n]), [s], [o])
                else:
                    cx.op(en, lambda e: e.tensor_copy(o.t[:, 0:n], s.t[:, 0:n]), [s], [o])
            return o, n
        return cast

    def _cv_fm(self, src, kc, nchunk, dst):
        cast = self._cv_load(src, kc, nchunk * 128)

        def fin():
            o, n = cast(True)
            self.cx.dma(self.cv_qout.next(), dst.rearrange("c p k n -> p c k n"),
                        o.t[:, 0:n].rearrange("p (c k n) -> p c k n", c=nchunk, k=kc), [o], [], ds=self.cv_dout.next())
        return fin

    def _cv_r(self, src, kc, nb, dst):
        cast = self._cv_load(src, kc, nb)

        def fin():
            o, n = cast(False)
            self.cx.dma(self.cv_qout.next(), dst, o.t[:, 0:n].rearrange("p (k n) -> p k n", k=kc), [o], [], ds=self.cv_dout.next())
        return fin

    def conv_jobs(self, li, part):
        jobs = []
        if part == "in":
            w = self.w_in
            for seg in range(8):
                c0 = FM_COLS[seg * 4]
                jobs.append(lambda seg=seg, c0=c0: self._cv_fm(w[li, :, c0:c0 + 512], 8, 4, self.win_fm[li, seg * 4:seg * 4 + 4]))
            jobs.append(lambda: self._cv_r(w[li, :, C_Z:C_Z + 512], 8, 512, self.wz_b[li]))
            jobs.append(lambda: self._cv_r(w[li, :, C_DT:C_DT + 16], 8, 16, self.wdt_b[li]))
        else:
            for j in range(4):
                jobs.append(lambda j=j: self._cv_r(self.w_out[li, :, j * 256:(j + 1) * 256], 12, 256, self.wout_b[li, :, :, j * 256:(j + 1) * 256]))
            for j in range(11):
                jobs.append(lambda j=j: self._cv_fm(self.w_up[li, :, j * 512:(j + 1) * 512], 8, 4, self.wup_fm[li, j * 4:j * 4 + 4]))
            for j in range(8):
                jobs.append(lambda j=j: self._cv_r(self.w_down[li, :, j * 128:(j + 1) * 128], 22, 128, self.wdown_b[li, :, :, j * 128:(j + 1) * 128]))
        return jobs

    def conv_tick(self):
        nxt = self.cv_jobs.pop(0)() if self.cv_jobs else None
        if self.cv_pending is not None:
            self.cv_pending()
        self.cv_pending = nxt

    def conv_flush(self):
        while self.cv_jobs or self.cv_pending is not None:
            self.conv_tick()

    def norm_setup(self):
        cx = self.cx
        self.n_tp = Ring([cx.ps([128, 8, 128], BF16, "n_tp") for _ in range(1)])
        self.gstg = Ring([cx.sb([64, 128], F32, "gstg") for _ in range(2)])

    def norm_bufs(self, nx=3, with_norm=True):
        cx = self.cx
        self.n_xt = Ring([cx.sb([128, D], F32, "n_xt") for _ in range(nx)])
        self.n_dx = Ring([cx.dsem() for _ in range(nx + 1)])
        if with_norm:
            self.n_junk = cx.sb([128, D], BF16, "n_junk")
            self.n_ss = Ring([cx.sb([128, 2], F32, "n_ss") for _ in range(3)])
            self.n_xn = Ring([cx.sb([128, D], BF16, "n_xn") for _ in range(2)])
            self.n_tp2 = Ring([self.n_tp.items[0], cx.ps([128, 8, 128], BF16, "n_tp2")])

    def _row_T(self, src_row, nchunk, dst_ap, dst_T, mult=1.0):
        cx = self.cx
        stg = self.gstg.next()
        cx.dma("sp", stg.t[0:nchunk, :], src_row.rearrange("(c p) -> c p", p=128), [], [stg], ds=self.dqr.next())
        tp = self.n_tp.next()
        pv = tp.t[:].rearrange("p a b -> p (a b)").bitcast(F32)
        self.mm(pv[:, 0:nchunk], stg.t[0:nchunk, :], self.ident_f.t[0:nchunk, 0:nchunk], True, True, [stg, self.ident_f], [tp], True)
        cx.op("dve", lambda e: e.tensor_scalar(dst_ap, pv[:, 0:nchunk], mult, None, op0=ALU.mult), [tp], [dst_T])

    def load_gT(self, src_row, nchunk, name, mult=1.0):
        g = self.cx.sb([128, nchunk], F32, name)
        self._row_T(src_row, nchunk, g.t[:, :], g, mult)
        return g

    def rstd_of(self, x_ap, xT, ss, n):
        cx = self.cx
        cx.op("dve", lambda e: e.scalar_tensor_tensor(self.n_junk.t[:, 0:n], x_ap, 1.0, x_ap, op0=ALU.mult, op1=ALU.mult,
                                                      accum_out=ss.t[:, 0:1]), [xT], [self.n_junk, ss])
        cx.op("act", lambda e: e.activation(ss.t[:, 1:2], ss.t[:, 0:1], AF.Ln, bias=self.eps1.t[:, 0:1], scale=1.0 / n), [ss, self.eps1], [ss])
        cx.op("act", lambda e: e.activation(ss.t[:, 1:2], ss.t[:, 1:2], AF.Exp, scale=-0.5), [ss], [ss])

    def norm_tile_a(self, xt):
        cx = self.cx
        ss = self.n_ss.next()
        xn = self.n_xn.next()
        tp = self.n_tp2.next()
        self.rstd_of(xt.t[:], xt, ss, D)
        cx.op("act", lambda e: e.activation(xn.t[:], xt.t[:], AF.Copy, scale=ss.t[:, 1:2]), [xt, ss], [xn])
        for j in range(8):
            cx.op("pe", lambda e, j=j: e.transpose(tp.t[:, j, :], xn.t[:, j * 128:(j + 1) * 128], self.ident_bf.t[:]),
                  [xn, self.ident_bf], [tp], inc=(j == 7))
        return tp

    def norm_tile_b(self, tp, gT, out_ap, out_R):
        self.cx.op("dve", lambda e: e.tensor_tensor(out_ap, tp.t[:], gT.t[:, :].unsqueeze(2).to_broadcast([128, 8, 128]), ALU.mult),
                   [tp, gT], [out_R])

    def phase_norm(self, x_src, gT, hT):
        cx = self.cx
        cx.open_scope()
        self.norm_bufs()
        prev = None
        for tt in range(NT):
            xt = self.n_xt.next()
            cx.dma("sp", xt.t[:], x_src[tt * 128:(tt + 1) * 128, :], [], [xt], ds=self.n_dx.next())
            tp = self.norm_tile_a(xt)
            if prev is not None:
                self.norm_tile_b(prev[0], gT, hT.t[:, :, PADH + prev[1] * 128:PADH + (prev[1] + 1) * 128], hT)
            prev = (tp, tt)
            if tt % 3 == 0:
                self.conv_tick()
        self.norm_tile_b(prev[0], gT, hT.t[:, :, PADH + prev[1] * 128:PADH + (prev[1] + 1) * 128], hT)
        cx.close_scope()

    def phase_attn(self, li, hT):
        cx = self.cx
        cx.open_scope()
        qT = cx.sb([128, S], BF16, "qT")
        kTs = [cx.sb([128, S + 2 * PADK], BF16, "kT0"), cx.sb([128, S + 2 * PADK], BF16, "kT1")]
        vT = cx.sb([128, S + 2 * PADK], BF16, "vT")
        NV = 117
        V = cx.sb([128, NV, 2, 65], BF16, "Vaug")
        acc = cx.sb([65, 1, S], F32, "acc")
        wq = cx.sb([128, 8, 128], BF16, "wq")
        wk = cx.sb([128, 8, 128], BF16, "wk")
        wv = cx.sb([128, 8, 128], BF16, "wv")
        dw = [cx.dsem() for _ in range(3)]
        pT = Ring([cx.sb([128, 8, 128], BF16, "pT") for _ in range(2)])
        nrm_a = Ring([cx.sb([64, 512], F32, "nrm_a") for _ in range(2)])
        nrm_b = Ring([cx.sb([64, 512], F32, "nrm_b") for _ in range(2)])
        nrm_c = Ring([cx.sb([65, 512], BF16, "nrm_c") for _ in range(2)])
        nrm_o = Ring([cx.sb([64, 512], BF16, "nrm_o") for _ in range(2)])
        d_o = Ring([cx.dsem() for _ in range(2)])
        banks7 = [cx.ps([128, 512], F32, "att_ps") for _ in range(7)]
        ps_proj = Ring(banks7)
        ps_s = Ring(banks7[0:4])
        ps_o = Ring(banks7[4:7])
        cx.op("pool", lambda e: e.memset(kTs[0].t[64:128, :], 0.0), [], [kTs[0]])
        cx.op("pool", lambda e: e.memset(kTs[1].t[0:64, :], 0.0), [], [kTs[1]])
        cx.op("pool", lambda e: e.memset(kTs[0].t[0:64, 0:PADK], 0.0), [], [kTs[0]])
        cx.op("pool", lambda e: e.memset(kTs[0].t[0:64, PADK + S:], 0.0), [], [kTs[0]])
        cx.op("pool", lambda e: e.memset(kTs[1].t[64:128, 0:PADK], 0.0), [], [kTs[1]])
        cx.op("pool", lambda e: e.memset(kTs[1].t[64:128, PADK + S:], 0.0), [], [kTs[1]])
        cx.op("pool", lambda e: e.memset(vT.t[:, 0:PADK], 0.0), [], [vT])
        cx.op("pool", lambda e: e.memset(vT.t[:, PADK + S:], 0.0), [], [vT])
        cx.op("pool", lambda e: e.memset(V.t[:, :, :, 64:65], 1.0), [], [V])
        voff = []
        o = 0
        for b in range(3):
            voff.append(o)
            o += DIL[b] * (S // DIL[b] // 128 + 1)
        assert o == NV
        for b in range(3):
            d = DIL[b]
            ntq = S // d // 128
            for c in range(d):
                i0 = voff[b] + c * (ntq + 1)
                cx.op("pool", lambda e, i0=i0: e.memset(V.t[0:64, i0, :, 64:65], 0.0), [], [V])
                cx.op("pool", lambda e, i1=i0 + ntq: e.memset(V.t[64:128, i1, :, 64:65], 0.0), [], [V])

        def vps(ps):
            return ps.t[:].rearrange("p (a b) -> p a b", a=4)

        evac = Ring(["act", "dve"])
        for hp in range(4):
            for wt, fm0, ds in ((wq, FM_Q, dw[0]), (wk, FM_K, dw[1]), (wv, FM_V, dw[2])):
                cx.dma("sp", wt.t[:], self.win_fm[li, fm0 + hp], [], [wt], ds=ds)
            for which, wt in enumerate((wq, wk, wv)):
                for tb in range(8):
                    ps = ps_proj.next()
                    for kc in range(8):
                        self.mm(ps.t[:], wt.t[:, kc, :], hT.t[:, kc, PADH + tb * 512:PADH + (tb + 1) * 512],
                                kc == 0, kc == 7, [wt, hT], [ps], kc == 7)
                    en = evac.next()
                    if which == 0:
                        outs = [(qT, qT.t[:, tb * 512:(tb + 1) * 512], ps.t[:], 0.125)]
                    elif which == 1:
                        cs_ = slice(PADK + tb * 512, PADK + (tb + 1) * 512)
                        outs = [(kTs[0], kTs[0].t[0:64, cs_], ps.t[0:64, :], 1.0), (kTs[1], kTs[1].t[64:128, cs_], ps.t[64:128, :], 1.0)]
                    else:
                        outs = [(vT, vT.t[:, PADK + tb * 512:PADK + (tb + 1) * 512], ps.t[:], 1.0)]
                    for (dstT, dap, sap, scale) in outs:
                        if en == "act":
                            cx.op("act", lambda e, dap=dap, sap=sap, scale=scale: e.activation(dap, sap, AF.Copy, scale=scale), [ps], [dstT])
                        else:
                            cx.op("dve", lambda e, dap=dap, sap=sap, scale=scale: e.tensor_scalar(dap, sap, scale, None, op0=ALU.mult), [ps], [dstT])
            for b in range(3):
                d = DIL[b]
                ntq = S // d // 128
                for c in range(d):
                    m = 0
                    while m < ntq + 1:
                        g = min(4, ntq + 1 - m)
                        ps = ps_proj.next()
                        pv = ps.t[:].bitcast(BF16)
                        for j in range(g):
                            st = PADK + d * (128 * (m + j) - 64) + c
                            cx.op("pe", lambda e, j=j, st=st, d=d, pv=pv: e.transpose(
                                pv[:, j * 128:(j + 1) * 128], vT.t[:, sl(st, 128, d)], self.ident_bf.t[:]),
                                [vT, self.ident_bf], [ps], inc=(j == g - 1))
                        i0 = voff[b] + c * (ntq + 1) + m
                        en = evac.next()
                        src = pv[:, 0:g * 128].rearrange("p (g h f) -> p g h f", g=g, h=2)
                        dstap = V.t[:, i0:i0 + g, :, 0:64]
                        if en == "act":
                            cx.op("act", lambda e, src=src, dstap=dstap: e.copy(dstap, src), [ps], [V])
                        else:
                            cx.op("dve", lambda e, src=src, dstap=dstap: e.tensor_copy(dstap, src), [ps], [V])
                        m += g
            for hh in range(2):
                h = hp * 2 + hh
                kT = kTs[hh]
                groups = []
                for b in range(3):
                    d = DIL[b]
                    ntq = S // d // 128
                    G = min(4, ntq)
                    for c in range(d):
                        for j0 in range(0, ntq, G):
                            groups.append((b, d, ntq, G, c, j0))

                def emit_S(grp):
                    b, d, ntq, G, c, j0 = grp
                    banks = [ps_s.next() for _ in range((2 * G + 3) // 4)]
                    for jl in range(G):
                        j = j0 + jl
                        for ab in range(2):
                            slot = jl * 2 + ab
                            bank = banks[slot // 4]
                            ks = 128 * j - 64 + 128 * ab
                            kc0 = PADK + d * ks + c
                            qc0 = d * 128 * j + c
                            oap = vps(bank)[:, slot % 4, :]
                            self.mm(oap, kT.t[:, sl(kc0, 128, d)], qT.t[:, sl(qc0, 128, d)],
                                    slot % 4 == 0, False, [kT, qT], [bank], False)
                            if slot % 4 == 3:
                                i0 = (h * 3 + b) * 2
                                self.mm(bank.t[:], self.ident_bf.t[:],
                                        self.bias.t[:, i0:i0 + 2, :].unsqueeze(1).to_broadcast([128, 2, 2, 128]),
                                        False, True, [self.ident_bf, self.bias], [bank], True)
                    return banks

                def emit_rest(grp, banks):
                    b, d, ntq, G, c, j0 = grp
                    pt = pT.next()
                    for bi, bank in enumerate(banks):
                        ns = min(4, 2 * G - bi * 4)
                        cx.op("act", lambda e, bank=bank, bi=bi, ns=ns, pt=pt: e.activation(
                            pt.t[:, bi * 4:bi * 4 + ns, :], vps(bank)[:, 0:ns, :], AF.Exp), [bank], [pt])
                    po = ps_o.next()
                    for jl in range(G):
                        j = j0 + jl
                        for ab in range(2):
                            vi = voff[b] + c * (ntq + 1) + j + ab
                            self.mm(po.t[0:65, jl * 128:(jl + 1) * 128], V.t[:, vi, hh, :], pt.t[:, jl * 2 + ab, :],
                                    ab == 0, ab == 1, [V, pt], [po], (jl == G - 1 and ab == 1))
                    t0 = d * 128 * j0 + c
                    aap = acc.t[0:65, 0, sl(t0, G * 128, d)]
                    if b == 0:
                        cx.op("dve", lambda e, aap=aap, po=po, G=G: e.tensor_copy(aap, po.t[0:65, 0:G * 128]), [po], [acc])
                    else:
                        cx.op("dve", lambda e, aap=aap, po=po, G=G: e.tensor_tensor(aap, po.t[0:65, 0:G * 128], aap, ALU.add),
                              [po, acc], [acc])

                prev = None
                for grp in groups:
                    bk = emit_S(grp)
                    if prev is not None:
                        emit_rest(*prev)
                    prev = (grp, bk)
                emit_rest(*prev)
                def n1(tb):
                    cs = slice(tb * 512, (tb + 1) * 512)
                    rc = nrm_c.next()
                    cx.op("dve", lambda e: e.tensor_tensor(rc.t[:], acc.t[0:65, 0, cs], acc.t[0:65, 0, cs], ALU.mult), [acc], [rc])
                    ps2 = ps_proj.next()
                    self.mm(ps2.t[0:64, :], self.W65.t[0:65, :], rc.t[:], True, True, [self.W65, rc], [ps2], True)
                    return (cs, ps2)

                def n2(st, hh=hh, hp=hp):
                    cs, ps2 = st
                    ra = nrm_a.next()
                    cx.op("act", lambda e: e.activation(ra.t[:], ps2.t[0:64, :], AF.Ln), [ps2], [ra])
                    ra2 = nrm_b.next()
                    cx.op("act", lambda e: e.activation(ra2.t[:], ra.t[:], AF.Exp, scale=-0.5), [ra], [ra2])
                    ro = nrm_o.next()
                    cx.op("dve", lambda e: e.scalar_tensor_tensor(
                        ro.t[:], acc.t[0:64, 0, cs], self.g8c.t[:, hp * 2 + hh:hp * 2 + hh + 1], ra2.t[:], op0=ALU.mult, op1=ALU.mult), [acc, ra2, self.g8c], [ro])
                    row0 = (hp * 2 + hh) * 64
                    cx.dma("act", self.mixT[row0:row0 + 64, cs], ro.t[:], [ro], [], ds=d_o.next())

                pv_ = None
                for tb in range(8):
                    cur_ = n1(tb)
                    if pv_ is not None:
                        n2(pv_)
                    pv_ = cur_
                n2(pv_)
        cx.close_scope()

    def load_bcast(self, src_row, n, name):
        cx = self.cx
        t = cx.sb([128, n], F32, name)
        cx.dma("sp", t.t[:, :], src_row.unsqueeze(0).partition_broadcast(128)[:, 0, :], [], [t], ds=self.dqr.next())
        return t

    def load_cw(self, src, k, nchunk, name):
        t = self.cx.sb([128, nchunk, k], F32, name)
        for kk in range(k):
            self._row_T(src[kk, :], nchunk, t.t[:, :, kk], t)
        return t

    def phase_zdt(self, li, hT):
        cx = self.cx
        cx.open_scope()
        wz = cx.sb([128, 8, 512], BF16, "wz")
        wdt = cx.sb([128, 8, 16], BF16, "wdt")
        cx.dma("sp", wz.t[:], self.wz_b[li], [], [wz], ds=cx.dsem())
        cx.dma("sp", wdt.t[:], self.wdt_b[li], [], [wdt], ds=cx.dsem())
        zs = Ring([cx.sb([128, 512], F32, "zs") for _ in range(2)])
        dts = Ring([cx.sb([128, 16], F32, "dts") for _ in range(2)])
        dz = Ring([cx.dsem() for _ in range(2)])
        dd = Ring([cx.dsem() for _ in range(2)])
        psz = Ring([cx.ps([128, 512], F32, "psz") for _ in range(2)])
        psd = Ring([cx.ps([128, 512], F32, "psd") for _ in range(2)])
        for tt in range(NT):
            pz = psz.next()
            pd = psd.next()
            tok = slice(PADH + tt * 128, PADH + (tt + 1) * 128)
            for kc in range(8):
                self.mm(pz.t[:], hT.t[:, kc, tok], wz.t[:, kc, :], kc == 0, kc == 7, [hT, wz], [pz], kc == 7)
            for kc in range(8):
                self.mm(pd.t[:, 0:16], hT.t[:, kc, tok], wdt.t[:, kc, :], kc == 0, kc == 7, [hT, wdt], [pd], kc == 7)
            z = zs.next()
            cx.op("act", lambda e, z=z, pz=pz: e.copy(z.t[:], pz.t[:]), [pz], [z])
            cx.dma("act", self.z_d[tt * 128:(tt + 1) * 128, :], z.t[:], [z], [], ds=dz.next())
            dt = dts.next()
            cx.op("dve", lambda e, dt=dt, pd=pd: e.tensor_copy(dt.t[:], pd.t[:, 0:16]), [pd], [dt])
            cx.dma("act", self.dt_d[tt * 128:(tt + 1) * 128, :], dt.t[:], [dt], [], ds=dd.next())
        cx.close_scope()

    def phase_xbc(self, li, hT):
        cx = self.cx
        cx.open_scope()
        cw = self.load_cw(self.ssd_conv_w[li], 5, 8, "xbc_cw")
        cb = self.load_gT(self.ssd_conv_b[li], 8, "xbc_cb")
        wts = Ring([cx.sb([128, 8, 128], BF16, "xbc_w") for _ in range(2)])
        dws = Ring([cx.dsem() for _ in range(2)])
        rows = Ring([cx.sb([128, S], BF16, "xbc_row") for _ in range(2)])
        drow = Ring([cx.dsem() for _ in range(2)])
        accs = Ring([cx.sb([128, 512], F32, "xbc_acc") for _ in range(2)])
        toks = Ring([cx.sb([128, 32, 128], BF16, "xbc_tok") for _ in range(2)])
        dtok = Ring([cx.dsem() for _ in range(2)])
        pss = Ring([cx.ps([128, 512], F32, "xbc_ps") for _ in range(3)])
        pst = Ring([cx.ps([128, 4, 128], BF16, "xbc_pst") for _ in range(2)])
        W = 508
        for fc in range(8):
            wt = wts.next()
            cx.dma("sp", wt.t[:], self.win_fm[li, FM_XBC + fc], [], [wt], ds=dws.next())
            row = rows.next()
            for t0 in range(0, S, W):
                w = min(W, S - t0)
                n = w + 4
                ps = pss.next()
                for kc in range(8):
                    self.mm(ps.t[:, 0:n], wt.t[:, kc, :], hT.t[:, kc, PADH + t0 - 2:PADH + t0 - 2 + n], kc == 0, kc == 7, [wt, hT], [ps], kc == 7)
                acc = accs.next()
                cx.op("act", lambda e, acc=acc, ps=ps, w=w, fc=fc: e.activation(acc.t[:, 0:w], ps.t[:, 2:2 + w], AF.Identity,
                      bias=cb.t[:, fc:fc + 1], scale=cw.t[:, fc, 2:3]), [ps, cb, cw], [acc])
                for k in (0, 1, 3, 4):
                    cx.op("dve", lambda e, acc=acc, ps=ps, w=w, fc=fc, k=k: e.scalar_tensor_tensor(
                        acc.t[:, 0:w], ps.t[:, k:k + w], cw.t[:, fc, k:k + 1], acc.t[:, 0:w], op0=ALU.mult, op1=ALU.add), [ps, cw, acc], [acc])
                cx.op("act", lambda e, acc=acc, row=row, t0=t0, w=w: e.activation(row.t[:, t0:t0 + w], acc.t[:, 0:w], AF.Silu), [acc], [row])
            if fc >= 4:
                dst = self.BT_d if fc < 6 else self.CT_d
                r0 = ((fc - 4) % 2) * 128
                cx.dma("act", dst[r0:r0 + 128, :], row.t[:], [row], [], ds=drow.next())
            if fc < 6:
                tokb = toks.next()
                for tq in range(8):
                    pt = pst.next()
                    for j in range(4):
                        tt = tq * 4 + j
                        cx.op("pe", lambda e, pt=pt, j=j, tt=tt, row=row: e.transpose(pt.t[:, j, :], row.t[:, tt * 128:(tt + 1) * 128], self.ident_bf.t[:]),
                              [row, self.ident_bf], [pt], inc=(j == 3))
                    cx.op("act", lambda e, pt=pt, tokb=tokb, tq=tq: e.copy(tokb.t[:, tq * 4:tq * 4 + 4, :], pt.t[:]), [pt], [tokb])
                if fc < 4:
                    dst = self.xtok_d[:, fc * 128:(fc + 1) * 128]
                else:
                    dst = self.Btok_d[:, (fc - 4) * 128:(fc - 3) * 128]
                dv = dst.rearrange("(t p) f -> p t f", p=128)
                dk = dtok.next()
                for q4 in range(4):
                    cx.dma("act", dv[:, q4 * 8:(q4 + 1) * 8, :], tokb.t[:, q4 * 8:(q4 + 1) * 8, :], [tokb], [], ds=dk)
        cx.close_scope()

    def phase_sc(self, li, hT):
        cx = self.cx
        cx.open_scope()
        cw = self.load_cw(self.sc_conv_w[li], 3, 4, "sc_cw")
        cb = self.load_gT(self.sc_conv_b[li], 4, "sc_cb")
        g8 = self.load_gT(self.sc_norm[li], 4, "sc_g8", mult=8.0)
        wts = [Ring([cx.sb([128, 8, 128], BF16, "sc_w") for _ in range(2)]) for _ in range(3)]
        dws = [Ring([cx.dsem() for _ in range(2)]) for _ in range(3)]
        pss = [Ring([cx.ps([128, 512], F32, "sc_ps") for _ in range(2)]) for _ in range(3)]
        psn = cx.ps([128, 512], F32, "sc_psn")
        gcs = Ring([cx.sb([128, 512], F32, "sc_gcs") for _ in range(2)])
        tts = Ring([cx.sb([128, 512], F32, "sc_tt") for _ in range(2)])
        accs = Ring([cx.sb([128, 512], F32, "sc_acc") for _ in range(2)])
        yvs = Ring([cx.sb([128, 512], F32, "sc_yv") for _ in range(2)])
        ysq = Ring([cx.sb([128, 512], BF16, "sc_ysq") for _ in range(2)])
        rrs = Ring([cx.sb([128, 512], F32, "sc_rr") for _ in range(2)])
        outs = Ring([cx.sb([128, 512], BF16, "sc_out") for _ in range(2)])
        douts = Ring([cx.dsem() for _ in range(2)])
        W = 510
        for c4 in range(4):
            ws = []
            for i, fm0 in enumerate((FM_GB, FM_GC, FM_HC)):
                wt = wts[i].next()
                cx.dma("sp", wt.t[:], self.win_fm[li, fm0 + c4], [], [wt], ds=dws[i].next())
                ws.append(wt)
            def sc_proj(t0, ws=ws):
                w = min(W, S - t0)
                n = w + 2
                pp = []
                for i in range(3):
                    ps = pss[i].next()
                    for kc in range(8):
                        self.mm(ps.t[:, 0:n], ws[i].t[:, kc, :], hT.t[:, kc, PADH + t0 - 1:PADH + t0 - 1 + n], kc == 0, kc == 7, [ws[i], hT], [ps], kc == 7)
                    pp.append(ps)
                return (t0, w, n, pp)

            def sc_rest(st, c4=c4):
                t0, w, n, pp = st
                pgb, pgc, phc = pp
                gc = gcs.next()
                cx.op("act", lambda e, gc=gc, pgc=pgc, n=n: e.copy(gc.t[:, 0:n], pgc.t[:, 0:n]), [pgc], [gc])
                tt = tts.next()
                cx.op("dve", lambda e, tt=tt, phc=phc, gc=gc, n=n: e.tensor_tensor(tt.t[:, 0:n], phc.t[:, 0:n], gc.t[:, 0:n], ALU.mult), [phc, gc], [tt])
                acc = accs.next()
                cx.op("act", lambda e, acc=acc, tt=tt, w=w, c4=c4: e.activation(acc.t[:, 0:w], tt.t[:, 1:1 + w], AF.Identity,
                      bias=cb.t[:, c4:c4 + 1], scale=cw.t[:, c4, 1:2]), [tt, cb, cw], [acc])
                for k in (0, 2):
                    cx.op("dve", lambda e, acc=acc, tt=tt, w=w, c4=c4, k=k: e.scalar_tensor_tensor(
                        acc.t[:, 0:w], tt.t[:, k:k + w], cw.t[:, c4, k:k + 1], acc.t[:, 0:w], op0=ALU.mult, op1=ALU.add), [tt, cw, acc], [acc])
                yv = yvs.next()
                cx.op("dve", lambda e, yv=yv, pgb=pgb, acc=acc, w=w: e.tensor_tensor(yv.t[:, 0:w], pgb.t[:, 1:1 + w], acc.t[:, 0:w], ALU.mult), [pgb, acc], [yv])
                yq = ysq.next()
                cx.op("dve", lambda e, yq=yq, yv=yv, w=w: e.tensor_tensor(yq.t[:, 0:w], yv.t[:, 0:w], yv.t[:, 0:w], ALU.mult), [yv], [yq])
                self.mm(psn.t[:, 0:w], self.blk64b.t[:], yq.t[:, 0:w], True, True, [self.blk64b, yq], [psn], True)
                rr = rrs.next()
                cx.op("act", lambda e, rr=rr, w=w: e.activation(rr.t[:, 0:w], psn.t[:, 0:w], AF.Ln, bias=self.eps64.t[:, 0:1]), [psn, self.eps64], [rr])
                cx.op("act", lambda e, rr=rr, w=w: e.activation(rr.t[:, 0:w], rr.t[:, 0:w], AF.Exp, scale=-0.5), [rr], [rr])
                ob = outs.next()
                cx.op("dve", lambda e, ob=ob, yv=yv, rr=rr, w=w, c4=c4: e.scalar_tensor_tensor(
                    ob.t[:, 0:w], yv.t[:, 0:w], g8.t[:, c4:c4 + 1], rr.t[:, 0:w], op0=ALU.mult, op1=ALU.mult), [yv, g8, rr], [ob])
                cx.dma("act", self.mixT[1024 + c4 * 128:1024 + (c4 + 1) * 128, t0:t0 + w], ob.t[:, 0:w], [ob], [], ds=douts.next())

            prev = None
            for t0 in range(0, S, W):
                cur = sc_proj(t0)
                if prev is not None:
                    sc_rest(prev)
                prev = cur
            sc_rest(prev)
        cx.close_scope()

    def phase_ssd(self, li):
        cx = self.cx
        cx.open_scope()
        bias16 = self.load_bcast(self.ssd_dt_bias[li], 16, "ssd_bias16")
        a16 = self.load_bcast(self.ssd_a_log[li], 16, "ssd_a16")
        cx.op("act", lambda e: e.activation(a16.t[:], a16.t[:], AF.Exp), [a16], [a16])
        cx.op("dve", lambda e: e.tensor_scalar(a16.t[:], a16.t[:], -1.0, None, op0=ALU.mult), [a16], [a16])
        d8 = self.load_bcast(self.ssd_d[li], 8, "ssd_d8")
        Dfull = cx.sb([128, 8, 64], F32, "ssd_Dfull")
        cx.op("dve", lambda e: e.tensor_copy(Dfull.t[:], d8.t[:, :].unsqueeze(2).to_broadcast([128, 8, 64])), [d8], [Dfull])
        gS = self.load_gT(self.ssd_norm[li], 4, "ssd_gS")
        prevB = cx.sb([128, NT, 512], BF16, "ssd_prevB")
        state_f = cx.sb([128, 512], F32, "ssd_state_f")
        state_b = cx.sb([128, 512], F32, "ssd_state_b")
        stf_bf = cx.sb([128, 512], BF16, "ssd_stf_bf")
        cx.op("pool", lambda e: e.memset(state_f.t[:], 0.0), [], [state_f])
        cx.op("pool", lambda e: e.memset(state_b.t[:], 0.0), [], [state_b])
        cx.op("pool", lambda e: e.memset(stf_bf.t[:], 0.0), [], [stf_bf])
        dtrs = Ring([cx.sb([128, 16], F32, "ssd_dtr") for _ in range(4)])
        xts = Ring([cx.sb([128, 512], BF16, "ssd_xt") for _ in range(4)])
        bts = Ring([cx.sb([128, 256], BF16, "ssd_bt") for _ in range(4)])
        BTs = Ring([cx.sb([128, 2, 128], BF16, "ssd_BT") for _ in range(4)])
        CTs = Ring([cx.sb([128, 2, 128], BF16, "ssd_CT") for _ in range(4)])
        zts = Ring([cx.sb([128, 512], F32, "ssd_zt") for _ in range(4)])
        dl = [Ring([cx.dsem() for _ in range(5)]) for _ in range(6)]
        t16 = Ring([cx.sb([128, 16], F32, "ssd_t16") for _ in range(8)])
        dts_ = Ring([cx.sb([128, 16], F32, "ssd_dt") for _ in range(4)])
        acs = Ring([cx.sb([128, 16], F32, "ssd_ac") for _ in range(4)])
        Es = Ring([cx.sb([128, 32], F32, "ssd_E") for _ in range(4)])
        wdts = Ring([cx.sb([128, 8], F32, "ssd_wdt") for _ in range(4)])
        xdtf = Ring([cx.sb([128, 512], BF16, "ssd_xdtf") for _ in range(2)])
        xdtb = Ring([cx.sb([128, 512], BF16, "ssd_xdtb") for _ in range(2)])
        xwf = Ring([cx.sb([128, 512], BF16, "ssd_xwf") for _ in range(2)])
        xDs = Ring([cx.sb([128, 512], BF16, "ssd_xD") for _ in range(2)])
        Gmf = Ring([cx.sb([128, 2, 128], F32, "ssd_Gmf") for _ in range(2)])
        Gmb = Ring([cx.sb([128, 2, 128], F32, "ssd_Gmb") for _ in range(2)])
        lhss = Ring([cx.sb([128, 128], F32, "ssd_lhs") for _ in range(32)])
        expds = Ring([cx.sb([128, 4, 128], F32, "ssd_expd") for _ in range(2)])
        MTs = [Ring([cx.sb([128, 8, 128], BF16, "ssd_MT") for _ in range(2)]) for _ in range(2)]
        y1s = Ring([cx.sb([128, 512], F32, "ssd_y1") for _ in range(2)])
        tmps = Ring([cx.sb([128, 512], F32, "ssd_tmp") for _ in range(2)])
        szs = Ring([cx.sb([128, 512], F32, "ssd_sz") for _ in range(2)])
        ss2 = Ring([cx.sb([128, 4], F32, "ssd_ss2") for _ in range(2)])
        yns = Ring([cx.sb([128, 512], BF16, "ssd_yn") for _ in range(2)])
        sTs = Ring([cx.sb([128, 4, 512], BF16, "ssd_sT") for _ in range(2)])
        dsT = Ring([cx.dsem() for _ in range(2)])
        junk = cx.sb([128, 256], F32, "ssd_junk")
        psA = cx.ps([128, 512], F32, "ssd_psA")
        RpsG = psA.r
        RpsS = psA.r
        psS = psA.t
        diffs = Ring([cx.ps([128, 4, 128], F32, "ssd_diff") for _ in range(2)])
        psy = cx.ps([128, 512], F32, "ssd_psy")
        psyo1 = cx.ps([128, 512], F32, "ssd_psyo")
        psyo2 = cx.ps([128, 512], F32, "ssd_psyo2")
        psyo = [psyo1, psyo2]
        pscs = cx.ps([128, 512], F32, "ssd_pscs")
        if self.phases is None or "conv" in self.phases:
            self.conv_setup(["act"], ["sp"], ["act"])
            self.cv_jobs = self.conv_jobs(li, "rest") + (self.conv_jobs(li + 1, "in") if li + 1 < self.layers else [])
        BTv = self.BT_d.rearrange("(g n) t -> n g t", g=2)
        CTv = self.CT_d.rearrange("(g n) t -> n g t", g=2)

        def bc8(ap8):
            return ap8.unsqueeze(2).to_broadcast([128, 8, 64])

        def v3(ap):
            return ap.rearrange("p (h f) -> p h f", h=8)

        def softplus(dst, src_ap, bias_ap, n, Rsrc):
            ta = t16.next()
            tb = t16.next()
            cx.op("dve", lambda e: e.tensor_tensor(ta.t[:, 0:n], src_ap, bias_ap, ALU.add), [Rsrc, bias16], [ta])
            cx.op("act", lambda e: e.activation(tb.t[:, 0:n], ta.t[:, 0:n], AF.Exp), [ta], [tb])
            cx.op("act", lambda e: e.activation(dst, tb.t[:, 0:n], AF.Ln, bias=1.0), [tb], [])

        for c in range(NT - 1, -1, -1):
            cx.op("act", lambda e, c=c: e.copy(prevB.t[:, c, :], state_b.t[:]), [state_b], [prevB])
            if c == 0:
                break
            if c % 2 == 0:
                self.conv_tick()
            tok = slice(c * 128, (c + 1) * 128)
            dtr = dtrs.next()
            cx.dma("sp", dtr.t[:], self.dt_d[tok, :], [], [dtr], ds=dl[0].next())
            xt = xts.next()
            cx.dma("sp", xt.t[:], self.xtok_d[tok, :], [], [xt], ds=dl[1].next())
            bt = bts.next()
            cx.dma("sp", bt.t[:], self.Btok_d[tok, :], [], [bt], ds=dl[2].next())
            dt = dts_.next()
            ta = t16.next()
            tb = t16.next()
            cx.op("dve", lambda e, ta=ta, dtr=dtr: e.tensor_tensor(ta.t[:, 0:8], dtr.t[:, 8:16], bias16.t[:, 8:16], ALU.add), [dtr, bias16], [ta])
            cx.op("act", lambda e, ta=ta, tb=tb: e.activation(tb.t[:, 0:8], ta.t[:, 0:8], AF.Exp), [ta], [tb])
            cx.op("act", lambda e, dt=dt, tb=tb: e.activation(dt.t[:, 0:8], tb.t[:, 0:8], AF.Ln, bias=1.0), [tb], [dt])
            ac = acs.next()
            cx.op("dve", lambda e, ac=ac, dt=dt: e.tensor_tensor(ac.t[:, 0:8], dt.t[:, 0:8], a16.t[:, 8:16], ALU.mult), [dt, a16], [ac])
            self.mm(psS[:, 0:8], self.Ustrict.t[:], ac.t[:, 0:8], True, True, [self.Ustrict, ac], [RpsS], False)
            self.mm(psS[:, 8:16], self.ones_f.t[:], ac.t[:, 0:8], True, True, [self.ones_f, ac], [RpsS], True)
            E = Es.next()
            cx.op("act", lambda e, E=E: e.activation(E.t[:, 0:16], psS[:, 0:16], AF.Exp), [RpsS], [E])
            wdt = wdts.next()
            cx.op("dve", lambda e, wdt=wdt, dt=dt, E=E: e.tensor_tensor(wdt.t[:], dt.t[:, 0:8], E.t[:, 0:8], ALU.mult), [dt, E], [wdt])
            xw = xwf.next()
            cx.op("dve", lambda e, xw=xw, xt=xt, wdt=wdt: e.tensor_tensor(v3(xw.t[:]), v3(xt.t[:]), bc8(wdt.t[:, :]), ALU.mult), [xt, wdt], [xw])
            for g in range(2):
                self.mm(pscs.t[:, g * 256:(g + 1) * 256], bt.t[:, g * 128:(g + 1) * 128], xw.t[:, g * 256:(g + 1) * 256], True, True, [bt, xw], [pscs], g == 1)
            cx.op("dve", lambda e, E=E: e.tensor_tensor(v3(state_b.t[:]), v3(state_b.t[:]), bc8(E.t[:, 8:16]), ALU.mult), [state_b, E], [state_b])
            cx.op("dve", lambda e: e.tensor_tensor(state_b.t[:], state_b.t[:], pscs.t[:], ALU.add), [state_b, pscs], [state_b])

        lhs_eng = Ring(["act", "dve"])
        sT_box = [None]

        def stageA0(c):
            tok = slice(c * 128, (c + 1) * 128)
            dtr = dtrs.next()
            cx.dma("sp", dtr.t[:], self.dt_d[tok, :], [], [dtr], ds=dl[0].next())
            xt = xts.next()
            cx.dma("sp", xt.t[:], self.xtok_d[tok, :], [], [xt], ds=dl[1].next())
            bt = bts.next()
            cx.dma("sp", bt.t[:], self.Btok_d[tok, :], [], [bt], ds=dl[2].next())
            BTc = BTs.next()
            cx.dma("sp", BTc.t[:], BTv[:, :, tok], [], [BTc], ds=dl[3].next())
            CTc = CTs.next()
            cx.dma("sp", CTc.t[:], CTv[:, :, tok], [], [CTc], ds=dl[4].next())
            zt = zts.next()
            cx.dma("sp", zt.t[:], self.z_d[tok, :], [], [zt], ds=dl[5].next())
            dt = dts_.next()
            ta = t16.next()
            tb = t16.next()
            cx.op("dve", lambda e, ta=ta, dtr=dtr: e.tensor_tensor(ta.t[:], dtr.t[:], bias16.t[:], ALU.add), [dtr, bias16], [ta])
            cx.op("act", lambda e, ta=ta, tb=tb: e.activation(tb.t[:], ta.t[:], AF.Exp), [ta], [tb])
            cx.op("act", lambda e, dt=dt, tb=tb: e.activation(dt.t[:], tb.t[:], AF.Ln, bias=1.0), [tb], [dt])
            ac = acs.next()
            cx.op("dve", lambda e, ac=ac, dt=dt: e.tensor_tensor(ac.t[:], dt.t[:], a16.t[:], ALU.mult), [dt, a16], [ac])
            self.mm(psS[:, 0:8], self.U_incl.t[:], ac.t[:, 0:8], True, True, [self.U_incl, ac], [RpsS], False)
            self.mm(psS[:, 8:16], self.L_incl.t[:], ac.t[:, 8:16], True, True, [self.L_incl, ac], [RpsS], False)
            self.mm(psS[:, 16:24], self.Lstrict.t[:], ac.t[:, 0:8], True, True, [self.Lstrict, ac], [RpsS], False)
            self.mm(psS[:, 24:32], self.ones_f.t[:], ac.t[:, 0:8], True, True, [self.ones_f, ac], [RpsS], True)
            E = Es.next()
            cx.op("act", lambda e, E=E: e.activation(E.t[:, 0:32], psS[:, 0:32], AF.Exp), [RpsS], [E])
            wdt = wdts.next()
            cx.op("dve", lambda e, wdt=wdt, dt=dt, E=E: e.tensor_tensor(wdt.t[:], dt.t[:, 0:8], E.t[:, 16:24], ALU.mult), [dt, E], [wdt])
            return dict(c=c, tok=tok, xt=xt, bt=bt, BTc=BTc, CTc=CTc, zt=zt, dt=dt, ac=ac, E=E, wdt=wdt)

        def stageBuild(s0):
            ac = s0["ac"]
            lst = []
            for j in range(16):
                smask = self.Lstrict if j < 8 else self.Ustrict
                lh = lhss.next()
                en = lhs_eng.next()
                if en == "act":
                    cx.op("act", lambda e, lh=lh, smask=smask, j=j: e.activation(lh.t[:], smask.t[:], AF.Copy, scale=ac.t[:, j:j + 1]), [smask, ac], [lh])
                else:
                    cx.op("dve", lambda e, lh=lh, smask=smask, j=j: e.tensor_scalar(lh.t[:], smask.t[:], ac.t[:, j:j + 1], None, op0=ALU.mult), [smask, ac], [lh])
                lst.append(lh)
            s0["lhs"] = lst

        def stageA1(s0):
            c = s0["c"]; tok = s0["tok"]; xt = s0["xt"]; bt = s0["bt"]; BTc = s0["BTc"]; CTc = s0["CTc"]; zt = s0["zt"]
            dt = s0["dt"]; ac = s0["ac"]; E = s0["E"]; wdt = s0["wdt"]
            xf = xdtf.next()
            cx.op("dve", lambda e, xf=xf, xt=xt, dt=dt: e.tensor_tensor(v3(xf.t[:]), v3(xt.t[:]), bc8(dt.t[:, 0:8]), ALU.mult), [xt, dt], [xf])
            xb_ = xdtb.next()
            cx.op("dve", lambda e, xb_=xb_, xt=xt, dt=dt: e.tensor_tensor(v3(xb_.t[:]), v3(xt.t[:]), bc8(dt.t[:, 8:16]), ALU.mult), [xt, dt], [xb_])
            xw = xwf.next()
            cx.op("dve", lambda e, xw=xw, xt=xt, wdt=wdt: e.tensor_tensor(v3(xw.t[:]), v3(xt.t[:]), bc8(wdt.t[:, :]), ALU.mult), [xt, wdt], [xw])
            xD = xDs.next()
            cx.op("dve", lambda e, xD=xD, xt=xt: e.tensor_tensor(v3(xD.t[:]), v3(xt.t[:]), Dfull.t[:], ALU.mult), [xt, Dfull], [xD])
            for g in range(2):
                self.mm(psA.t[:, 128 + g * 128:256 + g * 128], BTc.t[:, g, :], CTc.t[:, g, :], True, True, [BTc, CTc], [RpsG], g == 1)
            gmf = Gmf.next()
            gmb = Gmb.next()
            pg = psA.t[:, 128:384].rearrange("p (g l) -> p g l", g=2)
            cx.op("dve", lambda e, gmf=gmf, pg=pg: e.tensor_tensor(gmf.t[:], pg, self.U_incl.t[:, :].unsqueeze(1).to_broadcast([128, 2, 128]), ALU.mult), [RpsG, self.U_incl], [gmf])
            cx.op("dve", lambda e, gmb=gmb, pg=pg: e.tensor_tensor(gmb.t[:], pg, self.L_incl.t[:, :].unsqueeze(1).to_broadcast([128, 2, 128]), ALU.mult), [RpsG, self.L_incl], [gmb])
            MT = [MTs[0].next(), MTs[1].next()]
            for dr in range(2):
                smask = self.Lstrict if dr == 0 else self.Ustrict
                cmask = self.U_incl if dr == 0 else self.L_incl
                gm = gmf if dr == 0 else gmb
                for g in range(2):
                    bank = diffs.next()
                    for hh in range(4):
                        j = dr * 8 + g * 4 + hh
                        lh = s0["lhs"][j]
                        self.mm(bank.t[:, hh, :], lh.t[:], cmask.t[:], True, True, [lh, cmask], [bank], hh == 3)
                    ex = expds.next()
                    cx.op("act", lambda e, ex=ex, bank=bank: e.activation(ex.t[:], bank.t[:], AF.Exp), [bank], [ex])
                    cx.op("dve", lambda e, ex=ex, gm=gm, g=g, dr=dr: e.tensor_tensor(
                        MT[dr].t[:, g * 4:(g + 1) * 4, :], ex.t[:], gm.t[:, g:g + 1, :].to_broadcast([128, 4, 128]), ALU.mult), [ex, gm], [MT[dr]])
            return dict(c=c, tok=tok, xt=xt, bt=bt, CTc=CTc, zt=zt, E=E, xf=xf, xb_=xb_, xw=xw, xD=xD, MT=MT)

        def stageB(st):
            c = st["c"]; xt = st["xt"]; bt = st["bt"]; CTc = st["CTc"]; zt = st["zt"]; E = st["E"]
            xf = st["xf"]; xb_ = st["xb_"]; xw = st["xw"]; xD = st["xD"]; MT = st["MT"]
            self.mm(psy.t[:], self.ident_bf.t[:], xD.t[:], True, False, [self.ident_bf, xD], [psy], False)
            for h in range(8):
                hs = slice(h * 64, (h + 1) * 64)
                self.mm(psy.t[:, hs], MT[0].t[:, h, :], xf.t[:, hs], False, False, [MT[0], xf], [psy], False)
                self.mm(psy.t[:, hs], MT[1].t[:, h, :], xb_.t[:, hs], False, h == 7, [MT[1], xb_], [psy], h == 7)
            for g in range(2):
                gs = slice(g * 256, (g + 1) * 256)
                self.mm(psyo1.t[:, gs], CTc.t[:, g, :], stf_bf.t[:, gs], True, True, [CTc, stf_bf], [psyo1], g == 1)
            for g in range(2):
                self.mm(pscs.t[:, g * 256:(g + 1) * 256], bt.t[:, g * 128:(g + 1) * 128], xw.t[:, g * 256:(g + 1) * 256], True, True, [bt, xw], [pscs], g == 1)
            for g in range(2):
                gs = slice(g * 256, (g + 1) * 256)
                self.mm(psyo2.t[:, gs], CTc.t[:, g, :], prevB.t[:, c, gs], True, True, [CTc, prevB], [psyo2], g == 1)
            tmf = tmps.next()
            cx.op("dve", lambda e: e.tensor_tensor(v3(tmf.t[:]), v3(psyo1.t[:]), bc8(E.t[:, 0:8]), ALU.mult), [psyo1, E], [tmf])
            cx.op("dve", lambda e: e.tensor_tensor(v3(state_f.t[:]), v3(state_f.t[:]), bc8(E.t[:, 24:32]), ALU.mult), [state_f, E], [state_f])
            cx.op("dve", lambda e: e.tensor_tensor(state_f.t[:], state_f.t[:], pscs.t[:], ALU.add), [state_f, pscs], [state_f])
            cx.op("act", lambda e: e.copy(stf_bf.t[:], state_f.t[:]), [state_f], [stf_bf])
            y1 = y1s.next()
            cx.op("act", lambda e: e.copy(y1.t[:], psy.t[:]), [psy], [y1])
            cx.op("dve", lambda e: e.tensor_tensor(y1.t[:], y1.t[:], tmf.t[:], ALU.add), [y1, tmf], [y1])
            tmb = tmps.next()
            cx.op("dve", lambda e: e.tensor_tensor(v3(tmb.t[:]), v3(psyo2.t[:]), bc8(E.t[:, 8:16]), ALU.mult), [psyo2, E], [tmb])
            cx.op("dve", lambda e: e.tensor_tensor(y1.t[:], y1.t[:], tmb.t[:], ALU.add), [y1, tmb], [y1])
            return (c, y1, zt)

        def stageBt(sb):
            c, y1, zt = sb
            sz = szs.next()
            cx.op("act", lambda e: e.activation(sz.t[:], zt.t[:], AF.Silu), [zt], [sz])
            cx.op("dve", lambda e: e.tensor_tensor(y1.t[:], y1.t[:], sz.t[:], ALU.mult), [y1, sz], [y1])
            s2 = ss2.next()
            for g in range(2):
                cx.op("dve", lambda e, g=g: e.scalar_tensor_tensor(junk.t[:], y1.t[:, g * 256:(g + 1) * 256], 1.0, y1.t[:, g * 256:(g + 1) * 256],
                      op0=ALU.mult, op1=ALU.mult, accum_out=s2.t[:, g:g + 1]), [y1], [junk, s2])
            cx.op("act", lambda e: e.activation(s2.t[:, 2:4], s2.t[:, 0:2], AF.Ln, bias=self.eps1.t[:, 0:1], scale=1.0 / 256), [s2, self.eps1], [s2])
            cx.op("act", lambda e: e.activation(s2.t[:, 2:4], s2.t[:, 2:4], AF.Exp, scale=-0.5), [s2], [s2])
            yn = yns.next()
            cx.op("dve", lambda e: e.tensor_tensor(
                yn.t[:].rearrange("p (g f) -> p g f", g=2), y1.t[:].rearrange("p (g f) -> p g f", g=2),
                s2.t[:, 2:4].unsqueeze(2).to_broadcast([128, 2, 256]), ALU.mult), [y1, s2], [yn])
            return (c, yn)

        def stageC(stc):
            c, yn = stc
            tp = self.n_tp.next()
            for j in range(4):
                cx.op("pe", lambda e, j=j: e.transpose(tp.t[:, j, :], yn.t[:, j * 128:(j + 1) * 128], self.ident_bf.t[:]),
                      [yn, self.ident_bf], [tp], inc=(j == 3))
            if c % 4 == 0:
                sT_box[0] = sTs.next()
            sT = sT_box[0]
            q = c % 4
            cx.op("dve", lambda e: e.tensor_tensor(sT.t[:, :, q * 128:(q + 1) * 128], tp.t[:, 0:4, :],
                  gS.t[:, :].unsqueeze(2).to_broadcast([128, 4, 128]), ALU.mult), [tp, gS], [sT])
            if q == 3:
                cb4 = c // 4
                cx.dma("act", self.mixT[512:1024, cb4 * 512:(cb4 + 1) * 512].rearrange("(ch p) t -> p ch t", p=128), sT.t[:], [sT], [], ds=dsT.next())

        s0 = {}
        for k in range(min(3, NT)):
            s0[k] = stageA0(k)
        stageBuild(s0[0])
        if NT > 1:
            stageBuild(s0[1])
        stA = stageA1(s0.pop(0))
        stC = None
        for c in range(NT):
            sb = stageB(stA)
            stA = stageA1(s0.pop(c + 1)) if c + 1 < NT else None
            if c + 3 < NT:
                s0[c + 3] = stageA0(c + 3)
            if c + 2 < NT:
                stageBuild(s0[c + 2])
            cur = stageBt(sb)
            if stC is not None:
                stageC(stC)
            stC = cur
            if c % 2 == 1:
                self.conv_tick()
        stageC(stC)
        self.conv_flush()
        cx.close_scope()

    def phase_wout(self, li, x_src):
        cx = self.cx
        cx.open_scope()
        self.norm_bufs(nx=3, with_norm=False)
        wo = cx.sb([128, 12, 1024], BF16, "wo")
        cx.dma("sp", wo.t[:], self.wout_b[li], [], [wo], ds=cx.dsem())
        mts = Ring([cx.sb([128, 12, 512], BF16, "wo_mt") for _ in range(2)])
        dmt = Ring([cx.dsem() for _ in range(2)])
        xos = Ring([cx.sb([128, D], F32, "wo_xo") for _ in range(2)])
        dxo = Ring([cx.dsem() for _ in range(2)])
        pss = Ring([cx.ps([128, 512], F32, "wo_ps") for _ in range(4)])
        mv = self.mixT.rearrange("(k p) t -> p k t", p=128)
        for tb in range(8):
            mt = mts.next()
            cx.dma("sp", mt.t[:], mv[:, :, tb * 512:(tb + 1) * 512], [], [mt], ds=dmt.next())
            for t4 in range(4):
                tt = tb * 4 + t4
                xt = self.n_xt.next()
                cx.dma("sp", xt.t[:], x_src[tt * 128:(tt + 1) * 128, :], [], [xt], ds=self.n_dx.next())
                xo = xos.next()
                for half in range(2):
                    ps = pss.next()
                    hs = slice(half * 512, (half + 1) * 512)
                    for kc in range(12):
                        self.mm(ps.t[:], mt.t[:, kc, t4 * 128:(t4 + 1) * 128], wo.t[:, kc, hs], kc == 0, kc == 11, [mt, wo], [ps], kc == 11)
                    cx.op("dve", lambda e, xo=xo, ps=ps, xt=xt, hs=hs: e.tensor_tensor(xo.t[:, hs], ps.t[:], xt.t[:, hs], ALU.add), [ps, xt], [xo])
                cx.dma("act", self.xa[tt * 128:(tt + 1) * 128, :], xo.t[:], [xo], [], ds=dxo.next())
        cx.close_scope()

    def phase_ffn_norm(self, li):
        cx = self.cx
        cx.open_scope()
        self.norm_bufs()
        g2 = self.load_gT(self.ffn_norm[li], 8, "gT_ffn")
        stgs = Ring([cx.sb([128, 8, 512], BF16, "fn_stg") for _ in range(2)])
        dst = Ring([cx.dsem() for _ in range(2)])
        hv = self.h2T_d.rearrange("(k p) t -> p k t", p=128)
        prev = None

        def fin(pv):
            tp, stg, t4, tb = pv
            self.norm_tile_b(tp, g2, stg.t[:, :, t4 * 128:(t4 + 1) * 128], stg)
            if t4 == 3:
                cx.dma("act", hv[:, :, tb * 512:(tb + 1) * 512], stg.t[:], [stg], [], ds=dst.next())

        for tb in range(8):
            stg = stgs.next()
            for t4 in range(4):
                tt = tb * 4 + t4
                xt = self.n_xt.next()
                cx.dma("sp", xt.t[:], self.xa[tt * 128:(tt + 1) * 128, :], [], [xt], ds=self.n_dx.next())
                tp = self.norm_tile_a(xt)
                if prev is not None:
                    fin(prev)
                prev = (tp, stg, t4, tb)
        fin(prev)
        cx.close_scope()

    def phase_ffn(self, li, last):
        cx = self.cx
        cx.open_scope()
        self.norm_bufs(nx=3, with_norm=False)
        self.n_junk = cx.sb([128, D], BF16, "n_junk")
        wd = cx.sb([128, 22, 1024], BF16, "wd")
        cx.dma("sp", wd.t[:], self.wdown_b[li], [], [wd], ds=cx.dsem())
        cw = self.load_cw(self.ffn_conv_w[li], 3, 44, "ffn_cw")
        cb = self.load_gT(self.ffn_conv_b[li], 44, "ffn_cb")
        if last:
            gfin = self.load_bcast(self.final_norm, D, "gfin")
            fss = Ring([cx.sb([128, 2], F32, "fin_ss") for _ in range(2)])
        hbs = Ring([cx.sb([128, 8, 1026], BF16, "ffn_hb") for _ in range(2)])
        dhb = Ring([cx.dsem() for _ in range(2)])
        aT = cx.sb([128, 22, 1024], BF16, "ffn_aT")
        wus = Ring([cx.sb([128, 2, 8, 128], BF16, "ffn_wu") for _ in range(3)])
        dwu = Ring([cx.dsem() for _ in range(3)])
        accg = Ring([cx.sb([128, 512], F32, "ffn_accg") for _ in range(2)])
        accu = Ring([cx.sb([128, 512], F32, "ffn_accu") for _ in range(2)])
        sgs = Ring([cx.sb([128, 512], F32, "ffn_sg") for _ in range(2)])
        xos = Ring([cx.sb([128, D], F32, "ffn_xo") for _ in range(2)])
        dxo = Ring([cx.dsem() for _ in range(2)])
        psu = Ring([cx.ps([128, 512], F32, "ffn_psu") for _ in range(4)])
        psd = Ring([cx.ps([128, 512], F32, "ffn_psd") for _ in range(2)])
        hv = self.h2T_d.rearrange("(k p) t -> p k t", p=128)
        subs = ((0, 342), (342, 342), (684, 340))
        for bk in range(4):
            hb = hbs.next()
            lo = max(0, bk * 1024 - 1)
            hi = min(S, bk * 1024 + 1025)
            o0 = lo - (bk * 1024 - 1)
            if bk == 0:
                cx.op("pool", lambda e, hb=hb: e.memset(hb.t[:, :, 0:1], 0.0), [], [hb])
            if bk == 3:
                cx.op("pool", lambda e, hb=hb: e.memset(hb.t[:, :, 1025:1026], 0.0), [], [hb])
            cx.dma("sp", hb.t[:, :, o0:o0 + hi - lo], hv[:, :, lo:hi], [], [hb], ds=dhb.next())
            for fc in range(22):
                wu = wus.next()
                dd = dwu.next()
                cx.dma("sp", wu.t[:, 0], self.wup_fm[li, fc], [], [wu], ds=dd)
                cx.dma("sp", wu.t[:, 1], self.wup_fm[li, 22 + fc], [], [wu], ds=dd)
                for (s0, w) in subs:
                    n = w + 2
                    accs = []
                    for which in range(2):
                        ps = psu.next()
                        for kc in range(8):
                            self.mm(ps.t[:, 0:n], wu.t[:, which, kc, :], hb.t[:, kc, s0:s0 + n], kc == 0, kc == 7, [wu, hb], [ps], kc == 7)
                        ch = fc + 22 * which
                        acc = (accg if which == 0 else accu).next()
                        cx.op("act", lambda e, acc=acc, ps=ps, w=w, ch=ch: e.activation(acc.t[:, 0:w], ps.t[:, 1:1 + w], AF.Identity,
                              bias=cb.t[:, ch:ch + 1], scale=cw.t[:, ch, 1:2]), [ps, cb, cw], [acc])
                        for k in (0, 2):
                            cx.op("dve", lambda e, acc=acc, ps=ps, w=w, ch=ch, k=k: e.scalar_tensor_tensor(
                                acc.t[:, 0:w], ps.t[:, k:k + w], cw.t[:, ch, k:k + 1], acc.t[:, 0:w], op0=ALU.mult, op1=ALU.add), [ps, cw, acc], [acc])
                        accs.append(acc)
                    sg = sgs.next()
                    cx.op("act", lambda e, sg=sg, a=accs[0], w=w: e.activation(sg.t[:, 0:w], a.t[:, 0:w], AF.Silu), [accs[0]], [sg])
                    cx.op("dve", lambda e, sg=sg, a=accs[1], w=w, fc=fc, s0=s0: e.tensor_tensor(aT.t[:, fc, s0:s0 + w], sg.t[:, 0:w], a.t[:, 0:w], ALU.mult), [sg, accs[1]], [aT])
            for t8 in range(8):
                tt = bk * 8 + t8
                xt = self.n_xt.next()
                cx.dma("sp", xt.t[:], self.xa[tt * 128:(tt + 1) * 128, :], [], [xt], ds=self.n_dx.next())
                xo = xos.next()
                for half in range(2):
                    ps = psd.next()
                    hs = slice(half * 512, (half + 1) * 512)
                    for kc in range(22):
                        self.mm(ps.t[:], aT.t[:, kc, t8 * 128:(t8 + 1) * 128], wd.t[:, kc, hs], kc == 0, kc == 21, [aT, wd], [ps], kc == 21)
                    cx.op("dve", lambda e, xo=xo, ps=ps, xt=xt, hs=hs: e.tensor_tensor(xo.t[:, hs], ps.t[:], xt.t[:, hs], ALU.add), [ps, xt], [xo])
                if not last:
                    cx.dma("act", self.xb[tt * 128:(tt + 1) * 128, :], xo.t[:], [xo], [], ds=dxo.next())
                else:
                    ss = fss.next()
                    self.rstd_of(xo.t[:], xo, ss, D)
                    cx.op("dve", lambda e, xo=xo, ss=ss: e.scalar_tensor_tensor(xo.t[:], xo.t[:], ss.t[:, 1:2], gfin.t[:], op0=ALU.mult, op1=ALU.mult), [xo, ss, gfin], [xo])
                    cx.dma("act", self.y[tt * 128:(tt + 1) * 128, :], xo.t[:], [xo], [], ds=dxo.next())
        cx.close_scope()

    def build(self):
        cx = self.cx
        self.consts()
        self.cv_jobs = []
        self.cv_pending = None
        self.early_conv = self.want("conv")
        if self.early_conv and self.phases is not None:
            cx.open_scope()
            self.conv_setup(["dve", "act"], ["sp"], ["act"])
            for j in self.conv_jobs(0, "in"):
                j()()
            if "ssd" not in self.phases:
                for j in self.conv_jobs(0, "rest"):
                    j()()
            cx.close_scope()
            self.early_conv = False
        cx.open_scope()
        self.norm_setup()
        for li in range(self.layers):
            x_src = self.x if li == 0 else self.xb
            cx.open_scope()
            hT = cx.sb([128, 8, S + 2 * PADH], BF16, "hT")
            cx.op("pool", lambda e: e.memset(hT.t[:, :, 0:PADH], 0.0), [], [hT])
            cx.op("pool", lambda e: e.memset(hT.t[:, :, PADH + S:], 0.0), [], [hT])
            gT = self.load_gT(self.mix_norm[li], 8, "gT_mix")
            self.g8h = []
            g8c = cx.sb([64, 8], F32, "g8c")
            stg = self.gstg.next()
            cx.dma("sp", stg.t[0:8, 0:64], self.attn_norm[li].rearrange("(c p) -> c p", p=64), [], [stg], ds=self.dqr.next())
            tp = self.n_tp.next()
            pv = tp.t[:].rearrange("p a b -> p (a b)").bitcast(F32)
            self.mm(pv[0:64, 0:8], stg.t[0:8, 0:64], self.ident_f.t[0:8, 0:8], True, True, [stg, self.ident_f], [tp], True)
            cx.op("dve", lambda e: e.tensor_scalar(g8c.t[:, :], pv[0:64, 0:8], 8.0, None, op0=ALU.mult), [tp], [g8c])
            self.g8c = g8c
            ovl = self.early_conv and li == 0
            if ovl:
                cx.open_scope()
                self.conv_setup(["act", "dve"], ["sp"], ["act"])
                self.cv_jobs = self.conv_jobs(0, "in")
            if self.want("norm"):
                self.phase_norm(x_src, gT, hT)
            if ovl:
                self.conv_flush()
                cx.close_scope()
            if "hT" in self.debug:
                dd = cx.dsem()
                cx.dma("sp", self.scr("hT", [128, 8, S + 2 * PADH], BF16), hT.t[:], [hT], [], ds=dd)
            if self.want("attn"):
                self.phase_attn(li, hT)
            if self.want("zdt"):
                self.phase_zdt(li, hT)
            if self.want("xbc"):
                self.phase_xbc(li, hT)
            if self.want("sc"):
                self.phase_sc(li, hT)
            cx.close_scope()
            if self.want("ssd"):
                self.phase_ssd(li)
            if self.want("wout"):
                self.phase_wout(li, x_src)
            if self.want("ffn"):
                self.phase_ffn_norm(li)
                self.phase_ffn(li, li == DEPTH - 1)
        cx.close_scope()
        cx.finish()
        self.lp.close()
        return self.nc


_CACHE = {}


def kernel(**inputs):
    if "nc" not in _CACHE:
        _CACHE["nc"] = Builder().build()
    nc = _CACHE["nc"]
    names = ["mix_norm", "w_in", "ssd_conv_w", "ssd_conv_b", "ssd_dt_bias", "ssd_a_log", "ssd_d", "ssd_norm",
             "sc_conv_w", "sc_conv_b", "attn_norm", "sc_norm", "w_out", "ffn_norm", "w_up", "ffn_conv_w",
             "ffn_conv_b", "w_down", "final_norm"]
    shared = {}
    for n in names:
        a = np.ascontiguousarray(np.asarray(inputs[n], dtype=np.float32))
        if n in ("ssd_dt_bias", "ssd_a_log"):
            a = a.reshape(DEPTH, 16)
        shared[n] = a
    x = np.asarray(inputs["x"], dtype=np.float32)
    in_maps = [dict(shared, x=np.ascontiguousarray(x[b])) for b in range(8)]
    res = run_bass_kernel_spmd(nc, in_maps, core_ids=list(range(8)))
    return np.stack([res.results[b]["y"] for b in range(8)], axis=0).astype(np.float32)
```

```python
import math
import numpy as np
from contextlib import ExitStack
import concourse.bass as bass
import concourse.mybir as mybir
from concourse.bass_utils import run_bass_kernel_spmd

F32 = mybir.dt.float32
BF16 = mybir.dt.bfloat16
I32 = mybir.dt.int32
AF = mybir.ActivationFunctionType
ALU = mybir.AluOpType

S = 4096
D = 1024
DEPTH = 2
NT = S // 128
D_IN = 4624
D_MIX = 1536
D_FF = 2816
EPS = 1e-6
PADH = 2
PADK = 1024
MASKV = -30000.0
DIL = (1, 4, 16)
C_Q, C_K, C_V, C_Z, C_XBC, C_DT, C_GB, C_GC, C_HC = 0, 512, 1024, 1536, 2048, 3072, 3088, 3600, 4112
FM_COLS = ([C_Q + 128 * i for i in range(4)] + [C_K + 128 * i for i in range(4)] + [C_V + 128 * i for i in range(4)]
           + [C_XBC + 128 * i for i in range(8)] + [C_GB + 128 * i for i in range(4)]
           + [C_GC + 128 * i for i in range(4)] + [C_HC + 128 * i for i in range(4)])
FM_Q, FM_K, FM_V, FM_XBC, FM_GB, FM_GC, FM_HC = 0, 4, 8, 12, 20, 24, 28


def sl(st, n, d):
    return slice(st, st + (n - 1) * d + 1, d)


class R:
    __slots__ = ("w", "rs", "name")

    def __init__(self, name=""):
        self.w = None
        self.rs = []
        self.name = name


class T:
    def __init__(self, t, name=""):
        self.t = t
        self.r = R(name)


class Ring:
    def __init__(self, items):
        self.items = items
        self.i = 0

    def next(self):
        it = self.items[self.i % len(self.items)]
        self.i += 1
        return it


class Eng:
    def __init__(self, cx, name, h, is_pe=False):
        self.name = name
        self.h = h
        self.sem = cx.new_sem("e_" + name)
        self.cnt = 0
        self.waited = {}
        self.is_pe = is_pe
        self.pR = []
        self.pW = []
        self.nins = 0


class DSem:
    def __init__(self, cx, name):
        self.sem = cx.new_sem(name)
        self.cnt = 0


class Cx:
    def __init__(self, nc):
        self.nc = nc
        self.stack = ExitStack()
        self.scopes = []
        self.uid = 0
        self.E = {}
        self.E["pe"] = Eng(self, "pe", nc.tensor, is_pe=True)
        self.E["dve"] = Eng(self, "dve", nc.vector)
        self.E["act"] = Eng(self, "act", nc.scalar)
        self.E["pool"] = Eng(self, "pool", nc.gpsimd)
        self.E["sp"] = Eng(self, "sp", nc.sync)
        self.marks = []
        self.dsems = []
        self.free_ds = []
        self.scope_ds = []
        self.ninst = 0

    def new_sem(self, name):
        return self.stack.enter_context(self.nc.semaphore(name))

    def dsem(self, name=None):
        if self.free_ds:
            d = self.free_ds.pop()
        else:
            self.uid += 1
            d = DSem(self, name or f"d{self.uid}")
            self.dsems.append(d)
        if self.scope_ds:
            self.scope_ds[-1].append(d)
        return d

    def _stk(self):
        return self.scopes[-1] if self.scopes else self.stack

    def sb(self, shape, dtype, name=None):
        self.uid += 1
        nm = (name or "sb") + f"_{self.uid}"
        return T(self._stk().enter_context(self.nc.sbuf_tensor(nm, list(shape), dtype)), nm)

    def ps(self, shape, dtype, name=None):
        self.uid += 1
        nm = (name or "ps") + f"_{self.uid}"
        return T(self._stk().enter_context(self.nc.psum_tensor(nm, list(shape), dtype)), nm)

    def open_scope(self):
        self.scopes.append(ExitStack())
        self.scope_ds.append([])

    def close_scope(self):
        self.barrier()
        self.scopes.pop().close()
        self.free_ds += self.scope_ds.pop()

    def _wait(self, eng, tok):
        if tok is None:
            return
        sem, val, owner = tok
        if owner is eng and eng.is_pe:
            return
        key = id(sem)
        if eng.waited.get(key, 0) >= val:
            return
        eng.h.wait_ge(sem, val)
        eng.waited[key] = val
        self.ninst += 1

    def _deps(self, eng, Rd, Wr):
        for r in Rd:
            self._wait(eng, r.w)
        for w in Wr:
            self._wait(eng, w.w)
            for t in w.rs:
                self._wait(eng, t)

    def _commit(self, tok, Rd, Wr):
        for r in Rd:
            r.rs.append(tok)
            if len(r.rs) > 48:
                best = {}
                for t in r.rs:
                    k = id(t[0])
                    if k not in best or best[k][1] < t[1]:
                        best[k] = t
                r.rs = list(best.values())
        for w in Wr:
            w.w = tok
            w.rs = []

    def op(self, en, fn, Rd=(), Wr=(), inc=True):
        eng = self.E[en]
        Rd = [x.r if isinstance(x, T) else x for x in Rd]
        Wr = [x.r if isinstance(x, T) else x for x in Wr]
        self._deps(eng, Rd, Wr)
        ins = fn(eng.h)
        self.ninst += 1
        eng.nins += 1
        if not inc:
            assert eng.is_pe
            eng.pR += Rd
            eng.pW += Wr
            return None
        eng.cnt += 1
        ins.then_inc(eng.sem, 1)
        tok = (eng.sem, eng.cnt, eng)
        self._commit(tok, list(Rd) + eng.pR, list(Wr) + eng.pW)
        eng.pR = []
        eng.pW = []
        return tok

    def dma(self, qn, out, in_, Rd=(), Wr=(), ds=None, **kw):
        eng = self.E[qn]
        Rd = [x.r if isinstance(x, T) else x for x in Rd]
        Wr = [x.r if isinstance(x, T) else x for x in Wr]
        self._deps(eng, Rd, Wr)
        ins = eng.h.dma_start(out=out, in_=in_, **kw)
        ds.cnt += 16
        ins.then_inc(ds.sem, 16)
        tok = (ds.sem, ds.cnt, ds)
        self._commit(tok, Rd, Wr)
        self.ninst += 1
        return tok

    def barrier(self):
        assert not self.E["pe"].pR and not self.E["pe"].pW
        toks = []
        for e in self.E.values():
            if e.cnt:
                toks.append((e.sem, e.cnt, e))
        for d in self.dsems:
            if d.cnt:
                toks.append((d.sem, d.cnt, d))
        for e in self.E.values():
            for t in toks:
                if t[2] is e:
                    continue
                self._wait(e, t)

    def finish(self):
        self.barrier()
        while self.scopes:
            self.scopes.pop().close()
        self.stack.close()


class Builder:
    def __init__(self, debug=(), layers=DEPTH, phases=None):
        self.debug = set(debug)
        self.layers = layers
        self.phases = phases
        nc = bass.Bass("TRN2", target_bir_lowering=False)
        self.nc = nc
        self.lp = ExitStack()
        self.lp.enter_context(nc.allow_low_precision("bf16 matmul operands, fp32 accumulation (reference tolerance)"))
        self.lp.enter_context(nc.allow_non_contiguous_dma("small gain/bias vector layouts"))
        ein = lambda n, s: nc.dram_tensor(n, list(s), F32, kind="ExternalInput").ap()
        self.x = ein("x", [S, D])
        self.mix_norm = ein("mix_norm", [DEPTH, D])
        self.w_in = ein("w_in", [DEPTH, D, D_IN])
        self.ssd_conv_w = ein("ssd_conv_w", [DEPTH, 5, 1024])
        self.ssd_conv_b = ein("ssd_conv_b", [DEPTH, 1024])
        self.ssd_dt_bias = ein("ssd_dt_bias", [DEPTH, 16])
        self.ssd_a_log = ein("ssd_a_log", [DEPTH, 16])
        self.ssd_d = ein("ssd_d", [DEPTH, 8])
        self.ssd_norm = ein("ssd_norm", [DEPTH, 512])
        self.sc_conv_w = ein("sc_conv_w", [DEPTH, 3, 512])
        self.sc_conv_b = ein("sc_conv_b", [DEPTH, 512])
        self.attn_norm = ein("attn_norm", [DEPTH, 512])
        self.sc_norm = ein("sc_norm", [DEPTH, 512])
        self.w_out = ein("w_out", [DEPTH, D_MIX, D])
        self.ffn_norm = ein("ffn_norm", [DEPTH, D])
        self.w_up = ein("w_up", [DEPTH, D, 2 * D_FF])
        self.ffn_conv_w = ein("ffn_conv_w", [DEPTH, 3, 2 * D_FF])
        self.ffn_conv_b = ein("ffn_conv_b", [DEPTH, 2 * D_FF])
        self.w_down = ein("w_down", [DEPTH, D_FF, D])
        self.final_norm = ein("final_norm", [D])
        self.y = nc.dram_tensor("y", [S, D], F32, kind="ExternalOutput").ap()

        def scr(name, shape, dt):
            kind = "ExternalOutput" if name in self.debug else "Internal"
            return nc.dram_tensor(name, list(shape), dt, kind=kind).ap()

        self.scr = scr
        self.win_fm = scr("win_fm", [DEPTH, 32, 128, 8, 128], BF16)
        self.wz_b = scr("wz_b", [DEPTH, 128, 8, 512], BF16)
        self.wdt_b = scr("wdt_b", [DEPTH, 128, 8, 16], BF16)
        self.wout_b = scr("wout_b", [DEPTH, 128, 12, 1024], BF16)
        self.wup_fm = scr("wup_fm", [DEPTH, 44, 128, 8, 128], BF16)
        self.wdown_b = scr("wdown_b", [DEPTH, 128, 22, 1024], BF16)
        self.xa = scr("xa", [S, D], F32)
        self.xb = scr("xb", [S, D], F32)
        self.mixT = scr("mixT", [D_MIX, S], BF16)
        self.z_d = scr("z_d", [S, 512], F32)
        self.dt_d = scr("dt_d", [S, 16], F32)
        self.BT_d = scr("BT_d", [256, S], BF16)
        self.CT_d = scr("CT_d", [256, S], BF16)
        self.xtok_d = scr("xtok_d", [S, 512], BF16)
        self.Btok_d = scr("Btok_d", [S, 256], BF16)
        self.h2T_d = scr("h2T_d", [D, S], BF16)
        self.cx = Cx(nc)

    def mm(self, out, lhsT, rhs, start, stop, Rd, Wr, inc):
        self.cx.op("pe", lambda e: e.matmul(out, lhsT, rhs, start=start, stop=stop), Rd, Wr, inc=inc)

    def want(self, ph):
        ok = self.phases is None or ph in self.phases
        if ok:
            self.cx.marks.append((ph, {k: e.nins for k, e in self.cx.E.items()}))
        return ok

    def consts(self):
        cx = self.cx
        nc = self.nc
        self.dqr = Ring([cx.dsem() for _ in range(24)])
        di = cx.sb([128, 128], I32, "di")
        dF = cx.sb([128, 128], F32, "dF")
        cx.op("pool", lambda e: e.iota(di.t[:], pattern=[[-1, 128]], base=0, channel_multiplier=1), [], [di])
        cx.op("dve", lambda e: e.tensor_copy(dF.t[:], di.t[:]), [di], [dF])

        def cmpmask(name, opc, dt=F32):
            m = cx.sb([128, 128], dt, name)
            cx.op("dve", lambda e: e.tensor_scalar(m.t[:], dF.t[:], 0.0, None, op0=opc), [dF], [m])
            return m

        self.U_incl = cmpmask("U_incl", ALU.is_le)
        self.L_incl = cmpmask("L_incl", ALU.is_ge)
        self.Lstrict = cmpmask("Lstrict", ALU.is_gt)
        self.Ustrict = cmpmask("Ustrict", ALU.is_lt)
        self.ident_bf = cmpmask("ident_bf", ALU.is_equal, BF16)
        self.ident_f = cmpmask("ident_f", ALU.is_equal, F32)
        self.ones_f = cx.sb([128, 128], F32, "ones_f")
        cx.op("dve", lambda e: e.memset(self.ones_f.t[:], 1.0), [], [self.ones_f])
        self.blk64 = cx.sb([128, 128], F32, "blk64")
        cx.op("dve", lambda e: e.memset(self.blk64.t[:], 0.0), [], [self.blk64])
        cx.op("dve", lambda e: e.memset(self.blk64.t[0:64, 0:64], 1.0), [], [self.blk64])
        cx.op("dve", lambda e: e.memset(self.blk64.t[64:128, 64:128], 1.0), [], [self.blk64])
        self.eps1 = cx.sb([128, 1], F32, "eps1")
        cx.op("dve", lambda e: e.memset(self.eps1.t[:], EPS), [], [self.eps1])
        self.eps64 = cx.sb([128, 1], F32, "eps64")
        cx.op("dve", lambda e: e.memset(self.eps64.t[:], 64.0 * EPS), [], [self.eps64])
        self.W65 = cx.sb([128, 64], BF16, "W65")
        cx.op("dve", lambda e: e.memset(self.W65.t[:], 1.0), [], [self.W65])
        cx.op("dve", lambda e: e.memset(self.W65.t[64:65, :], 64.0 * EPS), [], [self.W65])
        self.blk64b = cx.sb([128, 128], BF16, "blk64b")
        cx.op("dve", lambda e: e.tensor_copy(self.blk64b.t[:], self.blk64.t[:]), [self.blk64], [self.blk64b])
        absA = cx.sb([128, 128], F32, "absA")
        absB = cx.sb([128, 128], F32, "absB")
        mA = cx.sb([128, 128], F32, "mA")
        mB = cx.sb([128, 128], F32, "mB")
        for aX, sh in ((absA, -64.0), (absB, 64.0)):
            cx.op("dve", lambda e, sh=sh: e.tensor_scalar(mA.t[:], dF.t[:], sh, None, op0=ALU.add), [dF], [mA])
            cx.op("dve", lambda e, sh=sh: e.tensor_scalar(mB.t[:], dF.t[:], -1.0, -sh, op0=ALU.mult, op1=ALU.add), [dF], [mB])
            cx.op("dve", lambda e, aX=aX: e.tensor_tensor(aX.t[:], mA.t[:], mB.t[:], ALU.max), [mA, mB], [aX])
        cx.op("dve", lambda e: e.tensor_scalar(mA.t[:], absA.t[:], 64.0, MASKV, op0=ALU.is_gt, op1=ALU.mult), [absA], [mA])
        cx.op("dve", lambda e: e.tensor_scalar(mB.t[:], absB.t[:], 64.0, MASKV, op0=ALU.is_gt, op1=ALU.mult), [absB], [mB])
        self.bias = cx.sb([128, 48, 128], BF16, "attbias")
        for h in range(8):
            slope = 2.0 ** (-8.0 * (h + 1) / 8)
            for b in range(3):
                coef = -slope * DIL[b]
                for ab, (aX, mX) in enumerate(((absA, mA), (absB, mB))):
                    idx = (h * 3 + b) * 2 + ab
                    cx.op("dve", lambda e, idx=idx, aX=aX, mX=mX, coef=coef: e.scalar_tensor_tensor(
                        self.bias.t[:, idx, :], aX.t[:], coef, mX.t[:], op0=ALU.mult, op1=ALU.add),
                        [aX, mX], [self.bias])

    def conv_setup(self, engs, qs_in, qs_out):
        cx = self.cx
        self.cv_stg = Ring([cx.sb([128, 4096], F32, "wstg") for _ in range(2)])
        self.cv_obf = Ring([cx.sb([128, 4096], BF16, "wobf") for _ in range(2)])
        self.cv_din = Ring([cx.dsem() for _ in range(2)])
        self.cv_dout = Ring([cx.dsem() for _ in range(2)])
        self.cv_engs = Ring(engs)
        self.cv_qin = Ring(qs_in)
        self.cv_qout = Ring(qs_out)

    def _cv_load(self, src, kc, nb):
        cx = self.cx
        s = self.cv_stg.next()
        n = kc * nb
        cx.dma(self.cv_qin.next(), s.t[:, 0:n].rearrange("p (k n) -> p k n", k=kc), src.rearrange("(k p) n -> p k n", p=128),
               [], [s], ds=self.cv_din.next())

        def cast(perm):
            o = self.cv_obf.next()
            en = self.cv_engs.next()
            if perm:
                nchunk = nb // 128
                ov = o.t[:, 0:n].rearrange("p (c k n) -> p c k n", c=nchunk, k=kc)
                iv = s.t[:, 0:n].rearrange("p (k c n) -> p c k n", k=kc, c=nchunk)
                for c in range(nchunk):
                    if en == "act":
                        cx.op(en, lambda e, c=c: e.copy(ov[:, c], iv[:, c]), [s], [o])
                    else:
                        cx.op(en, lambda e, c=c: e.tensor_copy(ov[:, c], iv[:, c]), [s], [o])
            else:
                if en == "act":
                    cx.op(en, lambda e: e.copy(o.t[:, 0:n], s.t[:, 0:n]), [s], [o])
                else:
                    cx.op(en, lambda e: e.tensor_copy(o.t[:, 0:n], s.t[:, 0:n]), [s], [o])
            return o, n
        return cast

    def _cv_fm(self, src, kc, nchunk, dst):
        cast = self._cv_load(src, kc, nchunk * 128)

        def fin():
            o, n = cast(True)
            self.cx.dma(self.cv_qout.next(), dst.rearrange("c p k n -> p c k n"),
                        o.t[:, 0:n].rearrange("p (c k n) -> p c k n", c=nchunk, k=kc), [o], [], ds=self.cv_dout.next())
        return fin

    def _cv_r(self, src, kc, nb, dst):
        cast = self._cv_load(src, kc, nb)

        def fin():
            o, n = cast(False)
            self.cx.dma(self.cv_qout.next(), dst, o.t[:, 0:n].rearrange("p (k n) -> p k n", k=kc), [o], [], ds=self.cv_dout.next())
        return fin

    def conv_jobs(self, li, part):
        jobs = []
        if part == "in":
            w = self.w_in
            for seg in range(8):
                c0 = FM_COLS[seg * 4]
                jobs.append(lambda seg=seg, c0=c0: self._cv_fm(w[li, :, c0:c0 + 512], 8, 4, self.win_fm[li, seg * 4:seg * 4 + 4]))
            jobs.append(lambda: self._cv_r(w[li, :, C_Z:C_Z + 512], 8, 512, self.wz_b[li]))
            jobs.append(lambda: self._cv_r(w[li, :, C_DT:C_DT + 16], 8, 16, self.wdt_b[li]))
        else:
            for j in range(4):
                jobs.append(lambda j=j: self._cv_r(self.w_out[li, :, j * 256:(j + 1) * 256], 12, 256, self.wout_b[li, :, :, j * 256:(j + 1) * 256]))
            for j in range(11):
                jobs.append(lambda j=j: self._cv_fm(self.w_up[li, :, j * 512:(j + 1) * 512], 8, 4, self.wup_fm[li, j * 4:j * 4 + 4]))
            for j in range(8):
                jobs.append(lambda j=j: self._cv_r(self.w_down[li, :, j * 128:(j + 1) * 128], 22, 128, self.wdown_b[li, :, :, j * 128:(j + 1) * 128]))
        return jobs

    def conv_tick(self):
        nxt = self.cv_jobs.pop(0)() if self.cv_jobs else None
        if self.cv_pending is not None:
            self.cv_pending()
        self.cv_pending = nxt

    def conv_flush(self):
        while self.cv_jobs or self.cv_pending is not None:
            self.conv_tick()

    def norm_setup(self):
        cx = self.cx
        self.n_tp = Ring([cx.ps([128, 8, 128], BF16, "n_tp") for _ in range(1)])
        self.gstg = Ring([cx.sb([64, 128], F32, "gstg") for _ in range(2)])

    def norm_bufs(self, nx=3, with_norm=True):
        cx = self.cx
        self.n_xt = Ring([cx.sb([128, D], F32, "n_xt") for _ in range(nx)])
        self.n_dx = Ring([cx.dsem() for _ in range(nx + 1)])
        if with_norm:
            self.n_junk = cx.sb([128, D], BF16, "n_junk")
            self.n_ss = Ring([cx.sb([128, 2], F32, "n_ss") for _ in range(3)])
            self.n_xn = Ring([cx.sb([128, D], BF16, "n_xn") for _ in range(2)])
            self.n_tp2 = Ring([self.n_tp.items[0], cx.ps([128, 8, 128], BF16, "n_tp2")])

    def _row_T(self, src_row, nchunk, dst_ap, dst_T, mult=1.0):
        cx = self.cx
        stg = self.gstg.next()
        cx.dma("sp", stg.t[0:nchunk, :], src_row.rearrange("(c p) -> c p", p=128), [], [stg], ds=self.dqr.next())
        tp = self.n_tp.next()
        pv = tp.t[:].rearrange("p a b -> p (a b)").bitcast(F32)
        self.mm(pv[:, 0:nchunk], stg.t[0:nchunk, :], self.ident_f.t[0:nchunk, 0:nchunk], True, True, [stg, self.ident_f], [tp], True)
        cx.op("dve", lambda e: e.tensor_scalar(dst_ap, pv[:, 0:nchunk], mult, None, op0=ALU.mult), [tp], [dst_T])

    def load_gT(self, src_row, nchunk, name, mult=1.0):
        g = self.cx.sb([128, nchunk], F32, name)
        self._row_T(src_row, nchunk, g.t[:, :], g, mult)
        return g

    def rstd_of(self, x_ap, xT, ss, n):
        cx = self.cx
        cx.op("dve", lambda e: e.scalar_tensor_tensor(self.n_junk.t[:, 0:n], x_ap, 1.0, x_ap, op0=ALU.mult, op1=ALU.mult,
                                                      accum_out=ss.t[:, 0:1]), [xT], [self.n_junk, ss])
        cx.op("act", lambda e: e.activation(ss.t[:, 1:2], ss.t[:, 0:1], AF.Ln, bias=self.eps1.t[:, 0:1], scale=1.0 / n), [ss, self.eps1], [ss])
        cx.op("act", lambda e: e.activation(ss.t[:, 1:2], ss.t[:, 1:2], AF.Exp, scale=-0.5), [ss], [ss])

    def norm_tile_a(self, xt):
        cx = self.cx
        ss = self.n_ss.next()
        xn = self.n_xn.next()
        tp = self.n_tp2.next()
        self.rstd_of(xt.t[:], xt, ss, D)
        cx.op("act", lambda e: e.activation(xn.t[:], xt.t[:], AF.Copy, scale=ss.t[:, 1:2]), [xt, ss], [xn])
        for j in range(8):
            cx.op("pe", lambda e, j=j: e.transpose(tp.t[:, j, :], xn.t[:, j * 128:(j + 1) * 128], self.ident_bf.t[:]),
                  [xn, self.ident_bf], [tp], inc=(j == 7))
        return tp

    def norm_tile_b(self, tp, gT, out_ap, out_R):
        self.cx.op("dve", lambda e: e.tensor_tensor(out_ap, tp.t[:], gT.t[:, :].unsqueeze(2).to_broadcast([128, 8, 128]), ALU.mult),
                   [tp, gT], [out_R])

    def phase_norm(self, x_src, gT, hT):
        cx = self.cx
        cx.open_scope()
        self.norm_bufs()
        prev = None
        for tt in range(NT):
            xt = self.n_xt.next()
            cx.dma("sp", xt.t[:], x_src[tt * 128:(tt + 1) * 128, :], [], [xt], ds=self.n_dx.next())
            tp = self.norm_tile_a(xt)
            if prev is not None:
                self.norm_tile_b(prev[0], gT, hT.t[:, :, PADH + prev[1] * 128:PADH + (prev[1] + 1) * 128], hT)
            prev = (tp, tt)
            if tt % 3 == 0:
                self.conv_tick()
        self.norm_tile_b(prev[0], gT, hT.t[:, :, PADH + prev[1] * 128:PADH + (prev[1] + 1) * 128], hT)
        cx.close_scope()

    def phase_attn(self, li, hT):
        cx = self.cx
        cx.open_scope()
        qT = cx.sb([128, S], BF16, "qT")
        kTs = [cx.sb([128, S + 2 * PADK], BF16, "kT0"), cx.sb([128, S + 2 * PADK], BF16, "kT1")]
        vT = cx.sb([128, S + 2 * PADK], BF16, "vT")
        NV = 117
        V = cx.sb([128, NV, 2, 65], BF16, "Vaug")
        acc = cx.sb([65, 1, S], F32, "acc")
        wq = cx.sb([128, 8, 128], BF16, "wq")
        wk = cx.sb([128, 8, 128], BF16, "wk")
        wv = cx.sb([128, 8, 128], BF16, "wv")
        dw = [cx.dsem() for _ in range(3)]
        pT = Ring([cx.sb([128, 8, 128], BF16, "pT") for _ in range(2)])
        nrm_a = Ring([cx.sb([64, 512], F32, "nrm_a") for _ in range(2)])
        nrm_b = Ring([cx.sb([64, 512], F32, "nrm_b") for _ in range(2)])
        nrm_c = Ring([cx.sb([65, 512], BF16, "nrm_c") for _ in range(2)])
        nrm_o = Ring([cx.sb([64, 512], BF16, "nrm_o") for _ in range(2)])
        d_o = Ring([cx.dsem() for _ in range(2)])
        banks7 = [cx.ps([128, 512], F32, "att_ps") for _ in range(7)]
        ps_proj = Ring(banks7)
        ps_s = Ring(banks7[0:4])
        ps_o = Ring(banks7[4:7])
        cx.op("pool", lambda e: e.memset(kTs[0].t[64:128, :], 0.0), [], [kTs[0]])
        cx.op("pool", lambda e: e.memset(kTs[1].t[0:64, :], 0.0), [], [kTs[1]])
        cx.op("pool", lambda e: e.memset(kTs[0].t[0:64, 0:PADK], 0.0), [], [kTs[0]])
        cx.op("pool", lambda e: e.memset(kTs[0].t[0:64, PADK + S:], 0.0), [], [kTs[0]])
        cx.op("pool", lambda e: e.memset(kTs[1].t[64:128, 0:PADK], 0.0), [], [kTs[1]])
        cx.op("pool", lambda e: e.memset(kTs[1].t[64:128, PADK + S:], 0.0), [], [kTs[1]])
        cx.op("pool", lambda e: e.memset(vT.t[:, 0:PADK], 0.0), [], [vT])
        cx.op("pool", lambda e: e.memset(vT.t[:, PADK + S:], 0.0), [], [vT])
        cx.op("pool", lambda e: e.memset(V.t[:, :, :, 64:65], 1.0), [], [V])
        voff = []
        o = 0
        for b in range(3):
            voff.append(o)
            o += DIL[b] * (S // DIL[b] // 128 + 1)
        assert o == NV
        for b in range(3):
            d = DIL[b]
            ntq = S // d // 128
            for c in range(d):
                i0 = voff[b] + c * (ntq + 1)
                cx.op("pool", lambda e, i0=i0: e.memset(V.t[0:64, i0, :, 64:65], 0.0), [], [V])
                cx.op("pool", lambda e, i1=i0 + ntq: e.memset(V.t[64:128, i1, :, 64:65], 0.0), [], [V])

        def vps(ps):
            return ps.t[:].rearrange("p (a b) -> p a b", a=4)

        evac = Ring(["act", "dve"])
        for hp in range(4):
            for wt, fm0, ds in ((wq, FM_Q, dw[0]), (wk, FM_K, dw[1]), (wv, FM_V, dw[2])):
                cx.dma("sp", wt.t[:], self.win_fm[li, fm0 + hp], [], [wt], ds=ds)
            for which, wt in enumerate((wq, wk, wv)):
                for tb in range(8):
                    ps = ps_proj.next()
                    for kc in range(8):
                        self.mm(ps.t[:], wt.t[:, kc, :], hT.t[:, kc, PADH + tb * 512:PADH + (tb + 1) * 512],
                                kc == 0, kc == 7, [wt, hT], [ps], kc == 7)
                    en = evac.next()
                    if which == 0:
                        outs = [(qT, qT.t[:, tb * 512:(tb + 1) * 512], ps.t[:], 0.125)]
                    elif which == 1:
                        cs_ = slice(PADK + tb * 512, PADK + (tb + 1) * 512)
                        outs = [(kTs[0], kTs[0].t[0:64, cs_], ps.t[0:64, :], 1.0), (kTs[1], kTs[1].t[64:128, cs_], ps.t[64:128, :], 1.0)]
                    else:
                        outs = [(vT, vT.t[:, PADK + tb * 512:PADK + (tb + 1) * 512], ps.t[:], 1.0)]
                    for (dstT, dap, sap, scale) in outs:
                        if en == "act":
                            cx.op("act", lambda e, dap=dap, sap=sap, scale=scale: e.activation(dap, sap, AF.Copy, scale=scale), [ps], [dstT])
                        else:
                            cx.op("dve", lambda e, dap=dap, sap=sap, scale=scale: e.tensor_scalar(dap, sap, scale, None, op0=ALU.mult), [ps], [dstT])
            for b in range(3):
                d = DIL[b]
                ntq = S // d // 128
                for c in range(d):
                    m = 0
                    while m < ntq + 1:
                        g = min(4, ntq + 1 - m)
                        ps = ps_proj.next()
                        pv = ps.t[:].bitcast(BF16)
                        for j in range(g):
                            st = PADK + d * (128 * (m + j) - 64) + c
                            cx.op("pe", lambda e, j=j, st=st, d=d, pv=pv: e.transpose(
                                pv[:, j * 128:(j + 1) * 128], vT.t[:, sl(st, 128, d)], self.ident_bf.t[:]),
                                [vT, self.ident_bf], [ps], inc=(j == g - 1))
                        i0 = voff[b] + c * (ntq + 1) + m
                        en = evac.next()
                        src = pv[:, 0:g * 128].rearrange("p (g h f) -> p g h f", g=g, h=2)
                        dstap = V.t[:, i0:i0 + g, :, 0:64]
                        if en == "act":
                            cx.op("act", lambda e, src=src, dstap=dstap: e.copy(dstap, src), [ps], [V])
                        else:
                            cx.op("dve", lambda e, src=src, dstap=dstap: e.tensor_copy(dstap, src), [ps], [V])
                        m += g
            for hh in range(2):
                h = hp * 2 + hh
                kT = kTs[hh]
                groups = []
                for b in range(3):
                    d = DIL[b]
                    ntq = S // d // 128
                    G = min(4, ntq)
                    for c in range(d):
                        for j0 in range(0, ntq, G):
                            groups.append((b, d, ntq, G, c, j0))

                def emit_S(grp):
                    b, d, ntq, G, c, j0 = grp
                    banks = [ps_s.next() for _ in range((2 * G + 3) // 4)]
                    for jl in range(G):
                        j = j0 + jl
                        for ab in range(2):
                            slot = jl * 2 + ab
                            bank = banks[slot // 4]
                            ks = 128 * j - 64 + 128 * ab
                            kc0 = PADK + d * ks + c
                            qc0 = d * 128 * j + c
                            oap = vps(bank)[:, slot % 4, :]
                            self.mm(oap, kT.t[:, sl(kc0, 128, d)], qT.t[:, sl(qc0, 128, d)],
                                    slot % 4 == 0, False, [kT, qT], [bank], False)
                            if slot % 4 == 3:
                                i0 = (h * 3 + b) * 2
                                self.mm(bank.t[:], self.ident_bf.t[:],
                                        self.bias.t[:, i0:i0 + 2, :].unsqueeze(1).to_broadcast([128, 2, 2, 128]),
                                        False, True, [self.ident_bf, self.bias], [bank], True)
                    return banks

                def emit_rest(grp, banks):
                    b, d, ntq, G, c, j0 = grp
                    pt = pT.next()
                    for bi, bank in enumerate(banks):
                        ns = min(4, 2 * G - bi * 4)
                        cx.op("act", lambda e, bank=bank, bi=bi, ns=ns, pt=pt: e.activation(
                            pt.t[:, bi * 4:bi * 4 + ns, :], vps(bank)[:, 0:ns, :], AF.Exp), [bank], [pt])
                    po = ps_o.next()
                    for jl in range(G):
                        j = j0 + jl
                        for ab in range(2):
                            vi = voff[b] + c * (ntq + 1) + j + ab
                            self.mm(po.t[0:65, jl * 128:(jl + 1) * 128], V.t[:, vi, hh, :], pt.t[:, jl * 2 + ab, :],
                                    ab == 0, ab == 1, [V, pt], [po], (jl == G - 1 and ab == 1))
                    t0 = d * 128 * j0 + c
                    aap = acc.t[0:65, 0, sl(t0, G * 128, d)]
                    if b == 0:
                        cx.op("dve", lambda e, aap=aap, po=po, G=G: e.tensor_copy(aap, po.t[0:65, 0:G * 128]), [po], [acc])
                    else:
                        cx.op("dve", lambda e, aap=aap, po=po, G=G: e.tensor_tensor(aap, po.t[0:65, 0:G * 128], aap, ALU.add),
                              [po, acc], [acc])

                prev = None
                for grp in groups:
                    bk = emit_S(grp)
                    if prev is not None:
                        emit_rest(*prev)
                    prev = (grp, bk)
                emit_rest(*prev)
                def n1(tb):
                    cs = slice(tb * 512, (tb + 1) * 512)
                    rc = nrm_c.next()
                    cx.op("dve", lambda e: e.tensor_tensor(rc.t[:], acc.t[0:65, 0, cs], acc.t[0:65, 0, cs], ALU.mult), [acc], [rc])
                    ps2 = ps_proj.next()
                    self.mm(ps2.t[0:64, :], self.W65.t[0:65, :], rc.t[:], True, True, [self.W65, rc], [ps2], True)
                    return (cs, ps2)

                def n2(st, hh=hh, hp=hp):
                    cs, ps2 = st
                    ra = nrm_a.next()
                    cx.op("act", lambda e: e.activation(ra.t[:], ps2.t[0:64, :], AF.Ln), [ps2], [ra])
                    ra2 = nrm_b.next()
                    cx.op("act", lambda e: e.activation(ra2.t[:], ra.t[:], AF.Exp, scale=-0.5), [ra], [ra2])
                    ro = nrm_o.next()
                    cx.op("dve", lambda e: e.scalar_tensor_tensor(
                        ro.t[:], acc.t[0:64, 0, cs], self.g8c.t[:, hp * 2 + hh:hp * 2 + hh + 1], ra2.t[:], op0=ALU.mult, op1=ALU.mult), [acc, ra2, self.g8c], [ro])
                    row0 = (hp * 2 + hh) * 64
                    cx.dma("act", self.mixT[row0:row0 + 64, cs], ro.t[:], [ro], [], ds=d_o.next())

                pv_ = None
                for tb in range(8):
                    cur_ = n1(tb)
                    if pv_ is not None:
                        n2(pv_)
                    pv_ = cur_
                n2(pv_)
        cx.close_scope()

    def load_bcast(self, src_row, n, name):
        cx = self.cx
        t = cx.sb([128, n], F32, name)
        cx.dma("sp", t.t[:, :], src_row.unsqueeze(0).partition_broadcast(128)[:, 0, :], [], [t], ds=self.dqr.next())
        return t

    def load_cw(self, src, k, nchunk, name):
        t = self.cx.sb([128, nchunk, k], F32, name)
        for kk in range(k):
            self._row_T(src[kk, :], nchunk, t.t[:, :, kk], t)
        return t

    def phase_zdt(self, li, hT):
        cx = self.cx
        cx.open_scope()
        wz = cx.sb([128, 8, 512], BF16, "wz")
        wdt = cx.sb([128, 8, 16], BF16, "wdt")
        cx.dma("sp", wz.t[:], self.wz_b[li], [], [wz], ds=cx.dsem())
        cx.dma("sp", wdt.t[:], self.wdt_b[li], [], [wdt], ds=cx.dsem())
        zs = Ring([cx.sb([128, 512], F32, "zs") for _ in range(2)])
        dts = Ring([cx.sb([128, 16], F32, "dts") for _ in range(2)])
        dz = Ring([cx.dsem() for _ in range(2)])
        dd = Ring([cx.dsem() for _ in range(2)])
        psz = Ring([cx.ps([128, 512], F32, "psz") for _ in range(2)])
        psd = Ring([cx.ps([128, 512], F32, "psd") for _ in range(2)])
        for tt in range(NT):
            pz = psz.next()
            pd = psd.next()
            tok = slice(PADH + tt * 128, PADH + (tt + 1) * 128)
            for kc in range(8):
                self.mm(pz.t[:], hT.t[:, kc, tok], wz.t[:, kc, :], kc == 0, kc == 7, [hT, wz], [pz], kc == 7)
            for kc in range(8):
                self.mm(pd.t[:, 0:16], hT.t[:, kc, tok], wdt.t[:, kc, :], kc == 0, kc == 7, [hT, wdt], [pd], kc == 7)
            z = zs.next()
            cx.op("act", lambda e, z=z, pz=pz: e.copy(z.t[:], pz.t[:]), [pz], [z])
            cx.dma("act", self.z_d[tt * 128:(tt + 1) * 128, :], z.t[:], [z], [], ds=dz.next())
            dt = dts.next()
            cx.op("dve", lambda e, dt=dt, pd=pd: e.tensor_copy(dt.t[:], pd.t[:, 0:16]), [pd], [dt])
            cx.dma("act", self.dt_d[tt * 128:(tt + 1) * 128, :], dt.t[:], [dt], [], ds=dd.next())
        cx.close_scope()

    def phase_xbc(self, li, hT):
        cx = self.cx
        cx.open_scope()
        cw = self.load_cw(self.ssd_conv_w[li], 5, 8, "xbc_cw")
        cb = self.load_gT(self.ssd_conv_b[li], 8, "xbc_cb")
        wts = Ring([cx.sb([128, 8, 128], BF16, "xbc_w") for _ in range(2)])
        dws = Ring([cx.dsem() for _ in range(2)])
        rows = Ring([cx.sb([128, S], BF16, "xbc_row") for _ in range(2)])
        drow = Ring([cx.dsem() for _ in range(2)])
        accs = Ring([cx.sb([128, 512], F32, "xbc_acc") for _ in range(2)])
        toks = Ring([cx.sb([128, 32, 128], BF16, "xbc_tok") for _ in range(2)])
        dtok = Ring([cx.dsem() for _ in range(2)])
        pss = Ring([cx.ps([128, 512], F32, "xbc_ps") for _ in range(3)])
        pst = Ring([cx.ps([128, 4, 128], BF16, "xbc_pst") for _ in range(2)])
        W = 508
        for fc in range(8):
            wt = wts.next()
            cx.dma("sp", wt.t[:], self.win_fm[li, FM_XBC + fc], [], [wt], ds=dws.next())
            row = rows.next()
            for t0 in range(0, S, W):
                w = min(W, S - t0)
                n = w + 4
                ps = pss.next()
                for kc in range(8):
                    self.mm(ps.t[:, 0:n], wt.t[:, kc, :], hT.t[:, kc, PADH + t0 - 2:PADH + t0 - 2 + n], kc == 0, kc == 7, [wt, hT], [ps], kc == 7)
                acc = accs.next()
                cx.op("act", lambda e, acc=acc, ps=ps, w=w, fc=fc: e.activation(acc.t[:, 0:w], ps.t[:, 2:2 + w], AF.Identity,
                      bias=cb.t[:, fc:fc + 1], scale=cw.t[:, fc, 2:3]), [ps, cb, cw], [acc])
                for k in (0, 1, 3, 4):
                    cx.op("dve", lambda e, acc=acc, ps=ps, w=w, fc=fc, k=k: e.scalar_tensor_tensor(
                        acc.t[:, 0:w], ps.t[:, k:k + w], cw.t[:, fc, k:k + 1], acc.t[:, 0:w], op0=ALU.mult, op1=ALU.add), [ps, cw, acc], [acc])
                cx.op("act", lambda e, acc=acc, row=row, t0=t0, w=w: e.activation(row.t[:, t0:t0 + w], acc.t[:, 0:w], AF.Silu), [acc], [row])
            if fc >= 4:
                dst = self.BT_d if fc < 6 else self.CT_d
                r0 = ((fc - 4) % 2) * 128
                cx.dma("act", dst[r0:r0 + 128, :], row.t[:], [row], [], ds=drow.next())
            if fc < 6:
                tokb = toks.next()
                for tq in range(8):
                    pt = pst.next()
                    for j in range(4):
                        tt = tq * 4 + j
                        cx.op("pe", lambda e, pt=pt, j=j, tt=tt, row=row: e.transpose(pt.t[:, j, :], row.t[:, tt * 128:(tt + 1) * 128], self.ident_bf.t[:]),
                              [row, self.ident_bf], [pt], inc=(j == 3))
                    cx.op("act", lambda e, pt=pt, tokb=tokb, tq=tq: e.copy(tokb.t[:, tq * 4:tq * 4 + 4, :], pt.t[:]), [pt], [tokb])
                if fc < 4:
                    dst = self.xtok_d[:, fc * 128:(fc + 1) * 128]
                else:
                    dst = self.Btok_d[:, (fc - 4) * 128:(fc - 3) * 128]
                dv = dst.rearrange("(t p) f -> p t f", p=128)
                dk = dtok.next()
                for q4 in range(4):
                    cx.dma("act", dv[:, q4 * 8:(q4 + 1) * 8, :], tokb.t[:, q4 * 8:(q4 + 1) * 8, :], [tokb], [], ds=dk)
        cx.close_scope()

    def phase_sc(self, li, hT):
        cx = self.cx
        cx.open_scope()
        cw = self.load_cw(self.sc_conv_w[li], 3, 4, "sc_cw")
        cb = self.load_gT(self.sc_conv_b[li], 4, "sc_cb")
        g8 = self.load_gT(self.sc_norm[li], 4, "sc_g8", mult=8.0)
        wts = [Ring([cx.sb([128, 8, 128], BF16, "sc_w") for _ in range(2)]) for _ in range(3)]
        dws = [Ring([cx.dsem() for _ in range(2)]) for _ in range(3)]
        pss = [Ring([cx.ps([128, 512], F32, "sc_ps") for _ in range(2)]) for _ in range(3)]
        psn = cx.ps([128, 512], F32, "sc_psn")
        gcs = Ring([cx.sb([128, 512], F32, "sc_gcs") for _ in range(2)])
        tts = Ring([cx.sb([128, 512], F32, "sc_tt") for _ in range(2)])
        accs = Ring([cx.sb([128, 512], F32, "sc_acc") for _ in range(2)])
        yvs = Ring([cx.sb([128, 512], F32, "sc_yv") for _ in range(2)])
        ysq = Ring([cx.sb([128, 512], BF16, "sc_ysq") for _ in range(2)])
        rrs = Ring([cx.sb([128, 512], F32, "sc_rr") for _ in range(2)])
        outs = Ring([cx.sb([128, 512], BF16, "sc_out") for _ in range(2)])
        douts = Ring([cx.dsem() for _ in range(2)])
        W = 510
        for c4 in range(4):
            ws = []
            for i, fm0 in enumerate((FM_GB, FM_GC, FM_HC)):
                wt = wts[i].next()
                cx.dma("sp", wt.t[:], self.win_fm[li, fm0 + c4], [], [wt], ds=dws[i].next())
                ws.append(wt)
            def sc_proj(t0, ws=ws):
                w = min(W, S - t0)
                n = w + 2
                pp = []
                for i in range(3):
                    ps = pss[i].next()
                    for kc in range(8):
                        self.mm(ps.t[:, 0:n], ws[i].t[:, kc, :], hT.t[:, kc, PADH + t0 - 1:PADH + t0 - 1 + n], kc == 0, kc == 7, [ws[i], hT], [ps], kc == 7)
                    pp.append(ps)
                return (t0, w, n, pp)

            def sc_rest(st, c4=c4):
                t0, w, n, pp = st
                pgb, pgc, phc = pp
                gc = gcs.next()
                cx.op("act", lambda e, gc=gc, pgc=pgc, n=n: e.copy(gc.t[:, 0:n], pgc.t[:, 0:n]), [pgc], [gc])
                tt = tts.next()
                cx.op("dve", lambda e, tt=tt, phc=phc, gc=gc, n=n: e.tensor_tensor(tt.t[:, 0:n], phc.t[:, 0:n], gc.t[:, 0:n], ALU.mult), [phc, gc], [tt])
                acc = accs.next()
                cx.op("act", lambda e, acc=acc, tt=tt, w=w, c4=c4: e.activation(acc.t[:, 0:w], tt.t[:, 1:1 + w], AF.Identity,
                      bias=cb.t[:, c4:c4 + 1], scale=cw.t[:, c4, 1:2]), [tt, cb, cw], [acc])
                for k in (0, 2):
                    cx.op("dve", lambda e, acc=acc, tt=tt, w=w, c4=c4, k=k: e.scalar_tensor_tensor(
                        acc.t[:, 0:w], tt.t[:, k:k + w], cw.t[:, c4, k:k + 1], acc.t[:, 0:w], op0=ALU.mult, op1=ALU.add), [tt, cw, acc], [acc])
                yv = yvs.next()
                cx.op("dve", lambda e, yv=yv, pgb=pgb, acc=acc, w=w: e.tensor_tensor(yv.t[:, 0:w], pgb.t[:, 1:1 + w], acc.t[:, 0:w], ALU.mult), [pgb, acc], [yv])
                yq = ysq.next()
                cx.op("dve", lambda e, yq=yq, yv=yv, w=w: e.tensor_tensor(yq.t[:, 0:w], yv.t[:, 0:w], yv.t[:, 0:w], ALU.mult), [yv], [yq])
                self.mm(psn.t[:, 0:w], self.blk64b.t[:], yq.t[:, 0:w], True, True, [self.blk64b, yq], [psn], True)
                rr = rrs.next()
                cx.op("act", lambda e, rr=rr, w=w: e.activation(rr.t[:, 0:w], psn.t[:, 0:w], AF.Ln, bias=self.eps64.t[:, 0:1]), [psn, self.eps64], [rr])
                cx.op("act", lambda e, rr=rr, w=w: e.activation(rr.t[:, 0:w], rr.t[:, 0:w], AF.Exp, scale=-0.5), [rr], [rr])
                ob = outs.next()
                cx.op("dve", lambda e, ob=ob, yv=yv, rr=rr, w=w, c4=c4: e.scalar_tensor_tensor(
                    ob.t[:, 0:w], yv.t[:, 0:w], g8.t[:, c4:c4 + 1], rr.t[:, 0:w], op0=ALU.mult, op1=ALU.mult), [yv, g8, rr], [ob])
                cx.dma("act", self.mixT[1024 + c4 * 128:1024 + (c4 + 1) * 128, t0:t0 + w], ob.t[:, 0:w], [ob], [], ds=douts.next())

            prev = None
            for t0 in range(0, S, W):
                cur = sc_proj(t0)
                if prev is not None:
                    sc_rest(prev)
                prev = cur
            sc_rest(prev)
        cx.close_scope()

    def phase_ssd(self, li):
        cx = self.cx
        cx.open_scope()
        bias16 = self.load_bcast(self.ssd_dt_bias[li], 16, "ssd_bias16")
        a16 = self.load_bcast(self.ssd_a_log[li], 16, "ssd_a16")
        cx.op("act", lambda e: e.activation(a16.t[:], a16.t[:], AF.Exp), [a16], [a16])
        cx.op("dve", lambda e: e.tensor_scalar(a16.t[:], a16.t[:], -1.0, None, op0=ALU.mult), [a16], [a16])
        d8 = self.load_bcast(self.ssd_d[li], 8, "ssd_d8")
        Dfull = cx.sb([128, 8, 64], F32, "ssd_Dfull")
        cx.op("dve", lambda e: e.tensor_copy(Dfull.t[:], d8.t[:, :].unsqueeze(2).to_broadcast([128, 8, 64])), [d8], [Dfull])
        gS = self.load_gT(self.ssd_norm[li], 4, "ssd_gS")
        prevB = cx.sb([128, NT, 512], BF16, "ssd_prevB")
        state_f = cx.sb([128, 512], F32, "ssd_state_f")
        state_b = cx.sb([128, 512], F32, "ssd_state_b")
        stf_bf = cx.sb([128, 512], BF16, "ssd_stf_bf")
        cx.op("pool", lambda e: e.memset(state_f.t[:], 0.0), [], [state_f])
        cx.op("pool", lambda e: e.memset(state_b.t[:], 0.0), [], [state_b])
        cx.op("pool", lambda e: e.memset(stf_bf.t[:], 0.0), [], [stf_bf])
        dtrs = Ring([cx.sb([128, 16], F32, "ssd_dtr") for _ in range(4)])
        xts = Ring([cx.sb([128, 512], BF16, "ssd_xt") for _ in range(4)])
        bts = Ring([cx.sb([128, 256], BF16, "ssd_bt") for _ in range(4)])
        BTs = Ring([cx.sb([128, 2, 128], BF16, "ssd_BT") for _ in range(4)])
        CTs = Ring([cx.sb([128, 2, 128], BF16, "ssd_CT") for _ in range(4)])
        zts = Ring([cx.sb([128, 512], F32, "ssd_zt") for _ in range(4)])
        dl = [Ring([cx.dsem() for _ in range(5)]) for _ in range(6)]
        t16 = Ring([cx.sb([128, 16], F32, "ssd_t16") for _ in range(8)])
        dts_ = Ring([cx.sb([128, 16], F32, "ssd_dt") for _ in range(4)])
        acs = Ring([cx.sb([128, 16], F32, "ssd_ac") for _ in range(4)])
        Es = Ring([cx.sb([128, 32], F32, "ssd_E") for _ in range(4)])
        wdts = Ring([cx.sb([128, 8], F32, "ssd_wdt") for _ in range(4)])
        xdtf = Ring([cx.sb([128, 512], BF16, "ssd_xdtf") for _ in range(2)])
        xdtb = Ring([cx.sb([128, 512], BF16, "ssd_xdtb") for _ in range(2)])
        xwf = Ring([cx.sb([128, 512], BF16, "ssd_xwf") for _ in range(2)])
        xDs = Ring([cx.sb([128, 512], BF16, "ssd_xD") for _ in range(2)])
        Gmf = Ring([cx.sb([128, 2, 128], F32, "ssd_Gmf") for _ in range(2)])
        Gmb = Ring([cx.sb([128, 2, 128], F32, "ssd_Gmb") for _ in range(2)])
        lhss = Ring([cx.sb([128, 128], F32, "ssd_lhs") for _ in range(32)])
        expds = Ring([cx.sb([128, 4, 128], F32, "ssd_expd") for _ in range(2)])
        MTs = [Ring([cx.sb([128, 8, 128], BF16, "ssd_MT") for _ in range(2)]) for _ in range(2)]
        y1s = Ring([cx.sb([128, 512], F32, "ssd_y1") for _ in range(2)])
        tmps = Ring([cx.sb([128, 512], F32, "ssd_tmp") for _ in range(2)])
        szs = Ring([cx.sb([128, 512], F32, "ssd_sz") for _ in range(2)])
        ss2 = Ring([cx.sb([128, 4], F32, "ssd_ss2") for _ in range(2)])
        yns = Ring([cx.sb([128, 512], BF16, "ssd_yn") for _ in range(2)])
        sTs = Ring([cx.sb([128, 4, 512], BF16, "ssd_sT") for _ in range(2)])
        dsT = Ring([cx.dsem() for _ in range(2)])
        junk = cx.sb([128, 256], F32, "ssd_junk")
        psA = cx.ps([128, 512], F32, "ssd_psA")
        RpsG = psA.r
        RpsS = psA.r
        psS = psA.t
        diffs = Ring([cx.ps([128, 4, 128], F32, "ssd_diff") for _ in range(3)])
        psy = cx.ps([128, 512], F32, "ssd_psy")
        psyo1 = cx.ps([128, 512], F32, "ssd_psyo")
        psyo = [psyo1, psyo1]
        pscs = cx.ps([128, 512], F32, "ssd_pscs")
        if self.phases is None or "conv" in self.phases:
            self.conv_setup(["act"], ["sp"], ["act"])
            self.cv_jobs = self.conv_jobs(li, "rest") + (self.conv_jobs(li + 1, "in") if li + 1 < self.layers else [])
        BTv = self.BT_d.rearrange("(g n) t -> n g t", g=2)
        CTv = self.CT_d.rearrange("(g n) t -> n g t", g=2)

        def bc8(ap8):
            return ap8.unsqueeze(2).to_broadcast([128, 8, 64])

        def v3(ap):
            return ap.rearrange("p (h f) -> p h f", h=8)

        def softplus(dst, src_ap, bias_ap, n, Rsrc):
            ta = t16.next()
            tb = t16.next()
            cx.op("dve", lambda e: e.tensor_tensor(ta.t[:, 0:n], src_ap, bias_ap, ALU.add), [Rsrc, bias16], [ta])
            cx.op("act", lambda e: e.activation(tb.t[:, 0:n], ta.t[:, 0:n], AF.Exp), [ta], [tb])
            cx.op("act", lambda e: e.activation(dst, tb.t[:, 0:n], AF.Ln, bias=1.0), [tb], [])

        for c in range(NT - 1, -1, -1):
            cx.op("act", lambda e, c=c: e.copy(prevB.t[:, c, :], state_b.t[:]), [state_b], [prevB])
            if c == 0:
                break
            if c % 2 == 0:
                self.conv_tick()
            tok = slice(c * 128, (c + 1) * 128)
            dtr = dtrs.next()
            cx.dma("sp", dtr.t[:], self.dt_d[tok, :], [], [dtr], ds=dl[0].next())
            xt = xts.next()
            cx.dma("sp", xt.t[:], self.xtok_d[tok, :], [], [xt], ds=dl[1].next())
            bt = bts.next()
            cx.dma("sp", bt.t[:], self.Btok_d[tok, :], [], [bt], ds=dl[2].next())
            dt = dts_.next()
            ta = t16.next()
            tb = t16.next()
            cx.op("dve", lambda e, ta=ta, dtr=dtr: e.tensor_tensor(ta.t[:, 0:8], dtr.t[:, 8:16], bias16.t[:, 8:16], ALU.add), [dtr, bias16], [ta])
            cx.op("act", lambda e, ta=ta, tb=tb: e.activation(tb.t[:, 0:8], ta.t[:, 0:8], AF.Exp), [ta], [tb])
            cx.op("act", lambda e, dt=dt, tb=tb: e.activation(dt.t[:, 0:8], tb.t[:, 0:8], AF.Ln, bias=1.0), [tb], [dt])
            ac = acs.next()
            cx.op("dve", lambda e, ac=ac, dt=dt: e.tensor_tensor(ac.t[:, 0:8], dt.t[:, 0:8], a16.t[:, 8:16], ALU.mult), [dt, a16], [ac])
            self.mm(psS[:, 0:8], self.Ustrict.t[:], ac.t[:, 0:8], True, True, [self.Ustrict, ac], [RpsS], False)
            self.mm(psS[:, 8:16], self.ones_f.t[:], ac.t[:, 0:8], True, True, [self.ones_f, ac], [RpsS], True)
            E = Es.next()
            cx.op("act", lambda e, E=E: e.activation(E.t[:, 0:16], psS[:, 0:16], AF.Exp), [RpsS], [E])
            wdt = wdts.next()
            cx.op("dve", lambda e, wdt=wdt, dt=dt, E=E: e.tensor_tensor(wdt.t[:], dt.t[:, 0:8], E.t[:, 0:8], ALU.mult), [dt, E], [wdt])
            xw = xwf.next()
            cx.op("dve", lambda e, xw=xw, xt=xt, wdt=wdt: e.tensor_tensor(v3(xw.t[:]), v3(xt.t[:]), bc8(wdt.t[:, :]), ALU.mult), [xt, wdt], [xw])
            for g in range(2):
                self.mm(pscs.t[:, g * 256:(g + 1) * 256], bt.t[:, g * 128:(g + 1) * 128], xw.t[:, g * 256:(g + 1) * 256], True, True, [bt, xw], [pscs], g == 1)
            cx.op("dve", lambda e, E=E: e.tensor_tensor(v3(state_b.t[:]), v3(state_b.t[:]), bc8(E.t[:, 8:16]), ALU.mult), [state_b, E], [state_b])
            cx.op("dve", lambda e: e.tensor_tensor(state_b.t[:], state_b.t[:], pscs.t[:], ALU.add), [state_b, pscs], [state_b])

        lhs_eng = Ring(["act", "dve"])
        sT_box = [None]

        def stageA0(c):
            tok = slice(c * 128, (c + 1) * 128)
            dtr = dtrs.next()
            cx.dma("sp", dtr.t[:], self.dt_d[tok, :], [], [dtr], ds=dl[0].next())
            xt = xts.next()
            cx.dma("sp", xt.t[:], self.xtok_d[tok, :], [], [xt], ds=dl[1].next())
            bt = bts.next()
            cx.dma("sp", bt.t[:], self.Btok_d[tok, :], [], [bt], ds=dl[2].next())
            BTc = BTs.next()
            cx.dma("sp", BTc.t[:], BTv[:, :, tok], [], [BTc], ds=dl[3].next())
            CTc = CTs.next()
            cx.dma("sp", CTc.t[:], CTv[:, :, tok], [], [CTc], ds=dl[4].next())
            zt = zts.next()
            cx.dma("sp", zt.t[:], self.z_d[tok, :], [], [zt], ds=dl[5].next())
            dt = dts_.next()
            ta = t16.next()
            tb = t16.next()
            cx.op("dve", lambda e, ta=ta, dtr=dtr: e.tensor_tensor(ta.t[:], dtr.t[:], bias16.t[:], ALU.add), [dtr, bias16], [ta])
            cx.op("act", lambda e, ta=ta, tb=tb: e.activation(tb.t[:], ta.t[:], AF.Exp), [ta], [tb])
            cx.op("act", lambda e, dt=dt, tb=tb: e.activation(dt.t[:], tb.t[:], AF.Ln, bias=1.0), [tb], [dt])
            ac = acs.next()
            cx.op("dve", lambda e, ac=ac, dt=dt: e.tensor_tensor(ac.t[:], dt.t[:], a16.t[:], ALU.mult), [dt, a16], [ac])
            self.mm(psS[:, 0:8], self.U_incl.t[:], ac.t[:, 0:8], True, True, [self.U_incl, ac], [RpsS], False)
            self.mm(psS[:, 8:16], self.L_incl.t[:], ac.t[:, 8:16], True, True, [self.L_incl, ac], [RpsS], False)
            self.mm(psS[:, 16:24], self.Lstrict.t[:], ac.t[:, 0:8], True, True, [self.Lstrict, ac], [RpsS], False)
            self.mm(psS[:, 24:32], self.ones_f.t[:], ac.t[:, 0:8], True, True, [self.ones_f, ac], [RpsS], True)
            E = Es.next()
            cx.op("act", lambda e, E=E: e.activation(E.t[:, 0:32], psS[:, 0:32], AF.Exp), [RpsS], [E])
            wdt = wdts.next()
            cx.op("dve", lambda e, wdt=wdt, dt=dt, E=E: e.tensor_tensor(wdt.t[:], dt.t[:, 0:8], E.t[:, 16:24], ALU.mult), [dt, E], [wdt])
            return dict(c=c, tok=tok, xt=xt, bt=bt, BTc=BTc, CTc=CTc, zt=zt, dt=dt, ac=ac, E=E, wdt=wdt)

        def stageBuild(s0):
            ac = s0["ac"]
            lst = []
            for j in range(16):
                smask = self.Lstrict if j < 8 else self.Ustrict
                lh = lhss.next()
                en = lhs_eng.next()
                if en == "act":
                    cx.op("act", lambda e, lh=lh, smask=smask, j=j: e.activation(lh.t[:], smask.t[:], AF.Copy, scale=ac.t[:, j:j + 1]), [smask, ac], [lh])
                else:
                    cx.op("dve", lambda e, lh=lh, smask=smask, j=j: e.tensor_scalar(lh.t[:], smask.t[:], ac.t[:, j:j + 1], None, op0=ALU.mult), [smask, ac], [lh])
                lst.append(lh)
            s0["lhs"] = lst

        def stageA1(s0):
            c = s0["c"]; tok = s0["tok"]; xt = s0["xt"]; bt = s0["bt"]; BTc = s0["BTc"]; CTc = s0["CTc"]; zt = s0["zt"]
            dt = s0["dt"]; ac = s0["ac"]; E = s0["E"]; wdt = s0["wdt"]
            xf = xdtf.next()
            cx.op("dve", lambda e, xf=xf, xt=xt, dt=dt: e.tensor_tensor(v3(xf.t[:]), v3(xt.t[:]), bc8(dt.t[:, 0:8]), ALU.mult), [xt, dt], [xf])
            xb_ = xdtb.next()
            cx.op("dve", lambda e, xb_=xb_, xt=xt, dt=dt: e.tensor_tensor(v3(xb_.t[:]), v3(xt.t[:]), bc8(dt.t[:, 8:16]), ALU.mult), [xt, dt], [xb_])
            xw = xwf.next()
            cx.op("dve", lambda e, xw=xw, xt=xt, wdt=wdt: e.tensor_tensor(v3(xw.t[:]), v3(xt.t[:]), bc8(wdt.t[:, :]), ALU.mult), [xt, wdt], [xw])
            xD = xDs.next()
            cx.op("dve", lambda e, xD=xD, xt=xt: e.tensor_tensor(v3(xD.t[:]), v3(xt.t[:]), Dfull.t[:], ALU.mult), [xt, Dfull], [xD])
            for g in range(2):
                self.mm(psA.t[:, 128 + g * 128:256 + g * 128], BTc.t[:, g, :], CTc.t[:, g, :], True, True, [BTc, CTc], [RpsG], g == 1)
            gmf = Gmf.next()
            gmb = Gmb.next()
            pg = psA.t[:, 128:384].rearrange("p (g l) -> p g l", g=2)
            cx.op("dve", lambda e, gmf=gmf, pg=pg: e.tensor_tensor(gmf.t[:], pg, self.U_incl.t[:, :].unsqueeze(1).to_broadcast([128, 2, 128]), ALU.mult), [RpsG, self.U_incl], [gmf])
            cx.op("dve", lambda e, gmb=gmb, pg=pg: e.tensor_tensor(gmb.t[:], pg, self.L_incl.t[:, :].unsqueeze(1).to_broadcast([128, 2, 128]), ALU.mult), [RpsG, self.L_incl], [gmb])
            MT = [MTs[0].next(), MTs[1].next()]
            for dr in range(2):
                smask = self.Lstrict if dr == 0 else self.Ustrict
                cmask = self.U_incl if dr == 0 else self.L_incl
                gm = gmf if dr == 0 else gmb
                for g in range(2):
                    bank = diffs.next()
                    for hh in range(4):
                        j = dr * 8 + g * 4 + hh
                        lh = s0["lhs"][j]
                        self.mm(bank.t[:, hh, :], lh.t[:], cmask.t[:], True, True, [lh, cmask], [bank], hh == 3)
                    ex = expds.next()
                    cx.op("act", lambda e, ex=ex, bank=bank: e.activation(ex.t[:], bank.t[:], AF.Exp), [bank], [ex])
                    cx.op("dve", lambda e, ex=ex, gm=gm, g=g, dr=dr: e.tensor_tensor(
                        MT[dr].t[:, g * 4:(g + 1) * 4, :], ex.t[:], gm.t[:, g:g + 1, :].to_broadcast([128, 4, 128]), ALU.mult), [ex, gm], [MT[dr]])
            return dict(c=c, tok=tok, xt=xt, bt=bt, CTc=CTc, zt=zt, E=E, xf=xf, xb_=xb_, xw=xw, xD=xD, MT=MT)

        def stageB(st):
            c = st["c"]; xt = st["xt"]; bt = st["bt"]; CTc = st["CTc"]; zt = st["zt"]; E = st["E"]
            xf = st["xf"]; xb_ = st["xb_"]; xw = st["xw"]; xD = st["xD"]; MT = st["MT"]
            self.mm(psy.t[:], self.ident_bf.t[:], xD.t[:], True, False, [self.ident_bf, xD], [psy], False)
            for h in range(8):
                hs = slice(h * 64, (h + 1) * 64)
                self.mm(psy.t[:, hs], MT[0].t[:, h, :], xf.t[:, hs], False, False, [MT[0], xf], [psy], False)
                self.mm(psy.t[:, hs], MT[1].t[:, h, :], xb_.t[:, hs], False, h == 7, [MT[1], xb_], [psy], h == 7)
            for g in range(2):
                gs = slice(g * 256, (g + 1) * 256)
                self.mm(psyo1.t[:, gs], CTc.t[:, g, :], stf_bf.t[:, gs], True, True, [CTc, stf_bf], [psyo1], g == 1)
            for g in range(2):
                self.mm(pscs.t[:, g * 256:(g + 1) * 256], bt.t[:, g * 128:(g + 1) * 128], xw.t[:, g * 256:(g + 1) * 256], True, True, [bt, xw], [pscs], g == 1)
            tmf = tmps.next()
            cx.op("dve", lambda e: e.tensor_tensor(v3(tmf.t[:]), v3(psyo1.t[:]), bc8(E.t[:, 0:8]), ALU.mult), [psyo1, E], [tmf])
            cx.op("dve", lambda e: e.tensor_tensor(v3(state_f.t[:]), v3(state_f.t[:]), bc8(E.t[:, 24:32]), ALU.mult), [state_f, E], [state_f])
            cx.op("dve", lambda e: e.tensor_tensor(state_f.t[:], state_f.t[:], pscs.t[:], ALU.add), [state_f, pscs], [state_f])
            cx.op("act", lambda e: e.copy(stf_bf.t[:], state_f.t[:]), [state_f], [stf_bf])
            y1 = y1s.next()
            cx.op("act", lambda e: e.copy(y1.t[:], psy.t[:]), [psy], [y1])
            cx.op("dve", lambda e: e.tensor_tensor(y1.t[:], y1.t[:], tmf.t[:], ALU.add), [y1, tmf], [y1])
            for g in range(2):
                gs = slice(g * 256, (g + 1) * 256)
                self.mm(psyo1.t[:, gs], CTc.t[:, g, :], prevB.t[:, c, gs], True, True, [CTc, prevB], [psyo1], g == 1)
            tmb = tmps.next()
            cx.op("dve", lambda e: e.tensor_tensor(v3(tmb.t[:]), v3(psyo1.t[:]), bc8(E.t[:, 8:16]), ALU.mult), [psyo1, E], [tmb])
            cx.op("dve", lambda e: e.tensor_tensor(y1.t[:], y1.t[:], tmb.t[:], ALU.add), [y1, tmb], [y1])
            return (c, y1, zt)

        def stageBt(sb):
            c, y1, zt = sb
            sz = szs.next()
            cx.op("act", lambda e: e.activation(sz.t[:], zt.t[:], AF.Silu), [zt], [sz])
            cx.op("dve", lambda e: e.tensor_tensor(y1.t[:], y1.t[:], sz.t[:], ALU.mult), [y1, sz], [y1])
            s2 = ss2.next()
            for g in range(2):
                cx.op("dve", lambda e, g=g: e.scalar_tensor_tensor(junk.t[:], y1.t[:, g * 256:(g + 1) * 256], 1.0, y1.t[:, g * 256:(g + 1) * 256],
                      op0=ALU.mult, op1=ALU.mult, accum_out=s2.t[:, g:g + 1]), [y1], [junk, s2])
            cx.op("act", lambda e: e.activation(s2.t[:, 2:4], s2.t[:, 0:2], AF.Ln, bias=self.eps1.t[:, 0:1], scale=1.0 / 256), [s2, self.eps1], [s2])
            cx.op("act", lambda e: e.activation(s2.t[:, 2:4], s2.t[:, 2:4], AF.Exp, scale=-0.5), [s2], [s2])
            yn = yns.next()
            cx.op("dve", lambda e: e.tensor_tensor(
                yn.t[:].rearrange("p (g f) -> p g f", g=2), y1.t[:].rearrange("p (g f) -> p g f", g=2),
                s2.t[:, 2:4].unsqueeze(2).to_broadcast([128, 2, 256]), ALU.mult), [y1, s2], [yn])
            return (c, yn)

        def stageC(stc):
            c, yn = stc
            tp = self.n_tp.next()
            for j in range(4):
                cx.op("pe", lambda e, j=j: e.transpose(tp.t[:, j, :], yn.t[:, j * 128:(j + 1) * 128], self.ident_bf.t[:]),
                      [yn, self.ident_bf], [tp], inc=(j == 3))
            if c % 4 == 0:
                sT_box[0] = sTs.next()
            sT = sT_box[0]
            q = c % 4
            cx.op("dve", lambda e: e.tensor_tensor(sT.t[:, :, q * 128:(q + 1) * 128], tp.t[:, 0:4, :],
                  gS.t[:, :].unsqueeze(2).to_broadcast([128, 4, 128]), ALU.mult), [tp, gS], [sT])
            if q == 3:
                cb4 = c // 4
                cx.dma("act", self.mixT[512:1024, cb4 * 512:(cb4 + 1) * 512].rearrange("(ch p) t -> p ch t", p=128), sT.t[:], [sT], [], ds=dsT.next())

        s0 = {}
        for k in range(min(3, NT)):
            s0[k] = stageA0(k)
        stageBuild(s0[0])
        if NT > 1:
            stageBuild(s0[1])
        stA = stageA1(s0.pop(0))
        stC = None
        for c in range(NT):
            sb = stageB(stA)
            stA = stageA1(s0.pop(c + 1)) if c + 1 < NT else None
            if c + 3 < NT:
                s0[c + 3] = stageA0(c + 3)
            if c + 2 < NT:
                stageBuild(s0[c + 2])
            cur = stageBt(sb)
            if stC is not None:
                stageC(stC)
            stC = cur
            if c % 2 == 1:
                self.conv_tick()
        stageC(stC)
        self.conv_flush()
        cx.close_scope()

    def phase_wout(self, li, x_src):
        cx = self.cx
        cx.open_scope()
        self.norm_bufs(nx=3, with_norm=False)
        wo = cx.sb([128, 12, 1024], BF16, "wo")
        cx.dma("sp", wo.t[:], self.wout_b[li], [], [wo], ds=cx.dsem())
        mts = Ring([cx.sb([128, 12, 512], BF16, "wo_mt") for _ in range(2)])
        dmt = Ring([cx.dsem() for _ in range(2)])
        xos = Ring([cx.sb([128, D], F32, "wo_xo") for _ in range(2)])
        dxo = Ring([cx.dsem() for _ in range(2)])
        pss = Ring([cx.ps([128, 512], F32, "wo_ps") for _ in range(4)])
        mv = self.mixT.rearrange("(k p) t -> p k t", p=128)
        for tb in range(8):
            mt = mts.next()
            cx.dma("sp", mt.t[:], mv[:, :, tb * 512:(tb + 1) * 512], [], [mt], ds=dmt.next())
            for t4 in range(4):
                tt = tb * 4 + t4
                xt = self.n_xt.next()
                cx.dma("sp", xt.t[:], x_src[tt * 128:(tt + 1) * 128, :], [], [xt], ds=self.n_dx.next())
                xo = xos.next()
                for half in range(2):
                    ps = pss.next()
                    hs = slice(half * 512, (half + 1) * 512)
                    for kc in range(12):
                        self.mm(ps.t[:], mt.t[:, kc, t4 * 128:(t4 + 1) * 128], wo.t[:, kc, hs], kc == 0, kc == 11, [mt, wo], [ps], kc == 11)
                    cx.op("dve", lambda e, xo=xo, ps=ps, xt=xt, hs=hs: e.tensor_tensor(xo.t[:, hs], ps.t[:], xt.t[:, hs], ALU.add), [ps, xt], [xo])
                cx.dma("act", self.xa[tt * 128:(tt + 1) * 128, :], xo.t[:], [xo], [], ds=dxo.next())
        cx.close_scope()

    def phase_ffn_norm(self, li):
        cx = self.cx
        cx.open_scope()
        self.norm_bufs()
        g2 = self.load_gT(self.ffn_norm[li], 8, "gT_ffn")
        stgs = Ring([cx.sb([128, 8, 512], BF16, "fn_stg") for _ in range(2)])
        dst = Ring([cx.dsem() for _ in range(2)])
        hv = self.h2T_d.rearrange("(k p) t -> p k t", p=128)
        prev = None

        def fin(pv):
            tp, stg, t4, tb = pv
            self.norm_tile_b(tp, g2, stg.t[:, :, t4 * 128:(t4 + 1) * 128], stg)
            if t4 == 3:
                cx.dma("act", hv[:, :, tb * 512:(tb + 1) * 512], stg.t[:], [stg], [], ds=dst.next())

        for tb in range(8):
            stg = stgs.next()
            for t4 in range(4):
                tt = tb * 4 + t4
                xt = self.n_xt.next()
                cx.dma("sp", xt.t[:], self.xa[tt * 128:(tt + 1) * 128, :], [], [xt], ds=self.n_dx.next())
                tp = self.norm_tile_a(xt)
                if prev is not None:
                    fin(prev)
                prev = (tp, stg, t4, tb)
        fin(prev)
        cx.close_scope()

    def phase_ffn(self, li, last):
        cx = self.cx
        cx.open_scope()
        self.norm_bufs(nx=3, with_norm=False)
        self.n_junk = cx.sb([128, D], BF16, "n_junk")
        wd = cx.sb([128, 22, 1024], BF16, "wd")
        cx.dma("sp", wd.t[:], self.wdown_b[li], [], [wd], ds=cx.dsem())
        cw = self.load_cw(self.ffn_conv_w[li], 3, 44, "ffn_cw")
        cb = self.load_gT(self.ffn_conv_b[li], 44, "ffn_cb")
        if last:
            gfin = self.load_bcast(self.final_norm, D, "gfin")
            fss = Ring([cx.sb([128, 2], F32, "fin_ss") for _ in range(2)])
        hbs = Ring([cx.sb([128, 8, 1026], BF16, "ffn_hb") for _ in range(2)])
        dhb = Ring([cx.dsem() for _ in range(2)])
        aT = cx.sb([128, 22, 1024], BF16, "ffn_aT")
        wus = Ring([cx.sb([128, 2, 8, 128], BF16, "ffn_wu") for _ in range(3)])
        dwu = Ring([cx.dsem() for _ in range(3)])
        accg = Ring([cx.sb([128, 512], F32, "ffn_accg") for _ in range(2)])
        accu = Ring([cx.sb([128, 512], F32, "ffn_accu") for _ in range(2)])
        sgs = Ring([cx.sb([128, 512], F32, "ffn_sg") for _ in range(2)])
        xos = Ring([cx.sb([128, D], F32, "ffn_xo") for _ in range(2)])
        dxo = Ring([cx.dsem() for _ in range(2)])
        psu = Ring([cx.ps([128, 512], F32, "ffn_psu") for _ in range(4)])
        psd = Ring([cx.ps([128, 512], F32, "ffn_psd") for _ in range(2)])
        hv = self.h2T_d.rearrange("(k p) t -> p k t", p=128)
        subs = ((0, 342), (342, 342), (684, 340))
        for bk in range(4):
            hb = hbs.next()
            lo = max(0, bk * 1024 - 1)
            hi = min(S, bk * 1024 + 1025)
            o0 = lo - (bk * 1024 - 1)
            if bk == 0:
                cx.op("pool", lambda e, hb=hb: e.memset(hb.t[:, :, 0:1], 0.0), [], [hb])
            if bk == 3:
                cx.op("pool", lambda e, hb=hb: e.memset(hb.t[:, :, 1025:1026], 0.0), [], [hb])
            cx.dma("sp", hb.t[:, :, o0:o0 + hi - lo], hv[:, :, lo:hi], [], [hb], ds=dhb.next())
            for fc in range(22):
                wu = wus.next()
                dd = dwu.next()
                cx.dma("sp", wu.t[:, 0], self.wup_fm[li, fc], [], [wu], ds=dd)
                cx.dma("sp", wu.t[:, 1], self.wup_fm[li, 22 + fc], [], [wu], ds=dd)
                for (s0, w) in subs:
                    n = w + 2
                    accs = []
                    for which in range(2):
                        ps = psu.next()
                        for kc in range(8):
                            self.mm(ps.t[:, 0:n], wu.t[:, which, kc, :], hb.t[:, kc, s0:s0 + n], kc == 0, kc == 7, [wu, hb], [ps], kc == 7)
                        ch = fc + 22 * which
                        acc = (accg if which == 0 else accu).next()
                        cx.op("act", lambda e, acc=acc, ps=ps, w=w, ch=ch: e.activation(acc.t[:, 0:w], ps.t[:, 1:1 + w], AF.Identity,
                              bias=cb.t[:, ch:ch + 1], scale=cw.t[:, ch, 1:2]), [ps, cb, cw], [acc])
                        for k in (0, 2):
                            cx.op("dve", lambda e, acc=acc, ps=ps, w=w, ch=ch, k=k: e.scalar_tensor_tensor(
                                acc.t[:, 0:w], ps.t[:, k:k + w], cw.t[:, ch, k:k + 1], acc.t[:, 0:w], op0=ALU.mult, op1=ALU.add), [ps, cw, acc], [acc])
                        accs.append(acc)
                    sg = sgs.next()
                    cx.op("act", lambda e, sg=sg, a=accs[0], w=w: e.activation(sg.t[:, 0:w], a.t[:, 0:w], AF.Silu), [accs[0]], [sg])
                    cx.op("dve", lambda e, sg=sg, a=accs[1], w=w, fc=fc, s0=s0: e.tensor_tensor(aT.t[:, fc, s0:s0 + w], sg.t[:, 0:w], a.t[:, 0:w], ALU.mult), [sg, accs[1]], [aT])
            for t8 in range(8):
                tt = bk * 8 + t8
                xt = self.n_xt.next()
                cx.dma("sp", xt.t[:], self.xa[tt * 128:(tt + 1) * 128, :], [], [xt], ds=self.n_dx.next())
                xo = xos.next()
                for half in range(2):
                    ps = psd.next()
                    hs = slice(half * 512, (half + 1) * 512)
                    for kc in range(22):
                        self.mm(ps.t[:], aT.t[:, kc, t8 * 128:(t8 + 1) * 128], wd.t[:, kc, hs], kc == 0, kc == 21, [aT, wd], [ps], kc == 21)
                    cx.op("dve", lambda e, xo=xo, ps=ps, xt=xt, hs=hs: e.tensor_tensor(xo.t[:, hs], ps.t[:], xt.t[:, hs], ALU.add), [ps, xt], [xo])
                if not last:
                    cx.dma("act", self.xb[tt * 128:(tt + 1) * 128, :], xo.t[:], [xo], [], ds=dxo.next())
                else:
                    ss = fss.next()
                    self.rstd_of(xo.t[:], xo, ss, D)
                    cx.op("dve", lambda e, xo=xo, ss=ss: e.scalar_tensor_tensor(xo.t[:], xo.t[:], ss.t[:, 1:2], gfin.t[:], op0=ALU.mult, op1=ALU.mult), [xo, ss, gfin], [xo])
                    cx.dma("act", self.y[tt * 128:(tt + 1) * 128, :], xo.t[:], [xo], [], ds=dxo.next())
        cx.close_scope()

    def build(self):
        cx = self.cx
        self.consts()
        self.cv_jobs = []
        self.cv_pending = None
        self.early_conv = self.want("conv")
        if self.early_conv and self.phases is not None:
            cx.open_scope()
            self.conv_setup(["dve", "act"], ["sp"], ["act"])
            for j in self.conv_jobs(0, "in"):
                j()()
            if "ssd" not in self.phases:
                for j in self.conv_jobs(0, "rest"):
                    j()()
            cx.close_scope()
            self.early_conv = False
        cx.open_scope()
        self.norm_setup()
        for li in range(self.layers):
            x_src = self.x if li == 0 else self.xb
            cx.open_scope()
            hT = cx.sb([128, 8, S + 2 * PADH], BF16, "hT")
            cx.op("pool", lambda e: e.memset(hT.t[:, :, 0:PADH], 0.0), [], [hT])
            cx.op("pool", lambda e: e.memset(hT.t[:, :, PADH + S:], 0.0), [], [hT])
            gT = self.load_gT(self.mix_norm[li], 8, "gT_mix")
            self.g8h = []
            g8c = cx.sb([64, 8], F32, "g8c")
            stg = self.gstg.next()
            cx.dma("sp", stg.t[0:8, 0:64], self.attn_norm[li].rearrange("(c p) -> c p", p=64), [], [stg], ds=self.dqr.next())
            tp = self.n_tp.next()
            pv = tp.t[:].rearrange("p a b -> p (a b)").bitcast(F32)
            self.mm(pv[0:64, 0:8], stg.t[0:8, 0:64], self.ident_f.t[0:8, 0:8], True, True, [stg, self.ident_f], [tp], True)
            cx.op("dve", lambda e: e.tensor_scalar(g8c.t[:, :], pv[0:64, 0:8], 8.0, None, op0=ALU.mult), [tp], [g8c])
            self.g8c = g8c
            ovl = self.early_conv and li == 0
            if ovl:
                cx.open_scope()
                self.conv_setup(["act", "dve"], ["sp"], ["act"])
                self.cv_jobs = self.conv_jobs(0, "in")
            if self.want("norm"):
                self.phase_norm(x_src, gT, hT)
            if ovl:
                self.conv_flush()
                cx.close_scope()
            if "hT" in self.debug:
                dd = cx.dsem()
                cx.dma("sp", self.scr("hT", [128, 8, S + 2 * PADH], BF16), hT.t[:], [hT], [], ds=dd)
            if self.want("attn"):
                self.phase_attn(li, hT)
            if self.want("zdt"):
                self.phase_zdt(li, hT)
            if self.want("xbc"):
                self.phase_xbc(li, hT)
            if self.want("sc"):
                self.phase_sc(li, hT)
            cx.close_scope()
            if self.want("ssd"):
                self.phase_ssd(li)
            if self.want("wout"):
                self.phase_wout(li, x_src)
            if self.want("ffn"):
                self.phase_ffn_norm(li)
                self.phase_ffn(li, li == DEPTH - 1)
        cx.close_scope()
        cx.finish()
        self.lp.close()
        return self.nc


_CACHE = {}


def kernel(**inputs):
    if "nc" not in _CACHE:
        _CACHE["nc"] = Builder().build()
    nc = _CACHE["nc"]
    names = ["mix_norm", "w_in", "ssd_conv_w", "ssd_conv_b", "ssd_dt_bias", "ssd_a_log", "ssd_d", "ssd_norm",
             "sc_conv_w", "sc_conv_b", "attn_norm", "sc_norm", "w_out", "ffn_norm", "w_up", "ffn_conv_w",
             "ffn_conv_b", "w_down", "final_norm"]
    shared = {}
    for n in names:
        a = np.ascontiguousarray(np.asarray(inputs[n], dtype=np.float32))
        if n in ("ssd_dt_bias", "ssd_a_log"):
            a = a.reshape(DEPTH, 16)
        shared[n] = a
    x = np.asarray(inputs["x"], dtype=np.float32)
    in_maps = [dict(shared, x=np.ascontiguousarray(x[b])) for b in range(8)]
    res = run_bass_kernel_spmd(nc, in_maps, core_ids=list(range(8)))
    return np.stack([res.results[b]["y"] for b in range(8)], axis=0).astype(np.float32)
```
